# Optimizing a Trainium2 kernel written in Bass

```python
import math
import jax
import jax.numpy as jnp
from jax import lax
import numpy as np

D_MODEL = 1024
BATCH = 4
SEQ = 4096
DEPTH = 4
DEC_BATCH = 128
DEC_SEQ = 4
PAST_LEN = 2048
PAGE_SIZE = 128

N_MIXERS = 4
N_LAYERS_A = len(range(0, DEPTH, N_MIXERS))
N_LAYERS_B = len(range(1, DEPTH, N_MIXERS))
N_LAYERS_C = len(range(2, DEPTH, N_MIXERS))
N_LAYERS_D = len(range(3, DEPTH, N_MIXERS))

HEAD_DIM = 64
ROPE_DIM = HEAD_DIM // 4
ROPE_THETA = 500000.0
EPS = 1e-6
D_FF = 2816
OUT_SCALE = 0.5

A_WINDOWS = (128, 512, 2048)
A_DILATIONS = (1, 4, 16)
A_GROUPS = 3
A_HEADS = D_MODEL // 128
A_WIDTH = A_HEADS * HEAD_DIM
A_SPAN = 128
A_BLOCK = A_SPAN

B_HEADS = D_MODEL // HEAD_DIM
B_KV_HEADS = 4
B_REP = B_HEADS // B_KV_HEADS
B_BLOCK = 256
B_TOPK = 3
B_QCHUNK = 64

C_HEADS = 4
C_KEY = D_MODEL // 2
C_VAL = D_MODEL
C_DK = C_KEY // C_HEADS
C_DV = C_VAL // C_HEADS
C_GATE_RANK = 16
C_TAU = 16.0
C_CHUNK = 32

D_INNER = 2 * D_MODEL
D_HEADDIM = 64
D_HEADS = D_INNER // D_HEADDIM
D_GROUPS = 4
D_HPG = D_HEADS // D_GROUPS
D_STATE = 128
D_CONV = 4
D_CHUNK = 64
D_XBC = D_INNER + 2 * D_GROUPS * D_STATE

kernel_name = 'hybrid_dilated_moba_gla_ssd_decoder_step'


def rms_norm(x, g):
    xf = x.astype(jnp.float32)
    y = xf * lax.rsqrt(jnp.mean(xf * xf, axis=-1, keepdims=True) + EPS)
    return (y * g.astype(jnp.float32)).astype(x.dtype)


def rope_partial(x, pos):
    half = ROPE_DIM // 2
    inv = ROPE_THETA ** (-jnp.arange(half, dtype=jnp.float32) / half)
    ang = pos.astype(jnp.float32)[:, None] * inv[None, :]
    cos = jnp.cos(ang)[:, None, :]
    sin = jnp.sin(ang)[:, None, :]
    xr = x[..., :ROPE_DIM].astype(jnp.float32)
    x1, x2 = xr[..., :half], xr[..., half:]
    rot = jnp.concatenate([x1 * cos - x2 * sin, x2 * cos + x1 * sin], axis=-1)
    return jnp.concatenate([rot.astype(x.dtype), x[..., ROPE_DIM:]], axis=-1)


def swiglu(x, w_up, w_down):
    g, u = jnp.split(x @ w_up, 2, axis=-1)
    return (jax.nn.silu(g) * u) @ w_down


def dilated_band_prompt(q, k, v, d):
    Bn, S, H, Dh = q.shape
    n = S // d
    nb = -(-n // A_BLOCK)
    npad = nb * A_BLOCK

    def split(t):
        return t.astype(jnp.float32).reshape(Bn, n, d, H, Dh).transpose(0, 2, 1, 3, 4)

    qs, ks, vs = split(q), split(k), split(v)
    qb = jnp.pad(qs, ((0, 0), (0, 0), (0, npad - n), (0, 0), (0, 0))).reshape(Bn, d, nb, A_BLOCK, H, Dh)
    kv_pad = ((0, 0), (0, 0), (A_BLOCK, npad - n), (0, 0), (0, 0))
    kb = jnp.pad(ks, kv_pad).reshape(Bn, d, nb + 1, A_BLOCK, H, Dh)
    vb = jnp.pad(vs, kv_pad).reshape(Bn, d, nb + 1, A_BLOCK, H, Dh)
    kband = jnp.concatenate([kb[:, :, :-1], kb[:, :, 1:]], axis=3)
    vband = jnp.concatenate([vb[:, :, :-1], vb[:, :, 1:]], axis=3)
    s = jnp.einsum('bdnqhe,bdnkhe->bdnhqk', qb, kband) * HEAD_DIM ** -0.5
    qq = jnp.arange(A_BLOCK)[:, None]
    kk = jnp.arange(2 * A_BLOCK)[None, :]
    rel = qq + A_BLOCK - kk
    upos = jnp.arange(nb)[:, None, None] * A_BLOCK - A_BLOCK + kk[None]
    valid = ((rel >= 0) & (rel <= A_SPAN))[None] & (upos >= 0)
    s = jnp.where(valid[None, None, :, None], s, -jnp.inf)
    m = jnp.max(s, axis=-1, keepdims=True)
    p = jnp.exp(s - m)
    den = jnp.sum(p, axis=-1, keepdims=True)
    o = jnp.einsum('bdnhqk,bdnkhe->bdnqhe', p / den, vband)
    lse = (m + jnp.log(den))[..., 0].transpose(0, 1, 2, 4, 3)
    o = o.reshape(Bn, d, npad, H, Dh)[:, :, :n].transpose(0, 2, 1, 3, 4).reshape(Bn, S, H, Dh)
    lse = lse.reshape(Bn, d, npad, H)[:, :, :n].transpose(0, 2, 1, 3).reshape(Bn, S, H)
    return o, lse


def dilated_gather_sample(q, k_new, v_new, k_buf, v_buf, d):
    L = k_buf.shape[1]
    T = q.shape[1]
    k_all = jnp.concatenate([k_buf.astype(jnp.float32), k_new.astype(jnp.float32)], axis=1)
    v_all = jnp.concatenate([v_buf.astype(jnp.float32), v_new.astype(jnp.float32)], axis=1)
    idx = L + jnp.arange(T)[:, None] - d * jnp.arange(A_SPAN + 1)[None, :]
    valid = idx >= 0
    idx = jnp.maximum(idx, 0)
    kg = k_all[:, idx]
    vg = v_all[:, idx]
    s = jnp.einsum('bthe,btjhe->bthj', q.astype(jnp.float32), kg) * HEAD_DIM ** -0.5
    s = jnp.where(valid[None, :, None, :], s, -jnp.inf)
    m = jnp.max(s, axis=-1, keepdims=True)
    p = jnp.exp(s - m)
    den = jnp.sum(p, axis=-1, keepdims=True)
    o = jnp.einsum('bthj,btjhe->bthe', p / den, vg)
    return o, (m + jnp.log(den))[..., 0]


def mixer_a(hp, hs, pos_p, pos_s, bufs, w_in, qk_gain, w_out):
    def project(h, pos):
        Bn, L, _ = h.shape
        qkv = (h @ w_in).reshape(Bn, L, A_GROUPS, 3, A_HEADS, HEAD_DIM).transpose(2, 3, 0, 1, 4, 5)
        q = rope_partial(rms_norm(qkv[:, 0], qk_gain[0]), pos)
        k = rope_partial(rms_norm(qkv[:, 1], qk_gain[1]), pos)
        return q, k, qkv[:, 2]

    qp, kp, vp = project(hp, pos_p)
    qs, ks, vs = project(hs, pos_s)
    S = hp.shape[1]
    op, lp, osm, lsm, new_p, new_s = [], [], [], [], [], []
    for g in range(A_GROUPS):
        d = A_DILATIONS[g]
        o, l = dilated_band_prompt(qp[g], kp[g], vp[g], d)
        op.append(o)
        lp.append(l)
        o, l = dilated_gather_sample(qs[g], ks[g], vs[g], bufs[g][:, :, 0], bufs[g][:, :, 1], d)
        osm.append(o)
        lsm.append(l)
        new_p.append(jnp.stack([kp[g], vp[g]], axis=2)[:, S - min(A_WINDOWS[g], S):])
        new_s.append(jnp.stack([ks[g], vs[g]], axis=2))

    def merge(o_list, l_list, dtype):
        w = jax.nn.softmax(jnp.stack(l_list), axis=0)
        o = jnp.einsum('gblh,gblhe->blhe', w, jnp.stack(o_list))
        return o.reshape(o.shape[0], o.shape[1], A_WIDTH).astype(dtype) @ w_out

    return merge(op, lp, hp.dtype), merge(osm, lsm, hs.dtype), new_p, new_s


def moba_blocks(k, v):
    Bn, L = k.shape[:2]
    nblk = -(-L // B_BLOCK)
    pad = ((0, 0), (0, nblk * B_BLOCK - L), (0, 0), (0, 0))
    kb = jnp.pad(k, pad).reshape(Bn, nblk, B_BLOCK, B_KV_HEADS, HEAD_DIM)
    vb = jnp.pad(v, pad).reshape(Bn, nblk, B_BLOCK, B_KV_HEADS, HEAD_DIM)
    kmean = jnp.mean(kb.astype(jnp.float32), axis=2)
    return kb, vb, kmean, min(B_TOPK, nblk)


def moba_attend(q, tq, kb, vb, kmean, n_sel):
    Bn, nblk = kb.shape[:2]
    scale = HEAD_DIM ** -0.5
    qf = q.astype(jnp.float32)
    own = tq // B_BLOCK
    gate = jnp.einsum('bqhrd,bnhd->bqhrn', qf, kmean)
    past = jnp.arange(nblk)[None, :] < own[:, None]
    gate = jnp.where(past[None, :, None, None, :], gate, -jnp.inf)
    sel = lax.top_k(gate, n_sel)[1]
    sel_ok = jnp.arange(n_sel)[None, :] < own[:, None]
    k_own = kb[:, own].astype(jnp.float32)
    v_own = vb[:, own].astype(jnp.float32)
    s_own = jnp.einsum('bqhrd,bqkhd->bqhrk', qf, k_own) * scale
    kpos = own[:, None] * B_BLOCK + jnp.arange(B_BLOCK)[None, :]
    s_own = jnp.where((kpos <= tq[:, None])[None, :, None, None, :], s_own, -jnp.inf)
    bi = jnp.arange(Bn)[:, None, None, None]
    hi = jnp.arange(B_KV_HEADS)[None, None, :, None]
    scores = [s_own]
    for slot in range(n_sel):
        kg = kb[bi, sel[..., slot], :, hi].astype(jnp.float32)
        s_r = jnp.einsum('bqhrd,bqhrkd->bqhrk', qf, kg) * scale
        scores.append(jnp.where(sel_ok[None, :, None, None, slot:slot + 1], s_r, -jnp.inf))
    p = jax.nn.softmax(jnp.concatenate(scores, axis=-1), axis=-1)
    o = jnp.einsum('bqhrk,bqkhd->bqhrd', p[..., :B_BLOCK], v_own)
    for slot in range(n_sel):
        vg = vb[bi, sel[..., slot], :, hi].astype(jnp.float32)
        o = o + jnp.einsum('bqhrk,bqhrkd->bqhrd', p[..., (slot + 1) * B_BLOCK:(slot + 2) * B_BLOCK], vg)
    return o


def mixer_b(hp, hs, pos_p, pos_s, pool, page_table, w_in, qk_gain, w_out):
    nq = B_HEADS * HEAD_DIM
    nk = B_KV_HEADS * HEAD_DIM

    def project(h, pos):
        Bn, L, _ = h.shape
        y = h @ w_in
        q = y[..., :nq].reshape(Bn, L, B_HEADS, HEAD_DIM)
        k = y[..., nq:nq + nk].reshape(Bn, L, B_KV_HEADS, HEAD_DIM)
        v = y[..., nq + nk:].reshape(Bn, L, B_KV_HEADS, HEAD_DIM)
        q = rope_partial(rms_norm(q, qk_gain[0]), pos)
        k = rope_partial(rms_norm(k, qk_gain[1]), pos)
        return q.reshape(Bn, L, B_KV_HEADS, B_REP, HEAD_DIM), k, v

    qp, kp, vp = project(hp, pos_p)
    qs, ks, vs = project(hs, pos_s)
    Bp, S = hp.shape[:2]
    kb, vb, kmean, n_sel = moba_blocks(kp, vp)
    nqc = S // B_QCHUNK
    qc = qp.reshape(Bp, nqc, B_QCHUNK, B_KV_HEADS, B_REP, HEAD_DIM).swapaxes(0, 1)
    tq = pos_p.reshape(nqc, B_QCHUNK)
    o_p = lax.map(lambda a: moba_attend(a[0], a[1], kb, vb, kmean, n_sel), (qc, tq))
    o_p = o_p.swapaxes(0, 1).reshape(Bp, S, nq)
    Bd, T = hs.shape[:2]
    past = pool[page_table].reshape(Bd, -1, 2, B_KV_HEADS, HEAD_DIM)
    k_all = jnp.concatenate([past[:, :, 0], ks], axis=1)
    v_all = jnp.concatenate([past[:, :, 1], vs], axis=1)
    kb_s, vb_s, kmean_s, n_sel_s = moba_blocks(k_all, v_all)
    o_s = moba_attend(qs, pos_s, kb_s, vb_s, kmean_s, n_sel_s).reshape(Bd, T, nq)
    out_p = o_p.astype(hp.dtype) @ w_out
    out_s = o_s.astype(hs.dtype) @ w_out
    return out_p, out_s, jnp.stack([kp, vp], axis=2), jnp.stack([ks, vs], axis=2)


def gla_chunked(q, k, v, la, s0):
    Bn, L, H, dk = q.shape
    dv = v.shape[-1]
    nc = -(-L // C_CHUNK)
    pad = nc * C_CHUNK - L

    def blk(t):
        t = jnp.pad(t.astype(jnp.float32), ((0, 0), (0, pad), (0, 0), (0, 0)))
        return t.reshape(Bn, nc, C_CHUNK, H, t.shape[-1])

    q, k, v, la = blk(q), blk(k), blk(v), blk(la)
    b = jnp.cumsum(la, axis=2)
    b_end = b[:, :, -1:]
    qe = q * jnp.exp(b)
    ke = k * jnp.exp(-b)
    kd = k * jnp.exp(b_end - b)
    causal = jnp.tril(jnp.ones((C_CHUNK, C_CHUNK), dtype=bool))
    att = jnp.where(causal, jnp.einsum('bnthk,bnshk->bnhts', qe, ke), 0.0)
    o_intra = jnp.einsum('bnhts,bnshv->bnthv', att, v)
    cs = jnp.einsum('bnshk,bnshv->bnhkv', kd, v)
    dec = jnp.exp(b_end[:, :, 0])

    def step(st, inp):
        d_c, cs_c = inp
        return d_c[..., None] * st + cs_c, st

    sT, s_in = lax.scan(step, s0.astype(jnp.float32), (dec.swapaxes(0, 1), cs.swapaxes(0, 1)))
    o_inter = jnp.einsum('bnthk,bnhkv->bnthv', qe, s_in.swapaxes(0, 1))
    o = (o_intra + o_inter).reshape(Bn, nc * C_CHUNK, H, dv)[:, :L]
    return o, sT


def mixer_c(hp, hs, state, w_in, w_gate2, b_gate, norm_g, w_out):
    def run(h, s0):
        Bn, L, _ = h.shape
        y = h @ w_in
        q = y[..., :C_KEY].reshape(Bn, L, C_HEADS, C_DK) * C_DK ** -0.5
        k = y[..., C_KEY:2 * C_KEY].reshape(Bn, L, C_HEADS, C_DK)
        v = y[..., 2 * C_KEY:2 * C_KEY + C_VAL].reshape(Bn, L, C_HEADS, C_DV)
        r = y[..., 2 * C_KEY + C_VAL:2 * C_KEY + 2 * C_VAL]
        glr = y[..., 2 * C_KEY + 2 * C_VAL:]
        la = jax.nn.log_sigmoid((glr @ w_gate2 + b_gate).astype(jnp.float32)) / C_TAU
        o, sT = gla_chunked(q, k, v, la.reshape(Bn, L, C_HEADS, C_DK), s0)
        o = rms_norm(o, norm_g).reshape(Bn, L, C_VAL).astype(h.dtype) * jax.nn.silu(r)
        return o @ w_out, sT

    out_p, s_p = run(hp, jnp.zeros((hp.shape[0], C_HEADS, C_DK, C_DV), jnp.float32))
    out_s, s_s = run(hs, state)
    return out_p, out_s, s_p.astype(hp.dtype), s_s.astype(hs.dtype)


def ssd_chunked(x, dt, A, Bm, Cm, h0):
    Bn, L = x.shape[:2]
    nc = -(-L // D_CHUNK)
    pad = nc * D_CHUNK - L

    def blk(t):
        t = jnp.pad(t.astype(jnp.float32), [(0, 0), (0, pad)] + [(0, 0)] * (t.ndim - 2))
        return t.reshape((Bn, nc, D_CHUNK) + t.shape[2:])

    x, dt, Bm, Cm = blk(x), blk(dt), blk(Bm), blk(Cm)
    cum = jnp.cumsum(dt * A, axis=2)
    seg = cum[:, :, :, None] - cum[:, :, None, :]
    causal = jnp.tril(jnp.ones((D_CHUNK, D_CHUNK), dtype=bool))
    lmat = jnp.exp(jnp.where(causal[:, :, None, None], seg, -jnp.inf))
    cb = jnp.einsum('bctgn,bcsgn->bctsg', Cm, Bm)
    y_diag = jnp.einsum('bctsgj,bcsgjp->bctgjp', cb[..., None] * lmat * dt[:, :, None], x)
    decay_end = jnp.exp(cum[:, :, -1:] - cum)
    states = jnp.einsum('bcsgn,bcsgj,bcsgjp->bcgjpn', Bm, decay_end * dt, x)
    chunk_dec = jnp.exp(cum[:, :, -1])

    def step(h, inp):
        d_c, st = inp
        return d_c[..., None, None] * h + st, h

    hT, h_in = lax.scan(step, h0.astype(jnp.float32), (chunk_dec.swapaxes(0, 1), states.swapaxes(0, 1)))
    y_off = jnp.einsum('bctgn,bcgjpn->bctgjp', Cm, h_in.swapaxes(0, 1)) * jnp.exp(cum)[..., None]
    y = (y_diag + y_off).reshape((Bn, nc * D_CHUNK) + x.shape[3:])[:, :L]
    return y, hT


def mixer_d(hp, hs, ssm_state, conv_state, w_in, conv_w, conv_b, dt_bias, a_log, d_skip, norm_g, w_out):
    A = -jnp.exp(a_log.astype(jnp.float32)).reshape(D_GROUPS, D_HPG)

    def run(h, h0, cbuf):
        Bn, L, _ = h.shape
        y = h @ w_in
        z = y[..., :D_INNER]
        xbc = y[..., D_INNER:D_INNER + D_XBC]
        dt_raw = y[..., D_INNER + D_XBC:]
        xpad = jnp.concatenate([cbuf.astype(xbc.dtype), xbc], axis=1)
        conv = conv_b
        for w in range(D_CONV):
            conv = conv + xpad[:, w:w + L] * conv_w[w]
        xbc = jax.nn.silu(conv)
        xs_ = xbc[..., :D_INNER].reshape(Bn, L, D_GROUPS, D_HPG, D_HEADDIM)
        Bm = xbc[..., D_INNER:D_INNER + D_GROUPS * D_STATE].reshape(Bn, L, D_GROUPS, D_STATE)
        Cm = xbc[..., D_INNER + D_GROUPS * D_STATE:].reshape(Bn, L, D_GROUPS, D_STATE)
        dt = jax.nn.softplus(dt_raw.astype(jnp.float32) + dt_bias).reshape(Bn, L, D_GROUPS, D_HPG)
        yssm, hT = ssd_chunked(xs_, dt, A, Bm, Cm, h0)
        yssm = yssm + d_skip.reshape(D_GROUPS, D_HPG)[..., None] * xs_.astype(jnp.float32)
        gated = yssm.reshape(Bn, L, D_INNER).astype(h.dtype) * jax.nn.silu(z)
        gated = rms_norm(gated.reshape(Bn, L, D_GROUPS, D_INNER // D_GROUPS),
                         norm_g.reshape(D_GROUPS, D_INNER // D_GROUPS)).reshape(Bn, L, D_INNER)
        return gated @ w_out, hT.reshape(Bn, D_HEADS, D_HEADDIM, D_STATE), xpad[:, L:]

    Bp = hp.shape[0]
    h0p = jnp.zeros((Bp, D_GROUPS, D_HPG, D_HEADDIM, D_STATE), jnp.float32)
    c0p = jnp.zeros((Bp, D_CONV - 1, D_XBC), hp.dtype)
    out_p, hp_T, cp_T = run(hp, h0p, c0p)
    h0s = ssm_state.reshape(hs.shape[0], D_GROUPS, D_HPG, D_HEADDIM, D_STATE)
    out_s, hs_T, cs_T = run(hs, h0s, conv_state)
    return out_p, out_s, hp_T.astype(hp.dtype), hs_T.astype(hs.dtype), cp_T, cs_T


def setup_inputs(seed: int = 0) -> dict:
    key = jax.random.key(seed)
    keys = iter(jax.random.split(key, 48))

    def nrm(shape, scale=1.0):
        return jax.random.normal(next(keys), shape, jnp.float32) * scale

    def gain(shape):
        return 1.0 + nrm(shape, 0.02)

    n_pages = PAST_LEN // PAGE_SIZE
    n_used = DEC_BATCH * n_pages
    n_pool = n_used + max(1, n_used // 4)
    perm = jax.random.permutation(next(keys), n_pool)
    page_table = perm[:n_used].reshape(DEC_BATCH, n_pages).astype(jnp.int32)
    dt0 = jnp.exp(jax.random.uniform(next(keys), (N_LAYERS_D, D_HEADS), jnp.float32,
                                     math.log(1e-3), math.log(1e-1)))
    dt_bias = dt0 + jnp.log(-jnp.expm1(-dt0))
    a_log = jnp.log(jax.random.uniform(next(keys), (N_LAYERS_D, D_HEADS), jnp.float32, 1.0, 16.0))
    return {
        'x_prompt': nrm((BATCH, SEQ, D_MODEL)),
        'x_sample': nrm((DEC_BATCH, DEC_SEQ, D_MODEL)),
        'cache_a_w1': nrm((N_LAYERS_A, DEC_BATCH, min(A_WINDOWS[0], PAST_LEN), 2, A_HEADS, HEAD_DIM)),
        'cache_a_w2': nrm((N_LAYERS_A, DEC_BATCH, min(A_WINDOWS[1], PAST_LEN), 2, A_HEADS, HEAD_DIM)),
        'cache_a_w3': nrm((N_LAYERS_A, DEC_BATCH, min(A_WINDOWS[2], PAST_LEN), 2, A_HEADS, HEAD_DIM)),
        'cache_b_kv': nrm((N_LAYERS_B, n_pool, PAGE_SIZE, 2, B_KV_HEADS, HEAD_DIM)),
        'page_table': page_table,
        'state_c': nrm((N_LAYERS_C, DEC_BATCH, C_HEADS, C_DK, C_DV)),
        'state_d_ssm': nrm((N_LAYERS_D, DEC_BATCH, D_HEADS, D_HEADDIM, D_STATE), 0.5),
        'state_d_conv': nrm((N_LAYERS_D, DEC_BATCH, D_CONV - 1, D_XBC)),
        'norm_gain': gain((DEPTH, 3, D_MODEL)),
        'w_ffn_up': nrm((DEPTH, 2, D_MODEL, 2 * D_FF), D_MODEL ** -0.5),
        'w_ffn_down': nrm((DEPTH, 2, D_FF, D_MODEL), OUT_SCALE * D_FF ** -0.5),
        'w_a_in': nrm((N_LAYERS_A, D_MODEL, A_GROUPS * 3 * A_WIDTH), D_MODEL ** -0.5),
        'a_qk_gain': gain((N_LAYERS_A, 2, HEAD_DIM)),
        'w_a_out': nrm((N_LAYERS_A, A_WIDTH, D_MODEL), OUT_SCALE * A_WIDTH ** -0.5),
        'w_b_in': nrm((N_LAYERS_B, D_MODEL, (B_HEADS + 2 * B_KV_HEADS) * HEAD_DIM), D_MODEL ** -0.5),
        'b_qk_gain': gain((N_LAYERS_B, 2, HEAD_DIM)),
        'w_b_out': nrm((N_LAYERS_B, B_HEADS * HEAD_DIM, D_MODEL), OUT_SCALE * (B_HEADS * HEAD_DIM) ** -0.5),
        'w_c_in': nrm((N_LAYERS_C, D_MODEL, 2 * C_KEY + 2 * C_VAL + C_GATE_RANK), D_MODEL ** -0.5),
        'w_c_gate2': nrm((N_LAYERS_C, C_GATE_RANK, C_KEY), C_GATE_RANK ** -0.5),
        'b_c_gate': nrm((N_LAYERS_C, C_KEY), 0.1),
        'c_norm_gain': gain((N_LAYERS_C, C_DV)),
        'w_c_out': nrm((N_LAYERS_C, C_VAL, D_MODEL), OUT_SCALE * C_VAL ** -0.5),
        'w_d_in': nrm((N_LAYERS_D, D_MODEL, D_INNER + D_XBC + D_HEADS), D_MODEL ** -0.5),
        'd_conv_w': nrm((N_LAYERS_D, D_CONV, D_XBC), D_CONV ** -0.5),
        'd_conv_b': nrm((N_LAYERS_D, D_XBC), 0.02),
        'd_dt_bias': dt_bias,
        'd_a_log': a_log,
        'd_skip': 1.0 + nrm((N_LAYERS_D, D_HEADS), 0.1),
        'd_norm_gain': gain((N_LAYERS_D, D_INNER)),
        'w_d_out': nrm((N_LAYERS_D, D_INNER, D_MODEL), OUT_SCALE * D_INNER ** -0.5),
    }


def reference(x_prompt, x_sample, cache_a_w1, cache_a_w2, cache_a_w3, cache_b_kv, page_table,
              state_c, state_d_ssm, state_d_conv, norm_gain, w_ffn_up, w_ffn_down,
              w_a_in, a_qk_gain, w_a_out, w_b_in, b_qk_gain, w_b_out,
              w_c_in, w_c_gate2, b_c_gate, c_norm_gain, w_c_out,
              w_d_in, d_conv_w, d_conv_b, d_dt_bias, d_a_log, d_skip, d_norm_gain, w_d_out):
    S = x_prompt.shape[1]
    T = x_sample.shape[1]
    past_len = page_table.shape[1] * cache_b_kv.shape[2]
    pos_p = jnp.arange(S, dtype=jnp.int32)
    pos_s = past_len + jnp.arange(T, dtype=jnp.int32)
    a_p = ([], [], [])
    a_s = ([], [], [])
    b_p, b_s, c_p, c_s, dh_p, dh_s, dc_p, dc_s = [], [], [], [], [], [], [], []
    xp, xs = x_prompt, x_sample
    for i in range(DEPTH):
        m, j = i % N_MIXERS, i // N_MIXERS
        g = norm_gain[i]
        xp = xp + 0.5 * swiglu(rms_norm(xp, g[0]), w_ffn_up[i, 0], w_ffn_down[i, 0])
        xs = xs + 0.5 * swiglu(rms_norm(xs, g[0]), w_ffn_up[i, 0], w_ffn_down[i, 0])
        hp, hs = rms_norm(xp, g[1]), rms_norm(xs, g[1])
        if m == 0:
            op, osm, np_, ns_ = mixer_a(hp, hs, pos_p, pos_s,
                                        (cache_a_w1[j], cache_a_w2[j], cache_a_w3[j]),
                                        w_a_in[j], a_qk_gain[j], w_a_out[j])
            for gi in range(A_GROUPS):
                a_p[gi].append(np_[gi])
                a_s[gi].append(ns_[gi])
        elif m == 1:
            op, osm, kvp, kvs = mixer_b(hp, hs, pos_p, pos_s, cache_b_kv[j], page_table,
                                        w_b_in[j], b_qk_gain[j], w_b_out[j])
            b_p.append(kvp)
            b_s.append(kvs)
        elif m == 2:
            op, osm, sp_, ss_ = mixer_c(hp, hs, state_c[j], w_c_in[j], w_c_gate2[j], b_c_gate[j],
                                        c_norm_gain[j], w_c_out[j])
            c_p.append(sp_)
            c_s.append(ss_)
        else:
            op, osm, hpT, hsT, cpT, csT = mixer_d(hp, hs, state_d_ssm[j], state_d_conv[j], w_d_in[j],
                                                  d_conv_w[j], d_conv_b[j], d_dt_bias[j], d_a_log[j],
                                                  d_skip[j], d_norm_gain[j], w_d_out[j])
            dh_p.append(hpT)
            dh_s.append(hsT)
            dc_p.append(cpT)
            dc_s.append(csT)
        xp = xp + op
        xs = xs + osm
        xp = xp + 0.5 * swiglu(rms_norm(xp, g[2]), w_ffn_up[i, 1], w_ffn_down[i, 1])
        xs = xs + 0.5 * swiglu(rms_norm(xs, g[2]), w_ffn_up[i, 1], w_ffn_down[i, 1])
    return (xp, xs,
            jnp.stack(a_p[0]), jnp.stack(a_s[0]), jnp.stack(a_p[1]), jnp.stack(a_s[1]),
            jnp.stack(a_p[2]), jnp.stack(a_s[2]),
            jnp.stack(b_p), jnp.stack(b_s), jnp.stack(c_p), jnp.stack(c_s),
            jnp.stack(dh_p), jnp.stack(dh_s), jnp.stack(dc_p), jnp.stack(dc_s))
```

```python
import contextlib
import numpy as np
import concourse.bass as bass
import concourse.mybir as mybir
from concourse.bass_utils import run_bass_kernel_spmd

F32 = mybir.dt.float32
BF16 = mybir.dt.bfloat16
I32 = mybir.dt.int32
AF = mybir.ActivationFunctionType
ALU = mybir.AluOpType
AX = mybir.AxisListType

ENG_NAMES = ("sp", "act", "dve", "pool", "pe")


class Region:
    __slots__ = ("name", "writer", "readers")

    def __init__(self, name):
        self.name = name
        self.writer = None
        self.readers = []


class Op:
    __slots__ = ("eng", "fn", "deps", "dma", "idx", "signal", "sig_val", "slot", "slot_prev")

    def __init__(self, eng, fn, dma):
        self.eng = eng
        self.fn = fn
        self.dma = dma
        self.deps = []
        self.signal = False
        self.sig_val = 0
        self.slot = None
        self.slot_prev = None


class Sched:
    def __init__(self, nc, n_dma_slots=24, same_engine_sync=True):
        self.nc = nc
        self.ops = {e: [] for e in ENG_NAMES}
        self.n_dma_slots = n_dma_slots
        self.same_engine_sync = same_engine_sync
        self.dma_count = {e: 0 for e in ENG_NAMES}
        self.slot_last = {}

    def region(self, name="r"):
        return Region(name)

    def regions(self, name, n):
        return [Region(f"{name}{i}") for i in range(n)]

    def op(self, eng, fn, reads=(), writes=(), dma=False):
        o = Op(eng, fn, dma)
        deps = []
        for r in reads:
            if r.writer is not None:
                deps.append(r.writer)
        for w in writes:
            if w.writer is not None:
                deps.append(w.writer)
            deps.extend(w.readers)
        seen = set()
        for d in deps:
            if id(d) in seen or d is o:
                continue
            seen.add(id(d))
            if d.eng == eng and not d.dma:
                if eng == "pe" or not self.same_engine_sync:
                    continue
            o.deps.append(d)
            d.signal = True
        for r in reads:
            r.readers.append(o)
        for w in writes:
            w.writer = o
            w.readers = []
        if dma:
            k = self.dma_count[eng]
            self.dma_count[eng] = k + 1
            o.slot = (eng, k % self.n_dma_slots)
            o.sig_val = 16 * (k // self.n_dma_slots + 1)
            o.slot_prev = self.slot_last.get(o.slot)
            if o.slot_prev is not None:
                o.slot_prev.signal = True
            self.slot_last[o.slot] = o
            o.signal = True
        self.ops[eng].append(o)
        return o

    def barrier(self):
        lasts = []
        for e in ENG_NAMES:
            for o in reversed(self.ops[e]):
                if o.fn is not None and not o.dma:
                    lasts.append(o)
                    break
        lasts += list(self.slot_last.values())
        for e in ENG_NAMES:
            if not self.ops[e]:
                continue
            o = Op(e, None, False)
            for d in lasts:
                if d.eng == e and not d.dma:
                    continue
                if all(d is not x for x in o.deps):
                    o.deps.append(d)
                    d.signal = True
            self.ops[e].append(o)

    def dma(self, eng, out, in_, reads=(), writes=(), **kw):
        return self.op(eng, lambda e: e.dma_start(out=out, in_=in_, **kw), reads, writes, dma=True)

    def emit(self, final_waits=()):
        nc = self.nc
        import contextlib
        with contextlib.ExitStack() as st:
            esem = {e: st.enter_context(nc.semaphore(f"s_{e}")) for e in ENG_NAMES}
            dsem = {}
            for e in ENG_NAMES:
                if self.dma_count[e] > 0:
                    for s in range(min(self.n_dma_slots, self.dma_count[e])):
                        dsem[(e, s)] = st.enter_context(nc.semaphore(f"d_{e}_{s}"))
            for e in ENG_NAMES:
                c = 0
                for o in self.ops[e]:
                    if not o.dma and o.signal:
                        c += 1
                        o.sig_val = c
            block = st.enter_context(nc.Block())

            def run(ename):
                def body(eng):
                    known = {}
                    for o in self.ops[ename]:
                        waits = []
                        deps = list(o.deps)
                        if o.dma and o.slot_prev is not None:
                            deps.append(o.slot_prev)
                        for d in deps:
                            sem = dsem[d.slot] if d.dma else esem[d.eng]
                            key = d.slot if d.dma else d.eng
                            if known.get(key, 0) >= d.sig_val:
                                continue
                            known[key] = d.sig_val
                            waits.append((sem, d.sig_val))
                        for sem, v in waits:
                            eng.wait_ge(sem, v)
                        if o.fn is None:
                            continue
                        ins = o.fn(eng)
                        if o.signal:
                            if o.dma:
                                ins.then_inc(dsem[o.slot], 16)
                            else:
                                ins.then_inc(esem[ename], 1)
                    if ename in final_waits:
                        for slot, last in self.slot_last.items():
                            if slot[0] == ename and known.get(slot, 0) < last.sig_val:
                                eng.wait_ge(dsem[slot], last.sig_val)
                return body

            if self.ops["sp"] or "sp" in final_waits:
                block.sync(run("sp"))
            if self.ops["act"]:
                block.scalar(run("act"))
            if self.ops["dve"]:
                block.vector(run("dve"))
            if self.ops["pool"] or "pool" in final_waits:
                block.gpsimd(run("pool"))
            if self.ops["pe"]:
                block.tensor(run("pe"))


D = 1024
DC = 8
DFF = 2816
NF = 22
EPS = 1e-6
NSAMP = 64
FPARTS = [(0, 4), (4, 8), (8, 12), (12, 16), (16, 19), (19, 22)]


class K:
    pass


def build_program(T=4096, depth=4, layer_mixers=(), n_ffn_parts=None, dbg_stop=99, dbg=(), NPOOL=2560):
    nc = bass.Bass("TRN2", target_bir_lowering=False)
    TT = T + NSAMP
    groups = [(s, 512) for s in range(0, T, 512)] + [(T, NSAMP)]
    tiles = [(s, 128) for s in range(0, T, 128)] + [(T, NSAMP)]

    def din(name, shape, dt=F32):
        return nc.dram_tensor(name, list(shape), dt, kind="ExternalInput").ap()

    def dout(name, shape, dt=F32):
        return nc.dram_tensor(name, list(shape), dt, kind="ExternalOutput").ap()

    xp = din("xp", [T, D])
    xsm = din("xsm", [NSAMP, D])
    norm_gain = din("norm_gain", [4, 3, D])
    w_up = din("w_ffn_up", [4, 2, D, 2 * DFF])
    w_down = din("w_ffn_down", [4, 2, DFF, D])
    ident_d = din("ident", [128, 128])
    triu_d = din("triu", [32, 32])
    NB = NSAMP // 4
    if "c" in layer_mixers:
        sc_in = din("sc_in", [NB, 4, 128, 256])
        w_c_in = din("w_c_in", [D, 3088])
        w_c_gate2 = din("w_c_gate2", [16, 512])
        b_c_gate = din("b_c_gate", [512])
        c_norm_gain = din("c_norm_gain", [256])
        w_c_out = din("w_c_out", [D, D])
        csp = dout("csp", [4, 128, 256])
        css = dout("css", [NB, 4, 128, 256])
    if "a" in layer_mixers:
        w_a_in = din("w_a_in", [D, 4608])
        a_qk_gain = din("a_qk_gain", [2, 64])
        w_a_out = din("w_a_out", [512, D])
        ca1 = din("ca1", [NB, 128, 2, 8, 64])
        ca2 = din("ca2", [NB, 512, 2, 8, 64])
        ca3 = din("ca3", [NB, 2048, 2, 8, 64])
        rope_cos = din("rope_cos", [len(tiles) * 128, 8])
        rope_sin = din("rope_sin", [len(tiles) * 128, 8])
        mask2_d = din("mask2", [128, 256])
        awp = [dout(f"aw{g}p", [min(wn, T), 2, 512]) for g, wn in enumerate((128, 512, 2048))]
        aws = [dout(f"aw{g}s", [NSAMP, 2, 512]) for g in range(3)]
        qT_scr = nc.dram_tensor("qT_scr", [3, 4, 128, TT], BF16).ap()
        kT_scr = nc.dram_tensor("kT_scr", [3, 4, 128, TT], BF16).ap()
        vtok_scr = nc.dram_tensor("vtok_scr", [3, TT, 512], BF16).ap()
        qs_scr = nc.dram_tensor("qs_scr", [3, NSAMP, 512], F32).ap()
        kvs_scr = nc.dram_tensor("kvs_scr", [3, 2, NSAMP, 512], F32).ap()
        os_scr = nc.dram_tensor("os_scr", [NB, 8, 4, 64], F32).ap()
    if "d" in layer_mixers:
        w_d_in = din("w_d_in", [D, 5152])
        d_conv_w = din("d_conv_w", [4, 3072])
        d_conv_b = din("d_conv_b", [3072])
        d_dt_bias = din("d_dt_bias", [32])
        d_a_log = din("d_a_log", [32])
        d_skip = din("d_skip", [32])
        d_norm_gain = din("d_norm_gain", [2048])
        w_d_out = din("w_d_out", [2048, D])
        sd_ssm = din("sd_ssm", [NB, 32, 64, 128])
        sd_conv = din("sd_conv", [NB, 3, 3072])
        ssd_l1 = din("ssd_l1", [64, 64])
        ssd_l2 = din("ssd_l2", [64, 64])
        dssp = dout("dssp", [32, 64, 128])
        dsss = dout("dsss", [NB, 32, 64, 128])
        dcp = dout("dcp", [3, 3072])
        dcs = dout("dcs", [NB, 3, 3072])
        xbcT_scr = nc.dram_tensor("xbcT_scr", [8, 128, TT], BF16).ap()
        xtokd_scr = nc.dram_tensor("xtokd_scr", [TT, 2560], BF16).ap()
        zs_scr = nc.dram_tensor("zs_scr", [TT, 2048], BF16).ap()
        dt_scr = nc.dram_tensor("dt_scr", [2, TT, 32], F32).ap()
    if "b" in layer_mixers:
        w_b_in = din("w_b_in", [D, 1536])
        b_qk_gain = din("b_qk_gain", [2, 64])
        w_b_out = din("w_b_out", [D, D])
        pool_kv = din("pool_kv", [NPOOL, 128, 2, 256])
        page_tab = din("page_tab", [NB, 16], I32)
        kind_d = din("kind", [16, T])
        kinds_d = din("kinds", [8, 2048])
        mask3_d = din("mask3", [128, 256])
        mask4_d = din("mask4", [4, 16])
        pidx_d = din("pidx", [128, 1])
        if "a" not in layer_mixers:
            rope_cos = din("rope_cos", [len(tiles) * 128, 8])
            rope_sin = din("rope_sin", [len(tiles) * 128, 8])
        bkvp = dout("bkvp", [T, 2, 256])
        if 'dbgo' in dbg:
            dbgo = dout("dbgo", [16, 64, 512])
        bkvs = dout("bkvs", [NSAMP, 2, 256])
        kaug_scr = nc.dram_tensor("kaug_scr", [4, 80, TT], BF16).ap()
        qaug_scr = nc.dram_tensor("qaug_scr", [16, 80, TT], BF16).ap()
        vtokb_scr = nc.dram_tensor("vtokb_scr", [TT, 256], BF16).ap()
        oTb_scr = nc.dram_tensor("oTb_scr", [16, 64, TT], BF16).ap()
        kmT_scr = nc.dram_tensor("kmT_scr", [2, 128, T // 256], F32).ap()
        qsb_scr = nc.dram_tensor("qsb_scr", [NSAMP, 1024], F32).ap()
        ksb_scr = nc.dram_tensor("ksb_scr", [NSAMP, 512], F32).ap()
    yp = dout("yp", [T, D])
    ysm = dout("ysm", [NSAMP, D])
    xs = nc.dram_tensor("xs_scratch", [DC, 128, TT], F32).ap()

    with contextlib.ExitStack() as st:
        def sb(name, shape, dt=F32):
            return st.enter_context(nc.sbuf_tensor("sb_" + name, list(shape), dt))

        def ps(name, shape, dt=F32):
            return st.enter_context(nc.psum_tensor("pp_" + name, list(shape), dt))

        S = Sched(nc)
        R = S.region

        ident = sb("ident", [128, 128]); r_ident = R("ident")
        ones_bf = sb("ones_bf", [128, 128], BF16); r_ones = R("ones")
        gain = sb("gain", [128, 12, DC]); r_gain = R("gain")
        epsc = sb("epsc", [128, 1]); r_eps = R("eps")
        S.dma("sp", ident[:], ident_d, writes=[r_ident])
        identb = sb("identb", [128, 128], BF16); r_identb = R("identb")
        S.op("dve", lambda e: e.tensor_copy(out=identb[:], in_=ident[:]), [r_ident], [r_identb])
        S.op("dve", lambda e: e.memset(ones_bf[:], 1.0), [], [r_ones])
        S.op("dve", lambda e: e.memset(epsc[:], EPS), [], [r_eps])
        S.dma("sp", gain[:], norm_gain.rearrange("l k (c p) -> p (l k) c", p=128), writes=[r_gain],
              allow_slow_non_contiguous=True)

        _ng = len(groups)
        HW = max(sum(w_ for (_, w_) in groups[:_ng // 2]), sum(w_ for (_, w_) in groups[_ng // 2:]))
        NA_H = 7
        hT_flat = sb("hT", [128, max(DC * TT, (DC + NA_H) * HW)], BF16); r_hT = [R(f"hT{g}") for g in range(len(groups))]
        hT = hT_flat[:, 0:DC * TT].rearrange("p (c t) -> p c t", c=DC)
        rs = [sb(f"rs{i}", [128, 512]) for i in range(2)]; r_rs = [R(f"rs{i}") for i in range(2)]
        ARENA = 34816
        arena = sb("arena", [128, ARENA])

        class Carver:
            def __init__(self):
                self.off = 0

            def take(self, shape, dt=F32):
                n = 1
                for d in shape[1:]:
                    n *= d
                nw = n if dt in (F32, I32) else (n + 1) // 2
                nw = (nw + 7) // 8 * 8
                v = arena[:, self.off:self.off + nw]
                self.off += nw
                assert self.off <= ARENA, ("arena overflow", self.off)
                if dt != F32:
                    v = v.bitcast(dt)
                v = v[:, 0:n]
                if shape[0] < 128:
                    v = v[0:shape[0], :]
                if len(shape) > 2:
                    names = "abcdefg"[:len(shape) - 1]
                    kw = {names[i]: shape[i + 1] for i in range(len(shape) - 2)}
                    v = v.rearrange("p (" + " ".join(names) + ") -> p " + " ".join(names), **kw)
                return v

        cv = Carver()
        xg = [cv.take([128, DC, 512]) for i in range(2)]; r_xg = [R(f"xg{i}") for i in range(2)]
        sq = cv.take([128, DC, 512], BF16); r_sq = R("sq")
        cvf = Carver()
        aT_h = hT_flat[:, DC * HW:DC * HW + NA_H * HW].rearrange("p (j t) -> p j t", j=NA_H)
        aT_a = cvf.take([128, NF - NA_H, HW], BF16)
        hTl = hT_flat[:, 0:DC * HW].rearrange("p (c t) -> p c t", c=DC)
        fxg = [cvf.take([128, DC, 256]) for i in range(2)]; r_fxg = [R("fxg0"), R("fxg1")]
        fsq = cvf.take([128, DC, 256], BF16); r_fsq = R("fsq")
        sg = [cvf.take([128, 512], BF16) for i in range(4)]; r_sg = [R(f"sg{i}") for i in range(4)]
        wst = [cvf.take([128, DC, 256]) for i in range(2)]; r_wst = [R(f"wst{i}") for i in range(2)]
        wbf = [cvf.take([128, DC, 256], BF16) for i in range(2)]; r_wbf = [R(f"wbf{i}") for i in range(2)]
        wdst = cvf.take([128, NF, 128]); r_wdst = R("wdst")
        wdbf = [cvf.take([128, NF, 128], BF16) for i in range(2)]; r_wdbf = [R("wdbf0"), R("wdbf1")]
        xl = [cvf.take([128, 512]) for i in range(2)]; r_xl = [R("xl0"), R("xl1")]
        r_aTf = [R(f"aTf{g}") for g in range(len(groups))]
        r_hTl = [R(f"hTl{g}") for g in range(len(groups))]
        cv0 = Carver()
        xtok = [cv0.take([128, D]) for i in range(2)]; r_xtok = [R(f"xtok{i}") for i in range(2)]
        xfm = [cv0.take([128, DC, 128]) for i in range(2)]; r_xfm = [R(f"xfm{i}") for i in range(2)]
        PS = [ps(f"ps{i}", [128, 512]) for i in range(8)]
        PSR = [R(f"ps{i}") for i in range(8)]

        r_xs = [R(f"xs{g}") for g in range(len(groups))]

        def grp_of_tile(s):
            return len(groups) - 1 if s >= T else s // 512

        cnt = {"w": 0, "x": 0, "ps": 0}
        for ti, (s, w) in enumerate(tiles):
            b = ti % 2
            src = xp[s:s + w, :] if s < T else xsm[:, :]
            S.dma("sp", xtok[b][0:w, :], src, writes=[r_xtok[b]])

            def tr(e, b=b, w=w):
                ins = None
                for c in range(DC):
                    ins = e.transpose(PS[6 + c // 4][:, (c % 4) * 128:(c % 4) * 128 + w],
                                      xtok[b][0:w, c * 128:(c + 1) * 128], ident[0:w, 0:w])
                return ins
            S.op("pe", tr, [r_xtok[b], r_ident], [PSR[6], PSR[7]])
            for hh in range(2):
                S.op("act", lambda e, b=b, w=w, hh=hh: e.copy(
                    out=xfm[b][:, 4 * hh:4 * hh + 4, 0:w],
                    in_=PS[6 + hh][:].rearrange("p (c t) -> p c t", c=4)[:, :, 0:w]),
                    [PSR[6 + hh]], [r_xfm[b]])
            S.dma("sp", xs[:, :, s:s + w].rearrange("c p t -> p c t"), xfm[b][:, :, 0:w],
                  reads=[r_xfm[b]], writes=[r_xs[grp_of_tile(s)]])

        S.barrier()

        def norm_pass(gk):
            for gi, (s, w) in enumerate(groups):
                b = gi % 2
                S.dma("sp", xg[b][:, :, 0:w], xs[:, :, s:s + w].rearrange("c p t -> p c t"),
                      reads=[r_xs[gi]], writes=[r_xg[b]])
                S.op("act", lambda e, b=b, w=w: e.activation(out=sq[:, :, 0:w], in_=xg[b][:, :, 0:w], func=AF.Square),
                     [r_xg[b]], [r_sq])
                pb = 4 + (gi % 2)

                def nm(e, w=w, pb=pb):
                    ins = None
                    for c in range(DC):
                        ins = e.matmul(PS[pb][:, 0:w], lhsT=ones_bf[:], rhs=sq[:, c, 0:w], start=(c == 0), stop=(c == DC - 1))
                    return ins
                S.op("pe", nm, [r_sq, r_ones], [PSR[pb]])
                S.op("act", lambda e, b=b, w=w, pb=pb: e.activation(out=rs[b][:, 0:w], in_=PS[pb][:, 0:w], func=AF.Ln,
                                                                    bias=epsc[:], scale=1.0 / D),
                     [PSR[pb], r_eps], [r_rs[b]])
                S.op("act", lambda e, b=b, w=w: e.activation(out=rs[b][:, 0:w], in_=rs[b][:, 0:w], func=AF.Exp, scale=-0.5),
                     [r_rs[b]], [r_rs[b]])

                def hmul(e, b=b, w=w, s=s):
                    ins = None
                    for c in range(DC):
                        ins = e.scalar_tensor_tensor(out=hT[:, c, s:s + w], in0=xg[b][:, c, 0:w],
                                                     scalar=gain[:, gk, c:c + 1], in1=rs[b][:, 0:w],
                                                     op0=ALU.mult, op1=ALU.mult)
                    return ins
                S.op("dve", hmul, [r_xg[b], r_rs[b], r_gain], [r_hT[gi]])

        def aT_of(j):
            return aT_h[:, j] if j < NA_H else aT_a[:, j - NA_H]

        def ffn(l, i):
            gk = l * 3 + (0 if i == 0 else 2)
            S.barrier()
            NG = len(groups)
            for gl in (list(range(0, NG // 2)), list(range(NG // 2, NG))):
                base = groups[gl[0]][0]
                subs = []
                for gi in gl:
                    s0, w0 = groups[gi]
                    if w0 == 512:
                        subs += [(gi, s0, 256), (gi, s0 + 256, 256)]
                    else:
                        subs.append((gi, s0, w0))
                for k_, (gi, s_, w) in enumerate(subs):
                    b = k_ % 2
                    S.dma("sp", fxg[b][:, :, 0:w], xs[:, :, s_:s_ + w].rearrange("c p t -> p c t"), reads=[r_xs[gi]], writes=[r_fxg[b]])
                    S.op("act", lambda e, b=b, w=w: e.activation(out=fsq[:, :, 0:w], in_=fxg[b][:, :, 0:w], func=AF.Square),
                         [r_fxg[b]], [r_fsq])
                    pb = 6 + b

                    def nm(e, w=w, pb=pb):
                        ins = None
                        for c in range(DC):
                            ins = e.matmul(PS[pb][:, 0:w], lhsT=ones_bf[:], rhs=fsq[:, c, 0:w], start=(c == 0), stop=(c == DC - 1))
                        return ins
                    S.op("pe", nm, [r_fsq, r_ones], [PSR[pb]])
                    S.op("act", lambda e, b=b, w=w, pb=pb: e.activation(out=rs[b][:, 0:w], in_=PS[pb][:, 0:w], func=AF.Ln,
                                                                        bias=epsc[:], scale=1.0 / D), [PSR[pb], r_eps], [r_rs[b]])
                    S.op("act", lambda e, b=b, w=w: e.activation(out=rs[b][:, 0:w], in_=rs[b][:, 0:w], func=AF.Exp, scale=-0.5),
                         [r_rs[b]], [r_rs[b]])

                    def hmul(e, b=b, w=w, lo=s_ - base):
                        ins = None
                        for c in range(DC):
                            ins = e.scalar_tensor_tensor(out=hTl[:, c, lo:lo + w], in0=fxg[b][:, c, 0:w], scalar=gain[:, gk, c:c + 1],
                                                         in1=rs[b][:, 0:w], op0=ALU.mult, op1=ALU.mult)
                        return ins
                    S.op("dve", hmul, [r_fxg[b], r_rs[b], r_gain], [r_hTl[gi]])
                wsrc = w_up[l, i].rearrange("(c p) n -> p c n", p=128)
                for j in range(NF):
                    wb = cnt["w"] % 2
                    cnt["w"] += 1
                    S.dma("sp", wst[wb][:, :, 0:128], wsrc[:, :, j * 128:(j + 1) * 128], writes=[r_wst[wb]])
                    S.dma("sp", wst[wb][:, :, 128:256], wsrc[:, :, DFF + j * 128:DFF + (j + 1) * 128], writes=[r_wst[wb]])
                    S.op("pool", lambda e, wb=wb: e.tensor_copy(out=wbf[wb][:], in_=wst[wb][:]), [r_wst[wb]], [r_wbf[wb]])
                    for gi in gl:
                        s_, w = groups[gi]
                        lo = s_ - base
                        pb = cnt["ps"] % 4
                        cnt["ps"] += 1

                        def up(e, wb=wb, lo=lo, w=w, pb=pb):
                            ins = None
                            for c in range(DC):
                                ins = e.matmul(PS[pb][:, 0:w], lhsT=wbf[wb][:, c, 0:128], rhs=hTl[:, c, lo:lo + w],
                                               start=(c == 0), stop=(c == DC - 1))
                            for c in range(DC):
                                ins = e.matmul(PS[4 + pb][:, 0:w], lhsT=wbf[wb][:, c, 128:256], rhs=hTl[:, c, lo:lo + w],
                                               start=(c == 0), stop=(c == DC - 1))
                            return ins
                        S.op("pe", up, [r_wbf[wb], r_hTl[gi]], [PSR[pb], PSR[4 + pb]])
                        S.op("act", lambda e, pb=pb, w=w: e.activation(out=sg[pb][:, 0:w], in_=PS[pb][:, 0:w], func=AF.Silu),
                             [PSR[pb]], [r_sg[pb]])
                        S.op("dve", lambda e, pb=pb, w=w, lo=lo, j=j: e.tensor_tensor(
                            out=aT_of(j)[:, lo:lo + w], in0=sg[pb][:, 0:w], in1=PS[4 + pb][:, 0:w], op=ALU.mult),
                            [r_sg[pb], PSR[4 + pb]], [r_aTf[gi]])
                steps = [(dc, gi) for dc in range(DC) for gi in gl]

                def issue_xload(k):
                    dc, gi = steps[k]
                    s_, w = groups[gi]
                    S.dma("sp", xl[k % 2][:, 0:w], xs[dc, :, s_:s_ + w], reads=[r_xs[gi]], writes=[r_xl[k % 2]])
                issue_xload(0)
                for k, (dc, gi) in enumerate(steps):
                    s_, w = groups[gi]
                    lo = s_ - base
                    db = dc % 2
                    if gi == gl[0]:
                        S.dma("sp", wdst[:], w_down[l, i, :, dc * 128:(dc + 1) * 128].rearrange("(j p) n -> p j n", p=128), writes=[r_wdst])
                        S.op("pool", lambda e, db=db: e.tensor_copy(out=wdbf[db][:], in_=wdst[:]), [r_wdst], [r_wdbf[db]])
                    if k + 1 < len(steps):
                        issue_xload(k + 1)
                    pb = 4 + (k % 2)

                    def dn(e, db=db, lo=lo, w=w, pb=pb):
                        ins = None
                        for j in range(NF):
                            ins = e.matmul(PS[pb][:, 0:w], lhsT=wdbf[db][:, j, :], rhs=aT_of(j)[:, lo:lo + w], start=(j == 0), stop=(j == NF - 1))
                        return ins
                    S.op("pe", dn, [r_wdbf[db], r_aTf[gi]], [PSR[pb]])
                    S.op("dve", lambda e, k=k, w=w, pb=pb: e.scalar_tensor_tensor(
                        out=xl[k % 2][:, 0:w], in0=PS[pb][:, 0:w], scalar=0.5, in1=xl[k % 2][:, 0:w], op0=ALU.mult, op1=ALU.add),
                        [PSR[pb], r_xl[k % 2]], [r_xl[k % 2]])
                    S.dma("act", xs[dc, :, s_:s_ + w], xl[k % 2][:, 0:w], reads=[r_xl[k % 2]], writes=[r_xs[gi]])
            S.barrier()

        def mixer_c(l):
            norm_pass(l * 3 + 1)
            S.barrier()
            cv = Carver()
            wcin = cv.take([128, DC, 3088], BF16); r_wcin = R("wcin")
            wcout = cv.take([128, DC, D], BF16); r_wcout = R("wcout")
            stg = [cv.take([128, DC, 256]) for _ in range(2)]; r_stg = [R("stg0"), R("stg1")]
            wg2s = cv.take([16, 512]); wg2 = cv.take([16, 512], BF16); r_wg2 = R("wg2")
            negb = cv.take([128, 4]); r_negb = R("negb")
            cng = cv.take([128, 2]); r_cng = R("cng")
            triu = cv.take([32, 32]); r_triu = R("triu")
            glr = cv.take([16, 256], BF16); r_glr = R("glr")
            W = 256
            l1 = cv.take([128, W]); r_l1 = R("l1")
            csA = cv.take([128, W]); csB = cv.take([128, W]); r_cs = [R("csA"), R("csB")]
            eb = cv.take([128, W]); enb = cv.take([128, W]); ekd = cv.take([128, W]); r_e = R("e")
            dec = cv.take([128, 16]); r_dec = R("dec")
            qe = cv.take([128, W], BF16); ke = cv.take([128, W], BF16); kd = cv.take([128, W], BF16); r_qk = R("qk")
            kdt = [cv.take([32, 128], BF16) for _ in range(2)]; r_kdt = [R("kdt0"), R("kdt1")]
            vt = [cv.take([32, 256], BF16) for _ in range(2)]; r_vt = [R("vt0"), R("vt1")]
            att = [cv.take([32, 32], BF16) for _ in range(2)]; r_att = [R("att0"), R("att1")]
            Sf = cv.take([128, 4, 256]); r_Sf = [R(f"Sf{h}") for h in range(4)]
            Sb = cv.take([128, 4, 256], BF16); r_Sb = [R(f"Sb{h}") for h in range(4)]
            oT = cv.take([128, 8, W]); r_oT = R("oT")
            osq = cv.take([128, 8, W], BF16); r_osq = R("osq")
            o2 = cv.take([128, 8, W], BF16); r_o2 = R("o2")
            rstd = cv.take([128, W]); r_rstd = R("rstd")
            sgr = cv.take([128, W]); r_sgr = R("sgr")
            t1 = cv.take([128, W]); r_t1 = R("t1")
            xu = cv.take([128, DC, W]); r_xu = R("xu")
            S.dma("sp", triu[:], triu_d, writes=[r_triu])
            S.dma("sp", wg2s[:], w_c_gate2, writes=[r_wg2])
            S.op("pool", lambda e: e.tensor_copy(out=wg2[:], in_=wg2s[:]), [r_wg2], [r_wg2])
            S.dma("sp", negb[:], b_c_gate.rearrange("(h p) -> p h", p=128), writes=[r_negb], allow_slow_non_contiguous=True)
            S.op("pool", lambda e: e.tensor_scalar(out=negb[:], in0=negb[:], scalar1=-1.0, scalar2=None, op0=ALU.mult),
                 [r_negb], [r_negb])
            S.dma("sp", cng[:], c_norm_gain.rearrange("(e p) -> p e", p=128), writes=[r_cng], allow_slow_non_contiguous=True)
            wsrc = w_c_in.rearrange("(c p) n -> p c n", p=128)
            k = 0
            for c0 in range(0, 3088, 256):
                cw = min(256, 3088 - c0)
                b = k % 2; k += 1
                S.dma("sp", stg[b][:, :, 0:cw], wsrc[:, :, c0:c0 + cw], writes=[r_stg[b]])
                S.op("pool", lambda e, b=b, c0=c0, cw=cw: e.tensor_copy(out=wcin[:, :, c0:c0 + cw], in_=stg[b][:, :, 0:cw]),
                     [r_stg[b]], [r_wcin])
            wsrc2 = w_c_out.rearrange("(c p) n -> p c n", p=128)
            for c0 in range(0, D, 256):
                b = k % 2; k += 1
                S.dma("sp", stg[b][:], wsrc2[:, :, c0:c0 + 256], writes=[r_stg[b]])
                S.op("pool", lambda e, b=b, c0=c0: e.tensor_copy(out=wcout[:, :, c0:c0 + 256], in_=stg[b][:]),
                     [r_stg[b]], [r_wcout])
            for h in range(4):
                S.op("dve", lambda e, h=h: e.memset(Sf[:, h, :], 0.0), [], [r_Sf[h]])
                S.op("dve", lambda e, h=h: e.memset(Sb[:, h, :], 0.0), [], [r_Sb[h]])

            cgroups = [(s0, W, 32, False) for s0 in range(0, T, W)] + [(T, NSAMP, 4, True)]
            cc = 0
            for (s, w, C, is_s) in cgroups:
                nch = w // C
                gi = grp_of_tile(s)
                S.dma("sp", xu[:, :, 0:w], xs[:, :, s:s + w].rearrange("c p t -> p c t"), reads=[r_xs[gi]], writes=[r_xu])

                def mm8(e, pst, col0, ncol, s=s, w=w):
                    ins = None
                    for c in range(DC):
                        ins = e.matmul(pst, lhsT=wcin[:, c, col0:col0 + ncol], rhs=hT[:, c, s:s + w],
                                       start=(c == 0), stop=(c == DC - 1))
                    return ins
                S.op("pe", lambda e, w=w, mm8=mm8: mm8(e, PS[2][0:16, 0:w], 3072, 16), [r_wcin, r_hT[gi]], [PSR[2]])
                S.op("act", lambda e, w=w: e.copy(out=glr[:, 0:w], in_=PS[2][0:16, 0:w]), [PSR[2]], [r_glr])
                for h in range(4):
                    S.op("pe", lambda e, w=w, h=h, mm8=mm8: mm8(e, PS[0][:, 0:w], h * 128, 128), [r_wcin, r_hT[gi]], [PSR[0]])
                    S.op("pe", lambda e, w=w, h=h, mm8=mm8: mm8(e, PS[1][:, 0:w], 512 + h * 128, 128), [r_wcin, r_hT[gi]], [PSR[1]])
                    S.op("pe", lambda e, w=w, h=h: e.matmul(PS[2][:, 0:w], lhsT=wg2[:, h * 128:(h + 1) * 128], rhs=glr[:, 0:w],
                                                            start=True, stop=True), [r_wg2, r_glr], [PSR[2]])
                    S.op("act", lambda e, w=w, h=h: e.activation(out=l1[:, 0:w], in_=PS[2][:, 0:w], func=AF.Exp,
                                                                 bias=negb[:, h:h + 1], scale=-1.0), [PSR[2], r_negb], [r_l1])
                    S.op("act", lambda e, w=w: e.activation(out=csA[:, 0:w], in_=l1[:, 0:w], func=AF.Ln, bias=1.0, scale=1.0),
                         [r_l1], [r_cs[0]])
                    bufs = [csA, csB]
                    cur = 0
                    kk = 1
                    while kk < C:
                        src = bufs[cur][:, 0:w].rearrange("p (n c) -> p n c", c=C)
                        dst = bufs[1 - cur][:, 0:w].rearrange("p (n c) -> p n c", c=C)
                        S.op("dve", lambda e, src=src, dst=dst, kk=kk: e.tensor_copy(out=dst[:, :, 0:kk], in_=src[:, :, 0:kk]),
                             [r_cs[cur]], [r_cs[1 - cur]])
                        S.op("dve", lambda e, src=src, dst=dst, kk=kk, C=C: e.tensor_tensor(
                            out=dst[:, :, kk:C], in0=src[:, :, kk:C], in1=src[:, :, 0:C - kk], op=ALU.add),
                            [r_cs[cur]], [r_cs[1 - cur]])
                        cur = 1 - cur
                        kk *= 2
                    csv = bufs[cur]; r_csv = r_cs[cur]
                    oth = bufs[1 - cur]; r_oth = r_cs[1 - cur]
                    cs3 = csv[:, 0:w].rearrange("p (n c) -> p n c", c=C)
                    S.op("act", lambda e, w=w, csv=csv: e.activation(out=eb[:, 0:w], in_=csv[:, 0:w], func=AF.Exp, scale=-1.0 / 16),
                         [r_csv], [r_e])
                    S.op("act", lambda e, w=w, csv=csv: e.activation(out=enb[:, 0:w], in_=csv[:, 0:w], func=AF.Exp, scale=1.0 / 16),
                         [r_csv], [r_e])
                    S.op("act", lambda e, cs3=cs3, nch=nch, C=C: e.activation(out=dec[:, 0:nch], in_=cs3[:, :, C - 1], func=AF.Exp,
                                                                           scale=-1.0 / 16), [r_csv], [r_dec])
                    S.op("dve", lambda e, cs3=cs3, oth=oth, w=w, C=C, nch=nch: e.tensor_tensor(
                        out=oth[:, 0:w].rearrange("p (n c) -> p n c", c=C), in0=cs3,
                        in1=cs3[:, :, C - 1:C].to_broadcast([128, nch, C]), op=ALU.subtract), [r_csv], [r_oth])
                    S.op("act", lambda e, w=w, oth=oth: e.activation(out=ekd[:, 0:w], in_=oth[:, 0:w], func=AF.Exp, scale=1.0 / 16),
                         [r_oth], [r_e])
                    S.op("dve", lambda e, w=w: e.scalar_tensor_tensor(out=qe[:, 0:w], in0=PS[0][:, 0:w], scalar=128 ** -0.5,
                                                                      in1=eb[:, 0:w], op0=ALU.mult, op1=ALU.mult),
                         [PSR[0], r_e], [r_qk])
                    S.op("dve", lambda e, w=w: e.tensor_tensor(out=ke[:, 0:w], in0=PS[1][:, 0:w], in1=enb[:, 0:w], op=ALU.mult),
                         [PSR[1], r_e], [r_qk])
                    S.op("dve", lambda e, w=w: e.tensor_tensor(out=kd[:, 0:w], in0=PS[1][:, 0:w], in1=ekd[:, 0:w], op=ALU.mult),
                         [PSR[1], r_e], [r_qk])
                    for m in range(nch):
                        pb = cc % 2; cc += 1
                        t0, t1_ = m * C, (m + 1) * C
                        if is_s:
                            S.dma("sp", Sf[:, h, :], sc_in[m, h], writes=[r_Sf[h]])
                            S.op("act", lambda e, h=h: e.copy(out=Sb[:, h, :], in_=Sf[:, h, :]), [r_Sf[h]], [r_Sb[h]])
                        def vproj(e, s=s, t0=t0, C=C, h=h):
                            ins = None
                            for c in range(DC):
                                ins = e.matmul(PS[3][0:C, 0:256], lhsT=hT[:, c, s + t0:s + t0 + C],
                                               rhs=wcin[:, c, 1024 + h * 256:1024 + (h + 1) * 256],
                                               start=(c == 0), stop=(c == DC - 1))
                            return ins
                        S.op("pe", vproj, [r_wcin, r_hT[gi]], [PSR[3]])
                        S.op("act", lambda e, pb=pb, C=C: e.copy(out=vt[pb][0:C, :], in_=PS[3][0:C, 0:256]), [PSR[3]], [r_vt[pb]])
                        pst = PS[4][:, 0:64].bitcast(BF16)
                        S.op("pe", lambda e, t0=t0, t1_=t1_, C=C, pst=pst: e.transpose(pst[0:C, 0:128], kd[:, t0:t1_], identb[:]),
                             [r_qk, r_identb], [PSR[4]])
                        S.op("dve", lambda e, pb=pb, C=C, pst=pst: e.tensor_copy(out=kdt[pb][0:C, :], in_=pst[0:C, 0:128]),
                             [PSR[4]], [r_kdt[pb]])
                        S.op("pe", lambda e, t0=t0, t1_=t1_, C=C: e.matmul(PS[5][0:C, 0:C], lhsT=ke[:, t0:t1_], rhs=qe[:, t0:t1_],
                                                                          start=True, stop=True), [r_qk], [PSR[5]])
                        S.op("dve", lambda e, pb=pb, C=C: e.tensor_tensor(out=att[pb][0:C, 0:C], in0=PS[5][0:C, 0:C],
                                                                          in1=triu[0:C, 0:C], op=ALU.mult),
                             [PSR[5], r_triu], [r_att[pb]])
                        def omm(e, pb=pb, C=C, h=h, t0=t0, t1_=t1_):
                            ins = None
                            for ee in range(2):
                                e.matmul(PS[6][:, ee * C:(ee + 1) * C], lhsT=vt[pb][0:C, ee * 128:(ee + 1) * 128],
                                         rhs=att[pb][0:C, 0:C], start=True, stop=False)
                                ins = e.matmul(PS[6][:, ee * C:(ee + 1) * C], lhsT=Sb[:, h, ee * 128:(ee + 1) * 128],
                                               rhs=qe[:, t0:t1_], start=False, stop=True)
                            return ins
                        S.op("pe", omm, [r_vt[pb], r_att[pb], r_Sb[h], r_qk], [PSR[6]])
                        S.op("act", lambda e, h=h, C=C, t0=t0, t1_=t1_: e.copy(
                            out=oT[:, 2 * h:2 * h + 2, t0:t1_], in_=PS[6][:, 0:2 * C].rearrange("p (a c) -> p a c", a=2)),
                            [PSR[6]], [r_oT])
                        S.op("pe", lambda e, pb=pb, C=C: e.matmul(PS[7][:, 0:256], lhsT=kdt[pb][0:C, :], rhs=vt[pb][0:C, :],
                                                                  start=True, stop=True), [r_kdt[pb], r_vt[pb]], [PSR[7]])
                        S.op("dve", lambda e, h=h, m=m: e.scalar_tensor_tensor(
                            out=Sf[:, h, :], in0=Sf[:, h, :], scalar=dec[:, m:m + 1], in1=PS[7][:, 0:256],
                            op0=ALU.mult, op1=ALU.add), [r_Sf[h], r_dec, PSR[7]], [r_Sf[h]])
                        if is_s:
                            S.dma("sp", css[m, h], Sf[:, h, :], reads=[r_Sf[h]])
                        else:
                            S.op("act", lambda e, h=h: e.copy(out=Sb[:, h, :], in_=Sf[:, h, :]), [r_Sf[h]], [r_Sb[h]])
                S.op("act", lambda e, w=w: e.activation(out=osq[:, :, 0:w], in_=oT[:, :, 0:w], func=AF.Square), [r_oT], [r_osq])
                for h in range(4):
                    def nmm(e, h=h, w=w):
                        e.matmul(PS[2][:, 0:w], lhsT=ones_bf[:], rhs=osq[:, 2 * h, 0:w], start=True, stop=False)
                        return e.matmul(PS[2][:, 0:w], lhsT=ones_bf[:], rhs=osq[:, 2 * h + 1, 0:w], start=False, stop=True)
                    S.op("pe", nmm, [r_osq, r_ones], [PSR[2]])
                    S.op("act", lambda e, w=w: e.activation(out=rstd[:, 0:w], in_=PS[2][:, 0:w], func=AF.Ln, bias=epsc[:],
                                                            scale=1.0 / 256), [PSR[2], r_eps], [r_rstd])
                    S.op("act", lambda e, w=w: e.activation(out=rstd[:, 0:w], in_=rstd[:, 0:w], func=AF.Exp, scale=-0.5),
                         [r_rstd], [r_rstd])
                    for ee in range(2):
                        ch = 2 * h + ee
                        pb = ch % 2
                        S.op("pe", lambda e, w=w, ch=ch, pb=pb, mm8=mm8: mm8(e, PS[pb][:, 0:w], 2048 + ch * 128, 128),
                             [r_wcin, r_hT[gi]], [PSR[pb]])
                        S.op("act", lambda e, w=w, pb=pb: e.activation(out=sgr[:, 0:w], in_=PS[pb][:, 0:w], func=AF.Silu),
                             [PSR[pb]], [r_sgr])
                        S.op("dve", lambda e, w=w, ch=ch, ee=ee: e.scalar_tensor_tensor(
                            out=t1[:, 0:w], in0=oT[:, ch, 0:w], scalar=cng[:, ee:ee + 1], in1=rstd[:, 0:w],
                            op0=ALU.mult, op1=ALU.mult), [r_oT, r_cng, r_rstd], [r_t1])
                        S.op("dve", lambda e, w=w, ch=ch: e.tensor_tensor(out=o2[:, ch, 0:w], in0=t1[:, 0:w], in1=sgr[:, 0:w],
                                                                          op=ALU.mult), [r_t1, r_sgr], [r_o2])
                for dc in range(DC):
                    pb = dc % 2

                    def ymm(e, dc=dc, pb=pb, w=w):
                        ins = None
                        for ch in range(8):
                            ins = e.matmul(PS[pb][:, 0:w], lhsT=wcout[:, ch, dc * 128:(dc + 1) * 128], rhs=o2[:, ch, 0:w],
                                           start=(ch == 0), stop=(ch == 7))
                        return ins
                    S.op("pe", ymm, [r_wcout, r_o2], [PSR[pb]])
                    S.op("dve", lambda e, dc=dc, pb=pb, w=w: e.tensor_tensor(out=xu[:, dc, 0:w], in0=PS[pb][:, 0:w],
                                                                              in1=xu[:, dc, 0:w], op=ALU.add),
                         [PSR[pb], r_xu], [r_xu])
                S.dma("sp", xs[:, :, s:s + w].rearrange("c p t -> p c t"), xu[:, :, 0:w], reads=[r_xu], writes=[r_xs[gi]])
                if (not is_s) and s + w == T:
                    for h in range(4):
                        S.dma("sp", csp[h], Sf[:, h, :], reads=[r_Sf[h]])
            S.barrier()

        def mixer_a(l):
            norm_pass(l * 3 + 1)
            S.barrier()
            NT = len(tiles)
            DIL = (1, 4, 16)
            WIN = tuple(min(x_, T) for x_ in (128, 512, 2048))
            cv = Carver()
            wa_st = [cv.take([128, DC, 256]) for _ in range(2)]; r_wa_st = [R("wa_st0"), R("wa_st1")]
            wa_bf = [cv.take([128, DC, 512], BF16) for _ in range(2)]; r_wa_bf = [R("wa_bf0"), R("wa_bf1")]
            sqv = cv.take([128, 512]); r_sqv = R("sqv")
            xn = [cv.take([128, 512]) for _ in range(2)]; r_xn = [R("xn0"), R("xn1")]
            xb = [cv.take([128, 512], BF16) for _ in range(2)]; r_xb = [R("xb0"), R("xb1")]
            rtmp = cv.take([128, 4, 64]); r_rtmp = R("rtmp")
            xT = [cv.take([128, 4, 128], BF16) for _ in range(2)]; r_xT = [R("xT0"), R("xT1")]
            ss = cv.take([128, 8]); r_ss = R("ss")
            grep_ = cv.take([128, 2, 64]); r_grep = R("grep")
            cosT = cv.take([128, NT, 8]); sinT = cv.take([128, NT, 8]); r_cs = R("cossin")
            S.dma("sp", grep_[:], a_qk_gain.rearrange("j e -> (j e)").partition_broadcast(128)
                  .rearrange("p (j e) -> p j e", j=2), writes=[r_grep])
            S.dma("sp", cosT[:], rope_cos.rearrange("(n p) i -> p n i", p=128), writes=[r_cs])
            S.dma("sp", sinT[:], rope_sin.rearrange("(n p) i -> p n i", p=128), writes=[r_cs])
            r_qT = R("qTscr"); r_kT = R("kTscr"); r_vtok = R("vtokscr"); r_qs = R("qsscr")
            wsrc = w_a_in.rearrange("(c p) n -> p c n", p=128)
            blk = 0
            tcnt = 0
            pj_pending = [None]
            for g in range(3):
                for j in range(3):
                    wb = blk % 2; blk += 1
                    col0 = g * 1536 + j * 512
                    for hf in range(2):
                        S.dma("sp", wa_st[hf][:], wsrc[:, :, col0 + hf * 256:col0 + (hf + 1) * 256], writes=[r_wa_st[hf]])
                        S.op("pool", lambda e, wb=wb, hf=hf: e.tensor_copy(out=wa_bf[wb][:, :, hf * 256:(hf + 1) * 256],
                                                                          in_=wa_st[hf][:]), [r_wa_st[hf]], [r_wa_bf[wb]])
                    for ti, (s, w) in enumerate(tiles):
                        gi = grp_of_tile(s)
                        pb = tcnt % 2; tcnt += 1
                        is_s = s >= T
                        if is_s and 'nosample' in dbg:
                            continue
                        if j != 2 and 'noqk' in dbg:
                            continue

                        def pj(e, s=s, w=w, wb=wb, pb=pb):
                            ins = None
                            for c in range(DC):
                                ins = e.matmul(PS[pb][0:w, :], lhsT=hT[:, c, s:s + w], rhs=wa_bf[wb][:, c, :],
                                               start=(c == 0), stop=(c == DC - 1))
                            return ins
                        S.op("pe", pj, [r_hT[gi], r_wa_bf[wb]], [PSR[pb]])
                        if pj_pending[0] is not None:
                            pj_pending[0]()
                            pj_pending[0] = None
                        lo = max(s, T - WIN[g])
                        if j == 2 and 'nov' in dbg:
                            continue
                        if j == 2:
                            S.op("act", lambda e, pb=pb, w=w: e.copy(out=xn[pb][0:w, :], in_=PS[pb][0:w, :]), [PSR[pb]], [r_xn[pb]])
                            S.op("dve", lambda e, pb=pb, w=w: e.tensor_copy(out=xb[pb][0:w, :], in_=xn[pb][0:w, :]), [r_xn[pb]], [r_xb[pb]])
                            if 'novtok' not in dbg:
                                S.dma("sp", vtok_scr[g, s:s + w, :], xb[pb][0:w, :], reads=[r_xb[pb]], writes=[r_vtok])
                            if is_s:
                                S.dma("sp", aws[g][:, 1, :], xn[pb][0:w, :], reads=[r_xn[pb]])
                                S.dma("sp", kvs_scr[g, 1], xn[pb][0:w, :], reads=[r_xn[pb]], writes=[r_qs])
                            elif lo < s + w and 'noawp' not in dbg:
                                S.dma("sp", awp[g][lo - (T - WIN[g]):s + w - (T - WIN[g]), 1, :], xn[pb][lo - s:w, :], reads=[r_xn[pb]])
                            continue
                        S.op("act", lambda e, pb=pb, w=w: e.copy(out=xn[pb][0:w, :], in_=PS[pb][0:w, :]), [PSR[pb]], [r_xn[pb]])
                        S.op("act", lambda e, pb=pb, w=w: e.activation(out=sqv[0:w, :], in_=PS[pb][0:w, :], func=AF.Square),
                             [PSR[pb]], [r_sqv])
                        S.op("dve", lambda e, w=w: e.tensor_reduce(out=ss[0:w, :], in_=sqv[0:w, :].rearrange("p (h e) -> p h e", h=8),
                                                                   axis=AX.X, op=ALU.add), [r_sqv], [r_ss])
                        S.op("act", lambda e, w=w: e.activation(out=ss[0:w, :], in_=ss[0:w, :], func=AF.Ln, bias=epsc[0:w, :],
                                                                scale=1.0 / 64), [r_ss, r_eps], [r_ss])
                        S.op("act", lambda e, w=w: e.activation(out=ss[0:w, :], in_=ss[0:w, :], func=AF.Exp, scale=-0.5),
                             [r_ss], [r_ss])
                        x3 = xn[pb][0:w, :].rearrange("p (h e) -> p h e", h=8)
                        S.op("dve", lambda e, pb=pb, w=w, x3=x3: e.tensor_tensor(
                            out=x3, in0=x3,
                            in1=ss[0:w, :].unsqueeze(2).to_broadcast([w, 8, 64]), op=ALU.mult), [r_xn[pb], r_ss], [r_xn[pb]])
                        S.op("dve", lambda e, w=w, x3=x3, j=j: e.tensor_tensor(
                            out=x3, in0=x3, in1=grep_[0:w, j:j + 1, :].to_broadcast([w, 8, 64]), op=ALU.mult),
                            [r_xn[pb], r_grep], [r_xn[pb]])
                        cb = cosT[0:w, ti:ti + 1, :].to_broadcast([w, 8, 8])
                        sb_ = sinT[0:w, ti:ti + 1, :].to_broadcast([w, 8, 8])
                        rt = rtmp[0:w, :, :].rearrange("p a (h i) -> p a h i", h=8)
                        S.op("dve", lambda e, x3=x3, rt=rt, cb=cb: e.tensor_tensor(out=rt[:, 0], in0=x3[:, :, 0:8], in1=cb, op=ALU.mult),
                             [r_xn[pb], r_cs], [r_rtmp])
                        S.op("dve", lambda e, x3=x3, rt=rt, sb_=sb_: e.tensor_tensor(out=rt[:, 1], in0=x3[:, :, 8:16], in1=sb_, op=ALU.mult),
                             [r_xn[pb], r_cs], [r_rtmp])
                        S.op("dve", lambda e, x3=x3, rt=rt, cb=cb: e.tensor_tensor(out=rt[:, 2], in0=x3[:, :, 8:16], in1=cb, op=ALU.mult),
                             [r_xn[pb], r_cs], [r_rtmp])
                        S.op("dve", lambda e, x3=x3, rt=rt, sb_=sb_: e.tensor_tensor(out=rt[:, 3], in0=x3[:, :, 0:8], in1=sb_, op=ALU.mult),
                             [r_xn[pb], r_cs], [r_rtmp])
                        S.op("dve", lambda e, x3=x3, rt=rt: e.tensor_tensor(out=x3[:, :, 0:8], in0=rt[:, 0], in1=rt[:, 1], op=ALU.subtract),
                             [r_rtmp], [r_xn[pb]])
                        S.op("dve", lambda e, x3=x3, rt=rt: e.tensor_tensor(out=x3[:, :, 8:16], in0=rt[:, 2], in1=rt[:, 3], op=ALU.add),
                             [r_rtmp], [r_xn[pb]])
                        if j == 1:
                            if is_s:
                                S.dma("sp", aws[g][:, 0, :], xn[pb][0:w, :], reads=[r_xn[pb]])
                            elif lo < s + w:
                                S.dma("sp", awp[g][lo - (T - WIN[g]):s + w - (T - WIN[g]), 0, :], xn[pb][lo - s:w, :], reads=[r_xn[pb]])
                        if is_s:
                            dsts = qs_scr[g] if j == 0 else kvs_scr[g, 0]
                            S.dma("sp", dsts, xn[pb][0:w, :], reads=[r_xn[pb]], writes=[r_qs])
                            continue
                        if 'notr' in dbg:
                            continue
                        def a_tail(pb=pb, w=w, s=s, g=g, j=j):
                            S.op("act", lambda e, pb=pb, w=w: e.copy(out=xb[pb][0:w, :], in_=xn[pb][0:w, :]), [r_xn[pb]], [r_xb[pb]])
                            pst = PS[2 + pb][:, 0:256].bitcast(BF16)

                            def trq(e, pb=pb, w=w, pst=pst):
                                ins = None
                                for hc in range(4):
                                    ins = e.transpose(pst[:, hc * 128:hc * 128 + w], xb[pb][0:w, hc * 128:(hc + 1) * 128], identb[0:w, 0:w])
                                return ins
                            S.op("pe", trq, [r_xb[pb], r_identb], [PSR[2 + pb]])
                            S.op("dve", lambda e, pb=pb, w=w, pst=pst: e.tensor_copy(
                                out=xT[pb][:, :, 0:w], in_=pst.rearrange("p (a t) -> p a t", a=4)[:, :, 0:w]), [PSR[2 + pb]], [r_xT[pb]])
                            dst = (qT_scr if j == 0 else kT_scr)[g]
                            S.dma("sp", dst[:, :, s:s + w].rearrange("a p t -> p a t"), xT[pb][:, :, 0:w], reads=[r_xT[pb]],
                                  writes=[r_qT if j == 0 else r_kT])
                        pj_pending[0] = a_tail
            if pj_pending[0] is not None:
                pj_pending[0]()
                pj_pending[0] = None
            S.barrier()
            if dbg_stop <= 1:
                return
            cv = Carver()
            oT = cv.take([128, 4, TT], BF16); r_oT = R("a_oT")
            OD = cv.take([128, 2, T]); r_OD = R("OD")
            qTs = cv.take([128, T], BF16); kTs = cv.take([128, T], BF16); r_qk = R("a_qk")
            vres = cv.take([128, T // 128, 128], BF16); r_vres = R("vres")
            pf = [cv.take([128, 256]) for _ in range(2)]; r_pf = [R("pf0"), R("pf1")]
            pbf = [cv.take([128, 256], BF16) for _ in range(2)]; r_pbf = [R("pbf0"), R("pbf1")]
            mask2 = cv.take([128, 256]); r_mask2 = R("mask2")
            S.dma("sp", mask2[:], mask2_d, writes=[r_mask2])
            it = 0
            a_pending = [None]
            for hc in range(4):
                S.op("dve", lambda e: e.memset(OD[:], 0.0), [], [r_OD])
                for g in range(3):
                    d = DIL[g]
                    n = T // d
                    nb = n // 128
                    S.dma("sp", qTs[:], qT_scr[g, hc, :, 0:T], reads=[r_qT], writes=[r_qk])
                    S.dma("sp", kTs[:], kT_scr[g, hc, :, 0:T], reads=[r_kT], writes=[r_qk])
                    for r_ in range(d):
                        S.dma("sp", vres[:, r_ * nb:(r_ + 1) * nb, :],
                              vtok_scr[g, r_:T:d, hc * 128:(hc + 1) * 128].rearrange("(kb i) f -> i kb f", i=128),
                              reads=[r_vtok], writes=[r_vres])
                    for hh in range(2):
                        bp = 64 * hh
                        for r_ in range(d):
                            for qb in range(nb):
                                pb = it % 2; it += 1
                                c0 = r_ + d * 128 * qb
                                qcols = qTs[bp:bp + 64, c0:c0 + 127 * d + 1:d]
                                has_prev = qb > 0

                                def smm(e, pb=pb, bp=bp, c0=c0, d=d, qcols=qcols, has_prev=has_prev):
                                    ins = None
                                    if has_prev:
                                        p0 = c0 - 128 * d
                                        ins = e.matmul(PS[pb][:, 0:128], lhsT=kTs[bp:bp + 64, p0:p0 + 127 * d + 1:d], rhs=qcols,
                                                       start=True, stop=True)
                                    ins = e.matmul(PS[pb][:, 128:256], lhsT=kTs[bp:bp + 64, c0:c0 + 127 * d + 1:d], rhs=qcols,
                                                   start=True, stop=True)
                                    return ins
                                S.op("pe", smm, [r_qk], [PSR[pb]])
                                if a_pending[0] is not None:
                                    a_pending[0]()
                                lo_c = 0 if has_prev else 128
                                S.op("act", lambda e, pb=pb, lo_c=lo_c: e.activation(out=pf[pb][:, lo_c:256], in_=PS[pb][:, lo_c:256],
                                                                                     func=AF.Exp, scale=0.125), [PSR[pb]], [r_pf[pb]])
                                S.op("dve", lambda e, pb=pb, lo_c=lo_c: e.tensor_tensor(out=pbf[pb][:, lo_c:256], in0=pf[pb][:, lo_c:256],
                                                                                        in1=mask2[:, lo_c:256], op=ALU.mult),
                                     [r_pf[pb], r_mask2], [r_pbf[pb]])

                                def finish(pb=pb, r_=r_, qb=qb, nb=nb, has_prev=has_prev, bp=bp, c0=c0, d=d):
                                    def pvm(e):
                                        ins = None
                                        halves = ([0] if has_prev else []) + [1]
                                        for k_, hv in enumerate(halves):
                                            kb = qb - 1 + hv
                                            e.matmul(PS[2 + pb][:, 0:128], lhsT=vres[:, r_ * nb + kb, :], rhs=pbf[pb][:, hv * 128:(hv + 1) * 128],
                                                     start=(k_ == 0), stop=(k_ == len(halves) - 1))
                                        for k_, hv in enumerate(halves):
                                            ins = e.matmul(PS[2 + pb][:, 128:256], lhsT=ones_bf[:], rhs=pbf[pb][:, hv * 128:(hv + 1) * 128],
                                                           start=(k_ == 0), stop=(k_ == len(halves) - 1))
                                        return ins
                                    S.op("pe", pvm, [r_vres, r_pbf[pb], r_ones], [PSR[2 + pb]])
                                    odv = OD[bp:bp + 64, :, c0:c0 + 127 * d + 1:d]
                                    S.op("dve", lambda e: e.tensor_tensor(
                                        out=odv, in0=PS[2 + pb][bp:bp + 64, 0:256].rearrange("p (a t) -> p a t", a=2), in1=odv, op=ALU.add),
                                        [PSR[2 + pb], r_OD], [r_OD])
                                    a_pending[0] = None
                                a_pending[0] = finish
                    if a_pending[0] is not None:
                        a_pending[0]()
                S.op("dve", lambda e: e.reciprocal(out=OD[:, 1, :], in_=OD[:, 1, :]), [r_OD], [r_OD])
                S.op("dve", lambda e, hc=hc: e.tensor_tensor(out=oT[:, hc, 0:T], in0=OD[:, 0, :], in1=OD[:, 1, :], op=ALU.mult),
                     [r_OD], [r_oT])
            S.barrier()
            if dbg_stop <= 2:
                return
            cvs = Carver()
            _keep = cvs.take([128, 4, TT], BF16)
            Kt = cvs.take([128, 132, 64]); r_Kt = R("Kt")
            Vt = cvs.take([128, 132, 64]); r_Vt = R("Vt")
            qv = cvs.take([128, 3, 4, 64]); r_qv = R("qv")
            knew = cvs.take([128, 3, 2, 4, 64]); r_knew = R("knew")
            sc = cvs.take([128, 132]); r_sc = R("sc")
            den = cvs.take([128, 4, 4]); r_den = R("den")
            oacc = cvs.take([128, 4, 4, 64]); r_oacc = R("oacc")
            osum = cvs.take([128, 4, 64]); r_osum = R("osum")
            otok = cvs.take([64, 512]); r_otok = R("otok")
            otb = cvs.take([64, 512], BF16); r_otb = R("otb")
            for g in range(3):
                for b in range(NB):
                    S.dma("sp", qv[b * 8:(b + 1) * 8, g], qs_scr[g][b * 4:(b + 1) * 4, :].rearrange("i (h e) -> h i e", h=8),
                          reads=[r_qs], writes=[r_qv])
                    for kv in range(2):
                        S.dma("sp", knew[b * 8:(b + 1) * 8, g, kv],
                              kvs_scr[g, kv][b * 4:(b + 1) * 4, :].rearrange("i (h e) -> h i e", h=8), reads=[r_qs], writes=[r_knew])
            caches = (ca1, ca2, ca3)
            for i in range(4):
                for g in range(3):
                    d = DIL[g]
                    if g == 0:
                        nsl = 129
                        if i == 0:
                            for b in range(NB):
                                S.dma("sp", Kt[b * 8:(b + 1) * 8, 0:128, :], ca1[b, :, 0].rearrange("r h e -> h r e"), writes=[r_Kt])
                                S.dma("sp", Vt[b * 8:(b + 1) * 8, 0:128, :], ca1[b, :, 1].rearrange("r h e -> h r e"), writes=[r_Vt])
                            S.op("dve", lambda e: e.tensor_copy(out=Kt[:, 128:132, :], in_=knew[:, 0, 0]), [r_knew], [r_Kt])
                            S.op("dve", lambda e: e.tensor_copy(out=Vt[:, 128:132, :], in_=knew[:, 0, 1]), [r_knew], [r_Vt])
                        else:
                            for b in range(NB):
                                S.dma("sp", Kt[b * 8:(b + 1) * 8, 0:128, :], ca1[b, :, 0].rearrange("r h e -> h r e"), writes=[r_Kt])
                                S.dma("sp", Vt[b * 8:(b + 1) * 8, 0:128, :], ca1[b, :, 1].rearrange("r h e -> h r e"), writes=[r_Vt])
                            S.op("dve", lambda e: e.tensor_copy(out=Kt[:, 128:132, :], in_=knew[:, 0, 0]), [r_knew], [r_Kt])
                            S.op("dve", lambda e: e.tensor_copy(out=Vt[:, 128:132, :], in_=knew[:, 0, 1]), [r_knew], [r_Vt])
                        k0 = i
                    else:
                        nsl = 129
                        cch = caches[g]
                        for b in range(NB):
                            S.dma("sp", Kt[b * 8:(b + 1) * 8, 0:128, :], cch[b, i::d, 0].rearrange("r h e -> h r e"), writes=[r_Kt])
                            S.dma("sp", Vt[b * 8:(b + 1) * 8, 0:128, :], cch[b, i::d, 1].rearrange("r h e -> h r e"), writes=[r_Vt])
                        S.op("dve", lambda e, g=g, i=i: e.tensor_copy(out=Kt[:, 128, :], in_=knew[:, g, 0, i]), [r_knew], [r_Kt])
                        S.op("dve", lambda e, g=g, i=i: e.tensor_copy(out=Vt[:, 128, :], in_=knew[:, g, 1, i]), [r_knew], [r_Vt])
                        k0 = 0
                    Kw = Kt[:, k0:k0 + nsl, :]
                    Vw = Vt[:, k0:k0 + nsl, :]
                    S.op("dve", lambda e, Kw=Kw, g=g, i=i, nsl=nsl: e.tensor_tensor(
                        out=Kw, in0=Kw, in1=qv[:, g, i:i + 1, :].to_broadcast([128, nsl, 64]), op=ALU.mult), [r_Kt, r_qv], [r_Kt])
                    S.op("dve", lambda e, Kw=Kw, nsl=nsl: e.tensor_reduce(out=sc[:, 0:nsl], in_=Kw, axis=AX.X, op=ALU.add), [r_Kt], [r_sc])
                    S.op("act", lambda e, nsl=nsl: e.activation(out=sc[:, 0:nsl], in_=sc[:, 0:nsl], func=AF.Exp, scale=0.125), [r_sc], [r_sc])
                    S.op("dve", lambda e, nsl=nsl, g=g, i=i: e.tensor_reduce(out=den[:, i, g:g + 1], in_=sc[:, 0:nsl].unsqueeze(1), axis=AX.X, op=ALU.add),
                         [r_sc], [r_den])
                    S.op("dve", lambda e, Vw=Vw, nsl=nsl: e.tensor_tensor(
                        out=Vw, in0=Vw, in1=sc[:, 0:nsl].unsqueeze(2).to_broadcast([128, nsl, 64]), op=ALU.mult), [r_Vt, r_sc], [r_Vt])
                    S.op("dve", lambda e, Vw=Vw, g=g, i=i: e.tensor_reduce(out=oacc[:, i, g, :], in_=Vw.rearrange("p s e -> p e s"),
                                                                           axis=AX.X, op=ALU.add), [r_Vt], [r_oacc])
            S.op("dve", lambda e: e.tensor_reduce(out=den[:, :, 3], in_=den[:, :, 0:3], axis=AX.X, op=ALU.add), [r_den], [r_den])
            S.op("dve", lambda e: e.reciprocal(out=den[:, :, 3:4], in_=den[:, :, 3:4]), [r_den], [r_den])
            S.op("dve", lambda e: e.tensor_reduce(out=osum[:], in_=oacc[:].rearrange("p i g e -> p i e g"), axis=AX.X, op=ALU.add),
                 [r_oacc], [r_osum])
            S.op("dve", lambda e: e.tensor_tensor(out=osum[:], in0=osum[:], in1=den[:, :, 3:4].to_broadcast([128, 4, 64]), op=ALU.mult),
                 [r_osum, r_den], [r_osum])
            r_os = R("os_scr")
            S.dma("sp", os_scr.rearrange("b h i e -> (b h) i e"), osum[:], reads=[r_osum], writes=[r_os])
            for b in range(NB):
                S.dma("sp", otok[b * 4:(b + 1) * 4, :].rearrange("i (h e) -> i h e", h=8), os_scr[b].rearrange("h i e -> i h e"),
                      reads=[r_os], writes=[r_otok])
            S.op("act", lambda e: e.copy(out=otb[:], in_=otok[:]), [r_otok], [r_otb])
            pst = PS[4][:, 0:256].bitcast(BF16)

            def tro(e, pst=pst):
                ins = None
                for hc in range(4):
                    ins = e.transpose(pst[:, hc * 64:(hc + 1) * 64], otb[:, hc * 128:(hc + 1) * 128], identb[0:64, 0:64])
                return ins
            S.op("pe", tro, [r_otb, r_identb], [PSR[4]])
            S.op("dve", lambda e, pst=pst: e.tensor_copy(out=oT[:, :, T:TT], in_=pst[:, 0:256].rearrange("p (a t) -> p a t", a=4)),
                 [PSR[4]], [r_oT])
            if dbg_stop <= 3:
                return
            S.barrier()
            cvs = Carver()
            _keep = cvs.take([128, 4, TT], BF16)
            waout_b = cvs.take([128, 4, D], BF16); r_wb2 = R("waout_b")
            wo_st2 = cvs.take([128, 4, D // 2]); r_wo2 = R("wo_st2")
            xu = cvs.take([128, DC, 512]); r_xu = R("a_xu")
            for hf in range(2):
                S.dma("sp", wo_st2[:], w_a_out.rearrange("(a p) n -> p a n", p=128)[:, :, hf * 512:(hf + 1) * 512], writes=[r_wo2])
                S.op("pool", lambda e, hf=hf: e.tensor_copy(out=waout_b[:, :, hf * 512:(hf + 1) * 512], in_=wo_st2[:]),
                     [r_wo2], [r_wb2])
            for gi, (s, w) in enumerate(groups):
                S.dma("sp", xu[:, :, 0:w], xs[:, :, s:s + w].rearrange("c p t -> p c t"), reads=[r_xs[gi]], writes=[r_xu])
                for dc in range(DC):
                    pb = dc % 2

                    def ymm(e, dc=dc, pb=pb, s=s, w=w):
                        ins = None
                        for hc in range(4):
                            ins = e.matmul(PS[pb][:, 0:w], lhsT=waout_b[:, hc, dc * 128:(dc + 1) * 128], rhs=oT[:, hc, s:s + w],
                                           start=(hc == 0), stop=(hc == 3))
                        return ins
                    S.op("pe", ymm, [r_wb2, r_oT], [PSR[pb]])
                    S.op("dve", lambda e, dc=dc, pb=pb, w=w: e.tensor_tensor(out=xu[:, dc, 0:w], in0=PS[pb][:, 0:w], in1=xu[:, dc, 0:w],
                                                                              op=ALU.add), [PSR[pb], r_xu], [r_xu])
                S.dma("sp", xs[:, :, s:s + w].rearrange("c p t -> p c t"), xu[:, :, 0:w], reads=[r_xu], writes=[r_xs[gi]])
            S.barrier()

        def mixer_d(l):
            norm_pass(l * 3 + 1)
            S.barrier()
            NG = len(groups)
            cv = Carver()
            wst_ = [cv.take([128, DC, 256]) for _ in range(2)]; r_wst_ = [R("d_wst0"), R("d_wst1")]
            wbf_ = [cv.take([128, DC, 256], BF16) for _ in range(2)]; r_wbf_ = [R("d_wbf0"), R("d_wbf1")]
            cw = cv.take([128, 24, 4]); cb = cv.take([128, 24]); r_cw = R("cw")
            carry = cv.take([128, 24, 3]); r_carry = R("carry")
            cbT = cv.take([128, 24, NB * 3]); r_cbT = R("cbT")
            cst = cv.take([NB * 3, 3072]); r_cst = R("cst")
            rawS = cv.take([128, 24, NSAMP]); r_rawS = R("rawS")
            buf = [cv.take([128, 3 + 512]) for _ in range(2)]; r_buf = [R("buf0"), R("buf1")]
            bufs_ = cv.take([128, NB, 7]); r_bufs = R("bufs")
            acc = [cv.take([128, 512]) for _ in range(2)]; r_acc = [R("acc0"), R("acc1")]
            xbT = [cv.take([128, 512], BF16) for _ in range(2)]; r_xbT = [R("xbT0"), R("xbT1")]
            xtk = [cv.take([128, 4, 128], BF16) for _ in range(2)]; r_xtk = [R("xtk0"), R("xtk1")]
            zt = [cv.take([128, 512], BF16) for _ in range(2)]; r_zt = [R("zt0"), R("zt1")]
            dtb = cv.take([128, 32]); alg = cv.take([128, 32]); r_dtc = R("dtc")
            dtt = [cv.take([128, 32]) for _ in range(2)]; dta = [cv.take([128, 32]) for _ in range(2)]; r_dtt = [R("dtt0"), R("dtt1")]
            trs = cv.take([64, 3072]); r_trs = R("trs")
            r_xbcT = R("xbcT_scr"); r_xtokd = R("xtokd_scr"); r_zs = R("zs_scr"); r_dts = R("dt_scr")
            for k_ in range(4):
                S.dma("sp", cw[:, :, k_], d_conv_w[k_].rearrange("(c p) -> p c", p=128), writes=[r_cw], allow_slow_non_contiguous=True)
            S.dma("sp", cb[:], d_conv_b.rearrange("(c p) -> p c", p=128), writes=[r_cw], allow_slow_non_contiguous=True)
            S.dma("sp", dtb[:], d_dt_bias.partition_broadcast(128), writes=[r_dtc])
            S.dma("sp", alg[:], d_a_log.partition_broadcast(128), writes=[r_dtc])
            S.op("act", lambda e: e.activation(out=alg[:], in_=alg[:], func=AF.Exp), [r_dtc], [r_dtc])
            S.op("dve", lambda e: e.tensor_scalar(out=alg[:], in0=alg[:], scalar1=-1.0, scalar2=None, op0=ALU.mult), [r_dtc], [r_dtc])
            S.op("dve", lambda e: e.memset(carry[:], 0.0), [], [r_carry])
            S.dma("sp", cst[:], sd_conv.rearrange("b w f -> (b w) f"), writes=[r_cst])
            for q4 in range(6):
                def trc(e, q4=q4):
                    ins = None
                    for k_ in range(4):
                        fc = q4 * 4 + k_
                        ins = e.transpose(PS[6][:, k_ * 48:(k_ + 1) * 48], cst[:, fc * 128:(fc + 1) * 128], ident[0:48, 0:48])
                    return ins
                S.op("pe", trc, [r_cst, r_ident], [PSR[6]])
                S.op("act", lambda e, q4=q4: e.copy(out=cbT[:, q4 * 4:(q4 + 1) * 4, :],
                                                    in_=PS[6][:, 0:192].rearrange("p (a t) -> p a t", a=4)), [PSR[6]], [r_cbT])
            wsrc = w_d_in.rearrange("(c p) n -> p c n", p=128)
            wk = 0
            it = 0
            for f2 in range(12):
                wb = wk % 2; wk += 1
                c0 = 2048 + f2 * 256
                S.dma("sp", wst_[wb][:], wsrc[:, :, c0:c0 + 256], writes=[r_wst_[wb]])
                S.op("pool", lambda e, wb=wb: e.tensor_copy(out=wbf_[wb][:], in_=wst_[wb][:]), [r_wst_[wb]], [r_wbf_[wb]])
                for k2 in range(2):
                    fc = f2 * 2 + k2
                    for gi, (s, w) in enumerate(groups):
                        pb = it % 2; it += 1
                        is_s = s >= T

                        def pj(e, wb=wb, k2=k2, s=s, w=w, pb=pb):
                            ins = None
                            for c in range(DC):
                                ins = e.matmul(PS[pb][:, 0:w], lhsT=wbf_[wb][:, c, k2 * 128:(k2 + 1) * 128], rhs=hT[:, c, s:s + w],
                                               start=(c == 0), stop=(c == DC - 1))
                            return ins
                        S.op("pe", pj, [r_wbf_[wb], r_hT[gi]], [PSR[pb]])
                        if not is_s:
                            bv = buf[pb]
                            S.op("dve", lambda e, bv=bv, fc=fc: e.tensor_copy(out=bv[:, 0:3], in_=carry[:, fc, :]), [r_carry], [r_buf[pb]])
                            S.op("act", lambda e, bv=bv, pb=pb, w=w: e.copy(out=bv[:, 3:3 + w], in_=PS[pb][:, 0:w]), [PSR[pb]], [r_buf[pb]])
                            S.op("dve", lambda e, bv=bv, fc=fc, w=w: e.tensor_copy(out=carry[:, fc, :], in_=bv[:, w:w + 3]),
                                 [r_buf[pb]], [r_carry])
                            src = [bv[:, k_:k_ + w] for k_ in range(4)]
                            av = acc[pb][:, 0:w]
                        else:
                            S.op("dve", lambda e, fc=fc: e.tensor_copy(out=bufs_[:, :, 0:3],
                                                                       in_=cbT[:, fc, :].rearrange("p (b w) -> p b w", w=3)),
                                 [r_cbT], [r_bufs])
                            S.op("act", lambda e, pb=pb: e.copy(out=bufs_[:, :, 3:7], in_=PS[pb][:, 0:NSAMP].rearrange("p (b i) -> p b i", i=4)),
                                 [PSR[pb]], [r_bufs])
                            S.op("dve", lambda e, fc=fc: e.tensor_copy(out=rawS[:, fc, :].rearrange("p (b i) -> p b i", i=4),
                                                                       in_=bufs_[:, :, 3:7]), [r_bufs], [r_rawS])
                            src = [bufs_[:, :, k_:k_ + 4] for k_ in range(4)]
                            av = acc[pb][:, 0:NSAMP].rearrange("p (b i) -> p b i", i=4)
                        rb = r_bufs if is_s else r_buf[pb]
                        S.op("dve", lambda e, av=av, src=src, fc=fc: e.tensor_scalar(out=av, in0=src[0], scalar1=cw[:, fc, 0:1],
                                                                                    scalar2=cb[:, fc:fc + 1], op0=ALU.mult, op1=ALU.add),
                             [rb, r_cw], [r_acc[pb]])
                        for k_ in range(1, 4):
                            S.op("dve", lambda e, av=av, src=src, fc=fc, k_=k_: e.scalar_tensor_tensor(
                                out=av, in0=src[k_], scalar=cw[:, fc, k_:k_ + 1], in1=av, op0=ALU.mult, op1=ALU.add),
                                [rb, r_cw, r_acc[pb]], [r_acc[pb]])
                        S.op("act", lambda e, pb=pb, w=w: e.activation(out=xbT[pb][:, 0:w], in_=acc[pb][:, 0:w], func=AF.Silu),
                             [r_acc[pb]], [r_xbT[pb]])
                        if fc >= 16:
                            S.dma("sp", xbcT_scr[fc - 16, :, s:s + w], xbT[pb][:, 0:w], reads=[r_xbT[pb]], writes=[r_xbcT])
                        if fc < 20:
                            nt_ = (w + 127) // 128
                            tw = min(w, 128)
                            pst = PS[2 + pb][:, 0:256].bitcast(BF16)

                            def trx(e, pb=pb, nt_=nt_, tw=tw, pst=pst):
                                ins = None
                                for k_ in range(nt_):
                                    ins = e.transpose(pst[0:tw, k_ * 128:(k_ + 1) * 128], xbT[pb][:, k_ * 128:k_ * 128 + tw], identb[:])
                                return ins
                            S.op("pe", trx, [r_xbT[pb], r_identb], [PSR[2 + pb]])
                            S.op("dve", lambda e, pb=pb, nt_=nt_, tw=tw, pst=pst: e.tensor_copy(
                                out=xtk[pb][0:tw, 0:nt_, :], in_=pst[0:tw, 0:nt_ * 128].rearrange("p (a f) -> p a f", a=nt_)),
                                [PSR[2 + pb]], [r_xtk[pb]])
                            S.dma("sp", xtokd_scr[s:s + w, fc * 128:(fc + 1) * 128].rearrange("(a p) f -> p a f", p=tw),
                                  xtk[pb][0:tw, 0:nt_, :], reads=[r_xtk[pb]], writes=[r_xtokd])
            for q4 in range(6):
                def trp(e, q4=q4):
                    ins = None
                    for k_ in range(4):
                        fc = q4 * 4 + k_
                        ins = e.transpose(PS[6][0:3, k_ * 128:(k_ + 1) * 128], carry[:, fc, :], ident[:])
                    return ins
                S.op("pe", trp, [r_carry, r_ident], [PSR[6]])
                S.op("act", lambda e, q4=q4: e.copy(out=trs[0:3, q4 * 512:(q4 + 1) * 512], in_=PS[6][0:3, :]), [PSR[6]], [r_trs])
            S.dma("sp", dcp, trs[0:3, :], reads=[r_trs])
            for q4 in range(6):
                def trs_(e, q4=q4):
                    ins = None
                    for k_ in range(4):
                        fc = q4 * 4 + k_
                        ins = e.transpose(PS[7][0:NSAMP, k_ * 128:(k_ + 1) * 128], rawS[:, fc, :], ident[:])
                    return ins
                S.op("pe", trs_, [r_rawS, r_ident], [PSR[7]])
                S.op("act", lambda e, q4=q4: e.copy(out=trs[:, q4 * 512:(q4 + 1) * 512], in_=PS[7][0:NSAMP, :]), [PSR[7]], [r_trs])
            for b in range(NB):
                S.dma("sp", dcs[b], trs[4 * b + 1:4 * b + 4, :], reads=[r_trs])
            for zb in range(4):
                wb = wk % 2
                for hf in range(2):
                    wb = wk % 2; wk += 1
                    S.dma("sp", wst_[wb][:], wsrc[:, :, zb * 512 + hf * 256:zb * 512 + (hf + 1) * 256], writes=[r_wst_[wb]])
                    S.op("pool", lambda e, wb=wb: e.tensor_copy(out=wbf_[wb][:], in_=wst_[wb][:]), [r_wst_[wb]], [r_wbf_[wb]])
                    for ti, (s, w) in enumerate(tiles):
                        gi = grp_of_tile(s)
                        pb = it % 2; it += 1

                        def pz(e, wb=wb, s=s, w=w, pb=pb):
                            ins = None
                            for c in range(DC):
                                ins = e.matmul(PS[pb][0:w, 0:256], lhsT=hT[:, c, s:s + w], rhs=wbf_[wb][:, c, :],
                                               start=(c == 0), stop=(c == DC - 1))
                            return ins
                        S.op("pe", pz, [r_wbf_[wb], r_hT[gi]], [PSR[pb]])
                        S.op("act", lambda e, pb=pb, w=w: e.activation(out=zt[pb][0:w, 0:256], in_=PS[pb][0:w, 0:256], func=AF.Silu),
                             [PSR[pb]], [r_zt[pb]])
                        S.dma("sp", zs_scr[s:s + w, zb * 512 + hf * 256:zb * 512 + (hf + 1) * 256], zt[pb][0:w, 0:256],
                              reads=[r_zt[pb]], writes=[r_zs])
            wb = wk % 2; wk += 1
            S.dma("sp", wst_[wb][:, :, 0:32], wsrc[:, :, 5120:5152], writes=[r_wst_[wb]])
            S.op("pool", lambda e, wb=wb: e.tensor_copy(out=wbf_[wb][:, :, 0:32], in_=wst_[wb][:, :, 0:32]), [r_wst_[wb]], [r_wbf_[wb]])
            for ti, (s, w) in enumerate(tiles):
                gi = grp_of_tile(s)
                pb = it % 2; it += 1

                def pdt(e, wb=wb, s=s, w=w, pb=pb):
                    ins = None
                    for c in range(DC):
                        ins = e.matmul(PS[pb][0:w, 0:32], lhsT=hT[:, c, s:s + w], rhs=wbf_[wb][:, c, 0:32], start=(c == 0), stop=(c == DC - 1))
                    return ins
                S.op("pe", pdt, [r_wbf_[wb], r_hT[gi]], [PSR[pb]])
                S.op("dve", lambda e, pb=pb, w=w: e.tensor_tensor(out=dtt[pb][0:w, :], in0=PS[pb][0:w, 0:32], in1=dtb[0:w, :], op=ALU.add),
                     [PSR[pb], r_dtc], [r_dtt[pb]])
                S.op("act", lambda e, pb=pb, w=w: e.activation(out=dtt[pb][0:w, :], in_=dtt[pb][0:w, :], func=AF.Exp), [r_dtt[pb]], [r_dtt[pb]])
                S.op("act", lambda e, pb=pb, w=w: e.activation(out=dtt[pb][0:w, :], in_=dtt[pb][0:w, :], func=AF.Ln, bias=1.0, scale=1.0),
                     [r_dtt[pb]], [r_dtt[pb]])
                S.op("dve", lambda e, pb=pb, w=w: e.tensor_tensor(out=dta[pb][0:w, :], in0=dtt[pb][0:w, :], in1=alg[0:w, :], op=ALU.mult),
                     [r_dtt[pb], r_dtc], [r_dtt[pb]])
                S.dma("sp", dt_scr[0, s:s + w, :], dtt[pb][0:w, :], reads=[r_dtt[pb]], writes=[r_dts])
                S.dma("sp", dt_scr[1, s:s + w, :], dta[pb][0:w, :], reads=[r_dtt[pb]], writes=[r_dts])
            S.barrier()
            if dbg_stop <= 1:
                return
            cv = Carver()
            wdo = cv.take([128, 16, D], BF16); r_wdo = R("wdo")
            GW = 256
            gT = cv.take([128, 16, GW], BF16); r_gT = R("gT")
            xu = cv.take([128, DC, GW]); r_xu = R("d_xu")
            xt_ = cv.take([64, 32, 64], BF16); r_xt = R("d_xt")
            Bt = cv.take([64, 4, 128], BF16); r_Bt = R("d_Bt")
            BCT = cv.take([128, 8, 64], BF16); r_BCT = R("d_BCT")
            zt2 = cv.take([64, 2048], BF16); r_zt2 = R("d_zt2")
            yt = cv.take([64, 2048]); r_yt = R("d_yt")
            gb = cv.take([64, 2048], BF16); r_gb = R("d_gb")
            xd = cv.take([64, 32, 64], BF16); r_xd = R("d_xd")
            Rm = cv.take([64, 8, 64]); r_Rm = R("d_R")
            Em = cv.take([64, 8, 64]); r_Em = R("d_E")
            Lw = cv.take([64, 8, 64], BF16); r_Lw = R("d_Lw")
            attm = cv.take([64, 64]); r_attm = R("d_attm")
            tmpy = cv.take([64, 512]); r_tmpy = R("d_tmpy")
            hS = cv.take([64, 32, 128]); r_hS = [R(f"hS{g}") for g in range(4)]
            hb = cv.take([64, 8, 128], BF16); r_hb = R("d_hb")
            hTb = cv.take([128, 32, 64], BF16); r_hTb = [R(f"hTb{g}") for g in range(4)]
            L1 = cv.take([64, 64]); L2 = cv.take([64, 64]); on64 = cv.take([64, 64]); r_Lc = R("d_Lc")
            Dm = cv.take([64, 32, 64], BF16); r_Dm = R("d_Dm")
            dsk = cv.take([64, 32]); r_dsk = R("d_dsk")
            ngr = cv.take([64, 2048], BF16); ngs = cv.take([64, 2048]); r_ngr = R("d_ngr")
            dtc = cv.take([64, 32]); dac = cv.take([64, 32]); r_dtch = R("d_dtch")
            cum = cv.take([64, 32]); ecum = cv.take([64, 32]); wdv = cv.take([64, 32]); decr = cv.take([64, 32]); r_cum = R("d_cum")
            ssq = cv.take([64, 4]); r_ssq = R("d_ssq")
            wdst_ = cv.take([128, 4, 512]); r_wdst_ = R("d_wdst")
            S.dma("sp", L1[:], ssd_l1, writes=[r_Lc])
            S.dma("sp", L2[:], ssd_l2, writes=[r_Lc])
            S.op("dve", lambda e: e.memset(on64[:], 1.0), [], [r_Lc])
            S.dma("sp", dsk[:], d_skip.partition_broadcast(64), writes=[r_dsk])
            S.op("dve", lambda e: e.tensor_tensor(out=Dm[:], in0=ident[0:64, 0:64].unsqueeze(1).to_broadcast([64, 32, 64]),
                                                  in1=dsk[:].unsqueeze(2).to_broadcast([64, 32, 64]), op=ALU.mult),
                 [r_ident, r_dsk], [r_Dm])
            S.dma("sp", ngs[:], d_norm_gain.partition_broadcast(64), writes=[r_ngr])
            S.op("dve", lambda e: e.tensor_copy(out=ngr[:], in_=ngs[:]), [r_ngr], [r_ngr])
            wsrc2 = w_d_out.rearrange("(a p) n -> p a n", p=128)
            for a4 in range(4):
                for hf in range(2):
                    S.dma("sp", wdst_[:], wsrc2[:, a4 * 4:(a4 + 1) * 4, hf * 512:(hf + 1) * 512], writes=[r_wdst_])
                    S.op("pool", lambda e, a4=a4, hf=hf: e.tensor_copy(out=wdo[:, a4 * 4:(a4 + 1) * 4, hf * 512:(hf + 1) * 512], in_=wdst_[:]),
                         [r_wdst_], [r_wdo])
            for g in range(4):
                S.op("dve", lambda e, g=g: e.memset(hS[:, g * 8:(g + 1) * 8, :], 0.0), [], [r_hS[g]])
                S.op("dve", lambda e, g=g: e.memset(hTb[:, g * 8:(g + 1) * 8, :], 0.0), [], [r_hTb[g]])
            dgroups = [(s0, GW, 64, False) for s0 in range(0, T, GW)] + [(T, NSAMP, 4, True)]
            for (s, w, Q, is_s) in dgroups:
                gi = grp_of_tile(s)
                nch = w // Q
                S.dma("sp", xu[:, :, 0:w], xs[:, :, s:s + w].rearrange("c p t -> p c t"), reads=[r_xs[gi]], writes=[r_xu])
                for m in range(nch):
                    t0 = s + m * Q
                    cl = m * Q
                    S.dma("sp", xt_[0:Q].rearrange("p h e -> p (h e)"), xtokd_scr[t0:t0 + Q, 0:2048], reads=[r_xtokd], writes=[r_xt])
                    S.dma("sp", Bt[0:Q].rearrange("p g n -> p (g n)"), xtokd_scr[t0:t0 + Q, 2048:2560], reads=[r_xtokd], writes=[r_Bt])
                    S.dma("sp", BCT[:, :, 0:Q], xbcT_scr[:, :, t0:t0 + Q].rearrange("a p t -> p a t"), reads=[r_xbcT], writes=[r_BCT])
                    S.dma("sp", zt2[0:Q, :], zs_scr[t0:t0 + Q, :], reads=[r_zs], writes=[r_zt2])
                    S.dma("sp", dtc[0:Q, :], dt_scr[0, t0:t0 + Q, :], reads=[r_dts], writes=[r_dtch])
                    S.dma("sp", dac[0:Q, :], dt_scr[1, t0:t0 + Q, :], reads=[r_dts], writes=[r_dtch])
                    if is_s:
                        for g in range(4):
                            S.dma("sp", hS[:, g * 8:(g + 1) * 8, :], sd_ssm[m, g * 8:(g + 1) * 8].rearrange("h p n -> p h n"), writes=[r_hS[g]])
                            S.op("act", lambda e, g=g: e.copy(out=hb[:], in_=hS[:, g * 8:(g + 1) * 8, :]), [r_hS[g]], [r_hb])
                            pst = PS[6][:, 0:256].bitcast(BF16)

                            def trh(e, pst=pst):
                                ins = None
                                for k_ in range(8):
                                    ins = e.transpose(pst[:, k_ * 64:(k_ + 1) * 64], hb[:, k_, :], identb[0:64, 0:64])
                                return ins
                            S.op("pe", trh, [r_hb, r_identb], [PSR[6]])
                            S.op("dve", lambda e, g=g, pst=pst: e.tensor_copy(out=hTb[:, g * 8:(g + 1) * 8, :],
                                                                              in_=pst.rearrange("p (a t) -> p a t", a=8)), [PSR[6]], [r_hTb[g]])
                    S.op("pe", lambda e, Q=Q: e.matmul(PS[0][0:Q, 0:32], lhsT=L2[0:Q, 0:Q], rhs=dac[0:Q, :], start=True, stop=True),
                         [r_Lc, r_dtch], [PSR[0]])
                    S.op("pe", lambda e, Q=Q: e.matmul(PS[0][0:64, 32:64], lhsT=on64[0:Q, 0:64], rhs=dac[0:Q, :], start=True, stop=True),
                         [r_Lc, r_dtch], [PSR[0]])
                    S.op("act", lambda e, Q=Q: e.copy(out=cum[0:Q, :], in_=PS[0][0:Q, 0:32]), [PSR[0]], [r_cum])
                    S.op("act", lambda e, Q=Q: e.activation(out=ecum[0:Q, :], in_=PS[0][0:Q, 0:32], func=AF.Exp), [PSR[0]], [r_cum])
                    S.op("act", lambda e: e.activation(out=decr[:], in_=PS[0][0:64, 32:64], func=AF.Exp), [PSR[0]], [r_cum])
                    S.op("act", lambda e, Q=Q: e.copy(out=wdv[0:Q, :], in_=PS[0][0:Q, 32:64]), [PSR[0]], [r_cum])
                    S.op("dve", lambda e, Q=Q: e.tensor_tensor(out=wdv[0:Q, :], in0=wdv[0:Q, :], in1=cum[0:Q, :], op=ALU.subtract),
                         [r_cum], [r_cum])
                    S.op("act", lambda e, Q=Q: e.activation(out=wdv[0:Q, :], in_=wdv[0:Q, :], func=AF.Exp), [r_cum], [r_cum])
                    S.op("dve", lambda e, Q=Q: e.tensor_tensor(out=wdv[0:Q, :], in0=wdv[0:Q, :], in1=dtc[0:Q, :], op=ALU.mult),
                         [r_cum, r_dtch], [r_cum])
                    S.op("pool", lambda e, Q=Q: e.tensor_tensor(out=xd[0:Q], in0=xt_[0:Q], in1=wdv[0:Q, :].unsqueeze(2).to_broadcast([Q, 32, 64]),
                                                               op=ALU.mult), [r_xt, r_cum], [r_xd])
                    for g in range(4):
                        hs = slice(g * 8, (g + 1) * 8)
                        S.op("pe", lambda e, g=g, Q=Q: e.matmul(PS[1][0:Q, 0:Q], lhsT=BCT[:, g, 0:Q], rhs=BCT[:, 4 + g, 0:Q], start=True, stop=True),
                             [r_BCT], [PSR[1]])
                        S.op("dve", lambda e, Q=Q: e.tensor_tensor(out=attm[0:Q, 0:Q], in0=PS[1][0:Q, 0:Q], in1=L2[0:Q, 0:Q], op=ALU.mult),
                             [PSR[1], r_Lc], [r_attm])
                        S.op("pool", lambda e, hs=hs, Q=Q: e.tensor_tensor(
                            out=Rm[0:Q, :, 0:Q], in0=L2[0:Q, 0:Q].unsqueeze(1).to_broadcast([Q, 8, Q]),
                            in1=dac[0:Q, hs].unsqueeze(2).to_broadcast([Q, 8, Q]), op=ALU.mult), [r_Lc, r_dtch], [r_Rm])

                        def segmm(e, Q=Q):
                            ins = None
                            for k_ in range(8):
                                ins = e.matmul(PS[2][0:Q, k_ * 64:k_ * 64 + Q], lhsT=L1[0:Q, 0:Q], rhs=Rm[0:Q, k_, 0:Q], start=True, stop=True)
                            return ins
                        S.op("pe", segmm, [r_Lc, r_Rm], [PSR[2]])
                        S.op("act", lambda e, Q=Q: e.activation(out=Em[0:Q, :, 0:Q], in_=PS[2][0:Q, :].rearrange("p (h t) -> p h t", h=8)[:, :, 0:Q],
                                                                func=AF.Exp), [PSR[2]], [r_Em])
                        S.op("pool", lambda e, hs=hs, Q=Q: e.tensor_tensor(out=Em[0:Q, :, 0:Q], in0=Em[0:Q, :, 0:Q],
                                                                          in1=dtc[0:Q, hs].unsqueeze(2).to_broadcast([Q, 8, Q]), op=ALU.mult),
                             [r_Em, r_dtch], [r_Em])
                        S.op("dve", lambda e, Q=Q: e.tensor_tensor(out=Em[0:Q, :, 0:Q], in0=Em[0:Q, :, 0:Q],
                                                                   in1=attm[0:Q, 0:Q].unsqueeze(1).to_broadcast([Q, 8, Q]), op=ALU.mult),
                             [r_Em, r_attm], [r_Em])
                        S.op("pool", lambda e, hs=hs, Q=Q: e.tensor_tensor(out=Lw[0:Q, :, 0:Q], in0=Em[0:Q, :, 0:Q], in1=Dm[0:Q, hs, 0:Q], op=ALU.add),
                             [r_Em, r_Dm], [r_Lw])

                        def ymm(e, g=g, Q=Q):
                            ins = None
                            for k_ in range(8):
                                ins = e.matmul(PS[3][0:Q, k_ * 64:(k_ + 1) * 64], lhsT=Lw[0:Q, k_, 0:Q], rhs=xt_[0:Q, g * 8 + k_, :],
                                               start=True, stop=True)
                            return ins
                        S.op("pe", ymm, [r_Lw, r_xt], [PSR[3]])
                        S.op("pe", lambda e, g=g, Q=Q, hs=hs: e.matmul(PS[4][0:Q, :], lhsT=BCT[:, 4 + g, 0:Q],
                                                                       rhs=hTb[:, hs, :].rearrange("p h e -> p (h e)"), start=True, stop=True),
                             [r_BCT, r_hTb[g]], [PSR[4]])
                        S.op("dve", lambda e, Q=Q, hs=hs: e.tensor_tensor(
                            out=tmpy[0:Q, :].rearrange("p (h e) -> p h e", h=8), in0=PS[4][0:Q, :].rearrange("p (h e) -> p h e", h=8),
                            in1=ecum[0:Q, hs].unsqueeze(2).to_broadcast([Q, 8, 64]), op=ALU.mult), [PSR[4], r_cum], [r_tmpy])
                        S.op("dve", lambda e, g=g, Q=Q: e.tensor_tensor(out=yt[0:Q, g * 512:(g + 1) * 512], in0=PS[3][0:Q, :], in1=tmpy[0:Q, :],
                                                                        op=ALU.add), [PSR[3], r_tmpy], [r_yt])
                        def stmm(e, g=g, Q=Q):
                            ins = None
                            for k_ in range(8):
                                ins = e.matmul(PS[5 + k_ // 4][0:64, (k_ % 4) * 128:(k_ % 4 + 1) * 128], lhsT=xd[0:Q, g * 8 + k_, :],
                                               rhs=Bt[0:Q, g, :], start=True, stop=True)
                            return ins
                        S.op("pe", stmm, [r_xd, r_Bt], [PSR[5], PSR[6]])
                        S.op("pool", lambda e, hs=hs: e.tensor_tensor(out=hS[:, hs, :], in0=hS[:, hs, :],
                                                                     in1=decr[:, hs].unsqueeze(2).to_broadcast([64, 8, 128]), op=ALU.mult),
                             [r_hS[g], r_cum], [r_hS[g]])
                        for hh2 in range(2):
                            S.op("dve", lambda e, g=g, hh2=hh2: e.tensor_tensor(
                                out=hS[:, g * 8 + hh2 * 4:g * 8 + hh2 * 4 + 4, :], in0=hS[:, g * 8 + hh2 * 4:g * 8 + hh2 * 4 + 4, :],
                                in1=PS[5 + hh2][0:64, :].rearrange("p (h n) -> p h n", h=4), op=ALU.add), [r_hS[g], PSR[5 + hh2]], [r_hS[g]])
                        if is_s:
                            S.dma("sp", dsss[m, g * 8:(g + 1) * 8].rearrange("h p n -> p h n"), hS[:, hs, :], reads=[r_hS[g]])
                        else:
                            S.op("act", lambda e, hs=hs: e.copy(out=hb[:], in_=hS[:, hs, :]), [r_hS[g]], [r_hb])
                            pst = PS[7][:, 0:256].bitcast(BF16)

                            def trh2(e, pst=pst):
                                ins = None
                                for k_ in range(8):
                                    ins = e.transpose(pst[:, k_ * 64:(k_ + 1) * 64], hb[:, k_, :], identb[0:64, 0:64])
                                return ins
                            S.op("pe", trh2, [r_hb, r_identb], [PSR[7]])
                            S.op("dve", lambda e, hs=hs, pst=pst: e.tensor_copy(out=hTb[:, hs, :], in_=pst.rearrange("p (a t) -> p a t", a=8)),
                                 [PSR[7]], [r_hTb[g]])
                    S.op("pool", lambda e, Q=Q: e.tensor_tensor(out=yt[0:Q, :], in0=yt[0:Q, :], in1=zt2[0:Q, :], op=ALU.mult), [r_yt, r_zt2], [r_yt])
                    S.op("act", lambda e, Q=Q: e.activation(out=ngs[0:Q, :], in_=yt[0:Q, :], func=AF.Square), [r_yt, r_ngr], [r_ngr])
                    S.op("dve", lambda e, Q=Q: e.tensor_reduce(out=ssq[0:Q, :], in_=ngs[0:Q, :].rearrange("p (g f) -> p g f", g=4), axis=AX.X, op=ALU.add),
                         [r_ngr], [r_ssq])
                    S.op("act", lambda e, Q=Q: e.activation(out=ssq[0:Q, :], in_=ssq[0:Q, :], func=AF.Ln, bias=epsc[0:Q, :], scale=1.0 / 512),
                         [r_ssq, r_eps], [r_ssq])
                    S.op("act", lambda e, Q=Q: e.activation(out=ssq[0:Q, :], in_=ssq[0:Q, :], func=AF.Exp, scale=-0.5), [r_ssq], [r_ssq])
                    S.op("dve", lambda e, Q=Q: e.tensor_tensor(out=yt[0:Q, :].rearrange("p (g f) -> p g f", g=4),
                                                               in0=yt[0:Q, :].rearrange("p (g f) -> p g f", g=4),
                                                               in1=ssq[0:Q, :].unsqueeze(2).to_broadcast([Q, 4, 512]), op=ALU.mult),
                         [r_yt, r_ssq], [r_yt])
                    S.op("pool", lambda e, Q=Q: e.tensor_tensor(out=gb[0:Q, :], in0=yt[0:Q, :], in1=ngr[0:Q, :], op=ALU.mult), [r_yt, r_ngr], [r_gb])
                    for q4 in range(4):
                        pb = 1 + (q4 % 2)
                        pst = PS[pb][:, 0:256].bitcast(BF16)

                        def trg(e, q4=q4, Q=Q, pst=pst):
                            ins = None
                            for k_ in range(4):
                                ch = q4 * 4 + k_
                                ins = e.transpose(pst[:, k_ * 64:k_ * 64 + Q], gb[0:Q, ch * 128:(ch + 1) * 128], identb[0:Q, 0:Q])
                            return ins
                        S.op("pe", trg, [r_gb, r_identb], [PSR[pb]])
                        S.op("act", lambda e, q4=q4, Q=Q, cl=cl, pst=pst: e.copy(
                            out=gT[:, q4 * 4:(q4 + 1) * 4, cl:cl + Q], in_=pst[:, 0:256].rearrange("p (a t) -> p a t", a=4)[:, :, 0:Q]),
                            [PSR[pb]], [r_gT])
                for dc in range(DC):
                    pb = 3 + dc % 2

                    def omm(e, dc=dc, pb=pb, w=w):
                        ins = None
                        for ch in range(16):
                            ins = e.matmul(PS[pb][:, 0:w], lhsT=wdo[:, ch, dc * 128:(dc + 1) * 128], rhs=gT[:, ch, 0:w], start=(ch == 0), stop=(ch == 15))
                        return ins
                    S.op("pe", omm, [r_wdo, r_gT], [PSR[pb]])
                    S.op("dve", lambda e, dc=dc, pb=pb, w=w: e.tensor_tensor(out=xu[:, dc, 0:w], in0=PS[pb][:, 0:w], in1=xu[:, dc, 0:w], op=ALU.add),
                         [PSR[pb], r_xu], [r_xu])
                S.dma("sp", xs[:, :, s:s + w].rearrange("c p t -> p c t"), xu[:, :, 0:w], reads=[r_xu], writes=[r_xs[gi]])
                if (not is_s) and s + w == T:
                    for g in range(4):
                        S.dma("sp", dssp[g * 8:(g + 1) * 8].rearrange("h p n -> p h n"), hS[:, g * 8:(g + 1) * 8, :], reads=[r_hS[g]])
            S.barrier()

        def mixer_b(l):
            norm_pass(l * 3 + 1)
            S.barrier()
            NT = len(tiles)
            NBLK = T // 256
            BIGNEG = -1.0e30
            cv = Carver()
            wst_ = [cv.take([128, DC, 256]) for _ in range(2)]; r_wst_ = [R("b_wst0"), R("b_wst1")]
            wbf_ = cv.take([128, DC, 512], BF16); r_wbf_ = R("b_wbf")
            sqv = cv.take([128, 512]); r_sqv = R("b_sqv")
            xn = [cv.take([128, 512]) for _ in range(2)]; r_xn = [R("b_xn0"), R("b_xn1")]
            xb = [cv.take([128, 512], BF16) for _ in range(2)]; r_xb = [R("b_xb0"), R("b_xb1")]
            rtmp = cv.take([128, 4, 64]); r_rtmp = R("b_rtmp")
            xT = [cv.take([128, 4, 128], BF16) for _ in range(2)]; r_xT = [R("b_xT0"), R("b_xT1")]
            ss = cv.take([128, 8]); r_ss = R("b_ss")
            grep_ = cv.take([128, 2, 64]); r_grep = R("b_grep")
            cosT = cv.take([128, NT, 8]); sinT = cv.take([128, NT, 8]); r_cs = R("b_cossin")
            kTres = cv.take([128, 2, T], BF16); r_kTres = R("kTres")
            kmT = cv.take([128, 2, NBLK]); r_kmT = R("kmT")
            kmd = cv.take([128, 4, NBLK]); r_kmd = R("kmd")
            qTf = cv.take([128, 4, 128]); r_qTf = R("qTf")
            gc = cv.take([128, 8, 16]); r_gc = R("gc")
            m8 = cv.take([128, 8, 8]); r_m8 = R("m8")
            biasf = cv.take([128, 8, 16]); r_biasf = R("biasf")
            biasb = cv.take([128, 8, 16], BF16); r_biasb = R("biasb")
            bT = [cv.take([16, 8, 128], BF16) for _ in range(2)]; r_bT = [R("bT0"), R("bT1")]
            indf = cv.take([16, 512]); indb = cv.take([16, 512], BF16); r_ind = R("ind")
            S.dma("sp", grep_[:], b_qk_gain.rearrange("j e -> (j e)").partition_broadcast(128)
                  .rearrange("p (j e) -> p j e", j=2), writes=[r_grep])
            S.dma("sp", cosT[:], rope_cos.rearrange("(n p) i -> p n i", p=128), writes=[r_cs])
            S.dma("sp", sinT[:], rope_sin.rearrange("(n p) i -> p n i", p=128), writes=[r_cs])
            r_oTbs = R("oTb_scr"); r_kaug = R("kaug_scr"); r_qaug = R("qaug_scr"); r_vtokb = R("vtokb_scr"); r_sq = R("b_sscr"); r_kms = R("kmT_scr")
            for c5 in range(T // 512):
                S.dma("sp", indf[:], kind_d[:, c5 * 512:(c5 + 1) * 512], writes=[r_ind])
                S.op("dve", lambda e: e.tensor_copy(out=indb[:], in_=indf[:]), [r_ind], [r_ind])
                for hk in range(4):
                    S.dma("sp", kaug_scr[hk, 64:80, c5 * 512:(c5 + 1) * 512], indb[:], reads=[r_ind], writes=[r_kaug])
            wsrc = w_b_in.rearrange("(c p) n -> p c n", p=128)

            def load_block(col0):
                for hf in range(2):
                    S.dma("sp", wst_[hf][:], wsrc[:, :, col0 + hf * 256:col0 + (hf + 1) * 256], writes=[r_wst_[hf]])
                    S.op("pool", lambda e, hf=hf: e.tensor_copy(out=wbf_[:, :, hf * 256:(hf + 1) * 256], in_=wst_[hf][:]),
                         [r_wst_[hf]], [r_wbf_])

            def normrope(pb, w, ti, nh, jg, c0):
                ncol = nh * 64
                S.op("act", lambda e: e.copy(out=xn[pb][0:w, :], in_=PS[pb][0:w, :]), [PSR[pb]], [r_xn[pb]])
                S.op("act", lambda e: e.activation(out=sqv[0:w, 0:ncol], in_=PS[pb][0:w, c0:c0 + ncol], func=AF.Square),
                     [PSR[pb]], [r_sqv])
                S.op("dve", lambda e: e.tensor_reduce(out=ss[0:w, 0:nh], in_=sqv[0:w, 0:ncol].rearrange("p (h e) -> p h e", h=nh),
                                                      axis=AX.X, op=ALU.add), [r_sqv], [r_ss])
                S.op("act", lambda e: e.activation(out=ss[0:w, 0:nh], in_=ss[0:w, 0:nh], func=AF.Ln, bias=epsc[0:w, :], scale=1.0 / 64),
                     [r_ss, r_eps], [r_ss])
                S.op("act", lambda e: e.activation(out=ss[0:w, 0:nh], in_=ss[0:w, 0:nh], func=AF.Exp, scale=-0.5), [r_ss], [r_ss])
                x3 = xn[pb][0:w, c0:c0 + ncol].rearrange("p (h e) -> p h e", h=nh)
                S.op("dve", lambda e: e.tensor_tensor(out=x3, in0=x3,
                                                      in1=ss[0:w, 0:nh].unsqueeze(2).to_broadcast([w, nh, 64]), op=ALU.mult),
                     [r_xn[pb], r_ss], [r_xn[pb]])
                S.op("dve", lambda e: e.tensor_tensor(out=x3, in0=x3, in1=grep_[0:w, jg:jg + 1, :].to_broadcast([w, nh, 64]), op=ALU.mult),
                     [r_xn[pb], r_grep], [r_xn[pb]])
                cb_ = cosT[0:w, ti:ti + 1, :].to_broadcast([w, nh, 8])
                sb_ = sinT[0:w, ti:ti + 1, :].to_broadcast([w, nh, 8])
                rt = rtmp[0:w, :, 0:nh * 8].rearrange("p a (h i) -> p a h i", h=nh)
                S.op("dve", lambda e: e.tensor_tensor(out=rt[:, 0], in0=x3[:, :, 0:8], in1=cb_, op=ALU.mult), [r_xn[pb], r_cs], [r_rtmp])
                S.op("dve", lambda e: e.tensor_tensor(out=rt[:, 1], in0=x3[:, :, 8:16], in1=sb_, op=ALU.mult), [r_xn[pb], r_cs], [r_rtmp])
                S.op("dve", lambda e: e.tensor_tensor(out=rt[:, 2], in0=x3[:, :, 8:16], in1=cb_, op=ALU.mult), [r_xn[pb], r_cs], [r_rtmp])
                S.op("dve", lambda e: e.tensor_tensor(out=rt[:, 3], in0=x3[:, :, 0:8], in1=sb_, op=ALU.mult), [r_xn[pb], r_cs], [r_rtmp])
                S.op("dve", lambda e: e.tensor_tensor(out=x3[:, :, 0:8], in0=rt[:, 0], in1=rt[:, 1], op=ALU.subtract), [r_rtmp], [r_xn[pb]])
                S.op("dve", lambda e: e.tensor_tensor(out=x3[:, :, 8:16], in0=rt[:, 2], in1=rt[:, 3], op=ALU.add), [r_rtmp], [r_xn[pb]])

            def proj(pb, s, w, gi):
                def pj(e):
                    ins = None
                    for c in range(DC):
                        ins = e.matmul(PS[pb][0:w, :], lhsT=hT[:, c, s:s + w], rhs=wbf_[:, c, :], start=(c == 0), stop=(c == DC - 1))
                    return ins
                S.op("pe", pj, [r_hT[gi], r_wbf_], [PSR[pb]])

            load_block(1024)
            tcnt = 0
            for ti, (s, w) in enumerate(tiles):
                gi = grp_of_tile(s)
                pb = tcnt % 2; tcnt += 1
                is_s = s >= T
                proj(pb, s, w, gi)
                normrope(pb, w, ti, 4, 1, 0)
                if is_s:
                    S.dma("sp", bkvs.rearrange("t k f -> t (k f)"), xn[pb][0:w, :], reads=[r_xn[pb]])
                    S.dma("sp", ksb_scr, xn[pb][0:w, :], reads=[r_xn[pb]], writes=[r_sq])
                    continue
                S.dma("sp", bkvp[s:s + w].rearrange("t k f -> t (k f)"), xn[pb][0:w, :], reads=[r_xn[pb]])
                S.op("dve", lambda e, pb=pb, w=w: e.tensor_copy(out=xb[pb][0:w, :], in_=xn[pb][0:w, :]), [r_xn[pb]], [r_xb[pb]])
                S.dma("sp", vtokb_scr[s:s + w, :], xb[pb][0:w, 256:512], reads=[r_xb[pb]], writes=[r_vtokb])
                pst = PS[2 + pb][:, 0:256].bitcast(BF16)

                def trk(e, pb=pb, w=w, pst=pst):
                    ins = None
                    for a in range(2):
                        ins = e.transpose(pst[:, a * 128:a * 128 + w], xb[pb][0:w, a * 128:(a + 1) * 128], identb[0:w, 0:w])
                    return ins
                S.op("pe", trk, [r_xb[pb], r_identb], [PSR[2 + pb]])
                S.op("dve", lambda e, w=w, s=s, pst=pst: e.tensor_copy(out=kTres[:, :, s:s + w],
                                                                       in_=pst[:, 0:256].rearrange("p (a t) -> p a t", a=2)[:, :, 0:w]),
                     [PSR[2 + pb]], [r_kTres])
                for half in range(2):
                    S.dma("sp", kaug_scr[half:4:2, 0:64, s:s + w].rearrange("a e t -> e a t"), kTres[half * 64:(half + 1) * 64, :, s:s + w],
                          reads=[r_kTres], writes=[r_kaug])
            S.op("dve", lambda e: e.tensor_reduce(out=kmT[:], in_=kTres[:].rearrange("p a (n k) -> p a n k", k=256), axis=AX.X, op=ALU.add),
                 [r_kTres], [r_kmT])
            S.op("dve", lambda e: e.tensor_scalar(out=kmT[:], in0=kmT[:], scalar1=1.0 / 256, scalar2=None, op0=ALU.mult), [r_kmT], [r_kmT])
            S.dma("sp", kmT_scr.rearrange("a p n -> p a n"), kmT[:], reads=[r_kmT], writes=[r_kms])
            for half in range(2):
                S.dma("sp", kmd[half * 64:(half + 1) * 64], kmT_scr.rearrange("a (b e) n -> e (a b) n", b=2), reads=[r_kms], writes=[r_kmd])
            bq_pending = [None]
            for qblk in range(2):
                load_block(qblk * 512)
                for ti, (s, w) in enumerate(tiles):
                    gi = grp_of_tile(s)
                    pb = tcnt % 2; tcnt += 1
                    is_s = s >= T
                    proj(pb, s, w, gi)
                    if bq_pending[0] is not None:
                        bq_pending[0]()
                        bq_pending[0] = None
                    normrope(pb, w, ti, 8, 0, 0)
                    if is_s:
                        S.dma("sp", qsb_scr[:, qblk * 512:(qblk + 1) * 512], xn[pb][0:w, :], reads=[r_xn[pb]], writes=[r_sq])
                        continue
                    def b_tail(pb=pb, w=w, s=s, qblk=qblk):
                        blk = s // 256
                        S.op("act", lambda e, pb=pb, w=w: e.copy(out=xb[pb][0:w, :], in_=xn[pb][0:w, :]), [r_xn[pb]], [r_xb[pb]])
                        pst = PS[2 + pb][:, 0:256].bitcast(BF16)

                        def trq(e, pb=pb, w=w, pst=pst):
                            ins = None
                            for a in range(4):
                                ins = e.transpose(pst[:, a * 128:a * 128 + w], xb[pb][0:w, a * 128:(a + 1) * 128], identb[0:w, 0:w])
                            return ins
                        S.op("pe", trq, [r_xb[pb], r_identb], [PSR[2 + pb]])
                        S.op("dve", lambda e, pb=pb, w=w, pst=pst: e.tensor_copy(out=xT[pb][:, :, 0:w], in_=pst.rearrange("p (a t) -> p a t", a=4)[:, :, 0:w]),
                             [PSR[2 + pb]], [r_xT[pb]])
                        for half in range(2):
                            S.dma("sp", qaug_scr[qblk * 8 + half:qblk * 8 + 8:2, 0:64, s:s + w].rearrange("a e t -> e a t"),
                                  xT[pb][half * 64:(half + 1) * 64, :, 0:w], reads=[r_xT[pb]], writes=[r_qaug])
                        if blk <= 3:
                            S.op("dve", lambda e, w=w: e.memset(biasf[0:w], -1.0), [], [r_biasf])
                            S.op("dve", lambda e, w=w, blk=blk: e.memset(biasf[0:w, :, 0:blk + 1], 0.0), [], [r_biasf])
                        else:
                            def trf(e, pb=pb, w=w):
                                ins = None
                                for a in range(4):
                                    ins = e.transpose(PS[4][:, a * 128:a * 128 + w], xn[pb][0:w, a * 128:(a + 1) * 128], ident[0:w, 0:w])
                                return ins
                            S.op("pe", trf, [r_xn[pb], r_ident], [PSR[4]])
                            S.op("act", lambda e, w=w: e.copy(out=qTf[:, :, 0:w], in_=PS[4][:].rearrange("p (a t) -> p a t", a=4)[:, :, 0:w]),
                                 [PSR[4]], [r_qTf])

                            def gmm(e, w=w, qblk=qblk):
                                ins = None
                                for hl in range(8):
                                    a, half = hl // 2, hl % 2
                                    hk = (qblk * 8 + hl) // 4
                                    ins = e.matmul(PS[5][0:w, hl * 16:hl * 16 + NBLK], lhsT=qTf[half * 64:(half + 1) * 64, a, 0:w],
                                                   rhs=kmd[half * 64:(half + 1) * 64, hk, :], start=True, stop=True)
                                return ins
                            S.op("pe", gmm, [r_qTf, r_kmd], [PSR[5]])
                            S.op("dve", lambda e, w=w: e.tensor_copy(out=gc[0:w, :, 0:NBLK],
                                                                     in_=PS[5][0:w, 0:128].rearrange("p (h n) -> p h n", h=8)[:, :, 0:NBLK]),
                                 [PSR[5]], [r_gc])
                            S.op("dve", lambda e, w=w, blk=blk: e.memset(gc[0:w, :, blk:16], BIGNEG), [], [r_gc])

                            def mx(e, w=w):
                                ins = None
                                for hl in range(8):
                                    ins = e.max(out=m8[0:w, hl, :], in_=gc[0:w, hl, :])
                                return ins
                            S.op("dve", mx, [r_gc], [r_m8])
                            S.op("dve", lambda e, w=w: e.tensor_tensor(out=biasf[0:w], in0=gc[0:w], in1=m8[0:w, :, 2:3].to_broadcast([w, 8, 16]),
                                                                       op=ALU.is_ge), [r_gc, r_m8], [r_biasf])
                            S.op("dve", lambda e, w=w: e.tensor_scalar(out=biasf[0:w], in0=biasf[0:w], scalar1=-1.0, scalar2=None, op0=ALU.add),
                                 [r_biasf], [r_biasf])
                            S.op("dve", lambda e, w=w, blk=blk: e.memset(biasf[0:w, :, blk:blk + 1], 0.0), [], [r_biasf])
                        S.op("dve", lambda e, w=w: e.tensor_copy(out=biasb[0:w], in_=biasf[0:w]), [r_biasf], [r_biasb])
                        pstb = PS[6 + pb][:, 0:512].bitcast(BF16)

                        def trb(e, w=w, pstb=pstb):
                            ins = None
                            for hl in range(8):
                                ins = e.transpose(pstb[0:16, hl * 128:hl * 128 + w], biasb[0:w, hl, :], identb[0:w, 0:w])
                            return ins
                        S.op("pe", trb, [r_biasb, r_identb], [PSR[6 + pb]])
                        S.op("act", lambda e, pb=pb, w=w, pstb=pstb: e.copy(out=bT[pb][:, :, 0:w],
                                                                            in_=pstb[0:16, :].rearrange("p (h t) -> p h t", h=8)[:, :, 0:w]),
                             [PSR[6 + pb]], [r_bT[pb]])
                        S.dma("sp", qaug_scr[qblk * 8:(qblk + 1) * 8, 64:80, s:s + w].rearrange("h n t -> n h t"), bT[pb][:, :, 0:w],
                              reads=[r_bT[pb]], writes=[r_qaug])
                    bq_pending[0] = b_tail
            if bq_pending[0] is not None:
                bq_pending[0]()
                bq_pending[0] = None
            S.barrier()
            if dbg_stop <= 1:
                return
            cv = Carver()
            kaug = cv.take([80, T], BF16); r_kaug_sb = R("kaug_sb")
            qaug = [cv.take([80, T], BF16) for _ in range(2)]; r_qaug_sb = [R("qaug_sb0"), R("qaug_sb1")]
            vtk = cv.take([128, T // 128, 72], BF16); r_vtk = R("vtk")
            rr = cv.take([65, 256]); r_rr = R("b_rr")
            onesf = cv.take([65, 64]); r_onesf = R("b_onesf")
            oTh = [cv.take([64, T], BF16) for _ in range(2)]; r_oTh = [R("oTh0"), R("oTh1")]
            pf = [cv.take([128, 256]) for _ in range(2)]; r_pf = [R("b_pf0"), R("b_pf1")]
            pbf = [cv.take([128, 256], BF16) for _ in range(3)]; r_pbf = [R("b_pbf0"), R("b_pbf1"), R("b_pbf2")]
            mask3 = cv.take([128, 256]); r_mask3 = R("mask3")
            rden = cv.take([64, 256]); r_rden = R("rden")
            S.dma("sp", mask3[:], mask3_d, writes=[r_mask3])
            S.op("dve", lambda e: e.memset(onesf[:], 1.0), [], [r_onesf])
            S.op("dve", lambda e: e.memset(vtk[:, :, 64:72], 1.0), [], [r_vtk])
            it = 0
            pc = 0
            for hk in range(4):
                S.dma("sp", kaug[:], kaug_scr[hk, :, 0:T], reads=[r_kaug], writes=[r_kaug_sb])
                S.dma("sp", vtk[:, :, 0:64], vtokb_scr[0:T, hk * 64:(hk + 1) * 64].rearrange("(kt i) f -> i kt f", i=128), reads=[r_vtokb], writes=[r_vtk])
                for r4 in range(4):
                    h = hk * 4 + r4
                    qa = qaug[h % 2]; r_qa = r_qaug_sb[h % 2]
                    ob = oTh[h % 2]; r_ob = r_oTh[h % 2]
                    S.dma("sp", qa[:], qaug_scr[h, :, 0:T], reads=[r_qaug], writes=[r_qa])
                    for blk in range(NBLK):
                        q0 = blk * 256
                        ob_ps = 2 + (it % 2)
                        dn_ps = 4 + (it % 2); it += 1
                        OP = PS[ob_ps]
                        DP = PS[dn_ps]
                        nkt = 2 * blk + 2
                        def emit_s(kt, st, q0=q0, qa=qa, r_qa=r_qa):
                            sp_, pbi, own0, own1, qlo, nq = st
                            S.op("pe", lambda e: e.matmul(
                                PS[sp_][:, 0:nq], lhsT=kaug[:, kt * 128:(kt + 1) * 128], rhs=qa[:, q0 + qlo:q0 + 256], start=True, stop=True),
                                [r_kaug_sb, r_qa], [PSR[sp_]])

                        def emit_exp(kt, st):
                            sp_, pbi, own0, own1, qlo, nq = st
                            if own0 or own1:
                                S.op("act", lambda e: e.activation(out=pf[sp_][:, 0:nq], in_=PS[sp_][:, 0:nq], func=AF.Exp, scale=0.125),
                                     [PSR[sp_]], [r_pf[sp_]])
                                S.op("dve", lambda e: e.tensor_tensor(out=pbf[pbi][:, 0:nq], in0=pf[sp_][:, 0:nq], in1=mask3[:, 0:nq], op=ALU.mult),
                                     [r_pf[sp_], r_mask3], [r_pbf[pbi]])
                            else:
                                S.op("act", lambda e: e.activation(out=pbf[pbi][:, 0:256], in_=PS[sp_][:, 0:256], func=AF.Exp, scale=0.125),
                                     [PSR[sp_]], [r_pbf[pbi]])

                        def emit_pv(kt, st, OP=OP, ob_ps=ob_ps):
                            sp_, pbi, own0, own1, qlo, nq = st

                            def pv(e):
                                ins = None
                                first = (kt == 0)
                                if own1:
                                    segs = [(128, 256, 0, 128, True)]
                                elif own0:
                                    segs = [(0, 128, 0, 128, True), (128, 256, 128, 256, False)]
                                else:
                                    segs = [(0, 256, 0, 256, False)]
                                for (o0, o1, p0, p1, last) in segs:
                                    ins = e.matmul(OP[0:65, o0:o1], lhsT=vtk[:, kt, 0:65], rhs=pbf[pbi][:, p0:p1], start=first, stop=last)
                                return ins
                            S.op("pe", pv, [r_vtk, r_pbf[pbi]], [PSR[ob_ps]])

                        prev = None
                        for kt in range(nkt):
                            sp_ = pc % 2; pc += 1
                            pbi = pc % 3
                            own0 = (kt == 2 * blk)
                            own1 = (kt == 2 * blk + 1)
                            qlo = 128 if own1 else 0
                            st = (sp_, pbi, own0, own1, qlo, 256 - qlo)
                            emit_s(kt, st)
                            if prev is not None:
                                emit_pv(*prev)
                            emit_exp(kt, st)
                            prev = (kt, st)
                        emit_pv(*prev)
                        S.op("dve", lambda e, OP=OP: e.reciprocal(out=rr[64:65, :], in_=OP[64:65, 0:256]), [PSR[ob_ps]], [r_rr])
                        S.op("pe", lambda e, DP=DP: e.matmul(DP[0:64, 0:256], lhsT=onesf[64:65, :], rhs=rr[64:65, :], start=True, stop=True),
                             [r_rr, r_onesf], [PSR[dn_ps]])
                        S.op("act", lambda e, DP=DP: e.copy(out=rden[:], in_=DP[0:64, 0:256]), [PSR[dn_ps]], [r_rden])
                        S.op("dve", lambda e, OP=OP, ob=ob, q0=q0: e.tensor_tensor(out=ob[:, q0:q0 + 256], in0=OP[0:64, 0:256], in1=rden[:], op=ALU.mult),
                             [PSR[ob_ps], r_rden], [r_ob])
                    S.dma("sp", oTb_scr[h, :, 0:T], ob[:], reads=[r_ob], writes=[r_oTbs])
            S.barrier()
            if dbg_stop <= 2:
                return
            cv = Carver()
            ptf = cv.take([128, NB * 16]); r_ptf = R("ptf")
            pti = cv.take([128, NB * 16], I32); r_pti = R("pti")
            pcol = cv.take([128, 1]); r_pcol = R("pcol")
            KV = cv.take([128, 16, 512]); r_KV = R("KV")
            KVb = [cv.take([128, 16, 512], BF16) for _ in range(2)]; r_KVb = [R("KVb0"), R("KVb1")]
            kTs = cv.take([64, 4, 2048], BF16); r_kTs = R("kTs")
            kms = cv.take([64, 4, 8]); r_kms2 = R("kms")
            qtok = cv.take([64, 1024]); r_qtok = R("qtok")
            ktok = cv.take([64, 512]); r_ktok = R("ktok")
            qTall = cv.take([64, 16, 64]); r_qTall = R("qTall")
            kTn = cv.take([64, 4, 64], BF16); r_kTn = R("kTn")
            vnew = cv.take([4, NB, 256], BF16); vnewf = cv.take([4, NB, 256]); r_vnew = R("vnew")
            qbf_ = cv.take([64, 4, 16]); qbb = cv.take([64, 4, 16], BF16); r_qb = R("qb")
            gs = cv.take([16, 4, 8]); m8s = cv.take([16, 4, 8]); bsf = cv.take([16, 4, 8]); bsb = cv.take([16, 4, 8], BF16); r_gs = R("gs")
            bTs = cv.take([8, 4, 16], BF16); r_bTs = R("bTs")
            inds = cv.take([8, 2048], BF16); indsf = cv.take([8, 2048]); r_inds = R("inds")
            PT = cv.take([128, 16, 16], BF16); r_PT = R("PT")
            pnf = cv.take([4, 16]); pnb = cv.take([4, 16], BF16); mask4 = cv.take([4, 16]); r_pn = R("pn")
            rds = cv.take([64, 4, 16]); r_rds = R("rds")
            oTs = cv.take([64, 16, NSAMP], BF16); r_oTs = R("oTs")
            S.dma("sp", pti[:], page_tab.rearrange("b g -> (b g)").partition_broadcast(128), writes=[r_pti])
            S.dma("sp", pcol[:], pidx_d, writes=[r_pcol])
            S.op("dve", lambda e: e.tensor_copy(out=ptf[:], in_=pti[:]), [r_pti], [r_ptf])
            S.op("dve", lambda e: e.tensor_scalar(out=ptf[:], in0=ptf[:], scalar1=128.0, scalar2=pcol[:, 0:1], op0=ALU.mult, op1=ALU.add),
                 [r_ptf, r_pcol], [r_ptf])
            S.op("dve", lambda e: e.tensor_copy(out=pti[:], in_=ptf[:]), [r_ptf], [r_pti])
            S.dma("sp", indsf[:], kinds_d, writes=[r_inds])
            S.op("dve", lambda e: e.tensor_copy(out=inds[:], in_=indsf[:]), [r_inds], [r_inds])
            S.dma("sp", mask4[:], mask4_d, writes=[r_pn])
            S.dma("sp", qtok[:], qsb_scr, reads=[r_sq], writes=[r_qtok])
            S.dma("sp", ktok[:], ksb_scr, reads=[r_sq], writes=[r_ktok])
            S.dma("sp", vnewf[:], ksb_scr[:, 256:512].rearrange("(b i) f -> i b f", i=4), reads=[r_sq], writes=[r_vnew])
            S.op("dve", lambda e: e.tensor_copy(out=vnew[:], in_=vnewf[:]), [r_vnew], [r_vnew])
            for q4 in range(4):
                def trqs(e, q4=q4):
                    ins = None
                    for k_ in range(4):
                        h = q4 * 4 + k_
                        ins = e.transpose(PS[0][0:64, k_ * 64:(k_ + 1) * 64], qtok[:, h * 64:(h + 1) * 64], ident[0:64, 0:64])
                    return ins
                S.op("pe", trqs, [r_qtok, r_ident], [PSR[0]])
                S.op("act", lambda e, q4=q4: e.copy(out=qTall[:, q4 * 4:(q4 + 1) * 4, :], in_=PS[0][0:64, 0:256].rearrange("p (a t) -> p a t", a=4)),
                     [PSR[0]], [r_qTall])

            def trks(e):
                ins = None
                for hk in range(4):
                    ins = e.transpose(PS[1][0:64, hk * 64:(hk + 1) * 64], ktok[:, hk * 64:(hk + 1) * 64], ident[0:64, 0:64])
                return ins
            S.op("pe", trks, [r_ktok, r_ident], [PSR[1]])
            S.op("act", lambda e: e.copy(out=kTn[:], in_=PS[1][0:64, 0:256].rearrange("p (a t) -> p a t", a=4)), [PSR[1]], [r_kTn])
            pool_rows = pool_kv.rearrange("g t k f -> (g t) (k f)")
            for bt in range(NB):
                kb_ = bt % 2
                for pg in range(16):
                    S.op("pool", lambda e, bt=bt, pg=pg: e.indirect_dma_start(
                        out=KV[:, pg, :], out_offset=None, in_=pool_rows,
                        in_offset=bass.IndirectOffsetOnAxis(ap=pti[:, bt * 16 + pg:bt * 16 + pg + 1], axis=0)),
                        [r_pti], [r_KV], dma=True)
                S.op("dve", lambda e, kb_=kb_: e.tensor_copy(out=KVb[kb_][:, 0:8, :], in_=KV[:, 0:8, :]), [r_KV], [r_KVb[kb_]])
                S.op("act", lambda e, kb_=kb_: e.copy(out=KVb[kb_][:, 8:16, :], in_=KV[:, 8:16, :]), [r_KV], [r_KVb[kb_]])
                for pg2 in range(8):
                    pst = PS[2 + pg2 % 2][:, 0:512].bitcast(BF16)

                    def trk2(e, kb_=kb_, pg2=pg2, pst=pst):
                        ins = None
                        for k_ in range(2):
                            pg = pg2 * 2 + k_
                            for hk in range(4):
                                ins = e.transpose(pst[0:64, hk * 256 + k_ * 128:hk * 256 + (k_ + 1) * 128],
                                                  KVb[kb_][:, pg, hk * 64:(hk + 1) * 64], identb[:])
                        return ins
                    S.op("pe", trk2, [r_KVb[kb_], r_identb], [PSR[2 + pg2 % 2]])
                    S.op("dve", lambda e, pg2=pg2, pst=pst: e.tensor_copy(out=kTs[:, :, pg2 * 256:(pg2 + 1) * 256],
                                                                         in_=pst[0:64, 0:1024].rearrange("p (h t) -> p h t", h=4)),
                         [PSR[2 + pg2 % 2]], [r_kTs])
                S.op("dve", lambda e: e.tensor_reduce(out=kms[:], in_=kTs[:].rearrange("p h (n k) -> p h n k", k=256), axis=AX.X, op=ALU.add),
                     [r_kTs], [r_kms2])
                S.op("dve", lambda e: e.tensor_scalar(out=kms[:], in0=kms[:], scalar1=1.0 / 256, scalar2=None, op0=ALU.mult), [r_kms2], [r_kms2])
                S.op("dve", lambda e, bt=bt: e.tensor_copy(out=qbf_[:].rearrange("p k (r i) -> p k r i", i=4),
                                                           in_=qTall[:, :, bt * 4:(bt + 1) * 4].rearrange("p (k r) i -> p k r i", r=4)),
                     [r_qTall], [r_qb])
                S.op("dve", lambda e: e.tensor_copy(out=qbb[:], in_=qbf_[:]), [r_qb], [r_qb])

                def gms(e):
                    ins = None
                    for hk in range(4):
                        ins = e.matmul(PS[4][0:16, hk * 8:(hk + 1) * 8], lhsT=qbf_[:, hk, :], rhs=kms[:, hk, :], start=True, stop=True)
                    return ins
                S.op("pe", gms, [r_qb, r_kms2], [PSR[4]])
                S.op("dve", lambda e: e.tensor_copy(out=gs[:], in_=PS[4][0:16, 0:32].rearrange("p (k n) -> p k n", k=4)), [PSR[4]], [r_gs])

                def mxs(e):
                    ins = None
                    for hk in range(4):
                        ins = e.max(out=m8s[:, hk, :], in_=gs[:, hk, :])
                    return ins
                S.op("dve", mxs, [r_gs], [r_gs])
                S.op("dve", lambda e: e.tensor_tensor(out=bsf[:], in0=gs[:], in1=m8s[:, :, 2:3].to_broadcast([16, 4, 8]), op=ALU.is_ge), [r_gs], [r_gs])
                S.op("dve", lambda e: e.tensor_scalar(out=bsb[:], in0=bsf[:], scalar1=-1.0, scalar2=None, op0=ALU.add), [r_gs], [r_gs])
                pstb = PS[5][:, 0:64].bitcast(BF16)

                def trbs(e, pstb=pstb):
                    ins = None
                    for hk in range(4):
                        ins = e.transpose(pstb[0:8, hk * 16:(hk + 1) * 16], bsb[:, hk, :], identb[0:16, 0:16])
                    return ins
                S.op("pe", trbs, [r_gs, r_identb], [PSR[5]])
                S.op("act", lambda e, pstb=pstb: e.copy(out=bTs[:], in_=pstb[0:8, 0:64].rearrange("p (k q) -> p k q", k=4)), [PSR[5]], [r_bTs])
                for hk in range(4):
                    sp_ = 6 + hk % 2

                    def smm(e, hk=hk, sp_=sp_):
                        ins = None
                        for kt in range(16):
                            e.matmul(PS[sp_][:, kt * 16:(kt + 1) * 16], lhsT=kTs[:, hk, kt * 128:(kt + 1) * 128], rhs=qbb[:, hk, :], start=True, stop=False)
                            ins = e.matmul(PS[sp_][:, kt * 16:(kt + 1) * 16], lhsT=inds[:, kt * 128:(kt + 1) * 128], rhs=bTs[:, hk, :], start=False, stop=True)
                        ins = e.matmul(PS[sp_][0:4, 256:272], lhsT=kTn[:, hk, bt * 4:(bt + 1) * 4], rhs=qbb[:, hk, :], start=True, stop=True)
                        return ins
                    S.op("pe", smm, [r_kTs, r_qb, r_inds, r_bTs, r_kTn], [PSR[sp_]])
                    S.op("act", lambda e, sp_=sp_: e.activation(out=PT[:].rearrange("p a b -> p (a b)"), in_=PS[sp_][:, 0:256], func=AF.Exp, scale=0.125),
                         [PSR[sp_]], [r_PT])
                    S.op("act", lambda e, sp_=sp_: e.activation(out=pnf[:], in_=PS[sp_][0:4, 256:272], func=AF.Exp, scale=0.125), [PSR[sp_]], [r_pn])
                    S.op("dve", lambda e: e.tensor_tensor(out=pnb[:], in0=pnf[:], in1=mask4[:], op=ALU.mult), [r_pn], [r_pn])

                    def pvs(e, hk=hk, kb_=kb_, bt=bt):
                        ins = None
                        for kt in range(16):
                            e.matmul(PS[0][0:64, hk * 32:hk * 32 + 16], lhsT=KVb[kb_][:, kt, 256 + hk * 64:256 + (hk + 1) * 64], rhs=PT[:, kt, :],
                                     start=(kt == 0), stop=False)
                        e.matmul(PS[0][0:64, hk * 32:hk * 32 + 16], lhsT=vnew[:, bt, hk * 64:(hk + 1) * 64], rhs=pnb[:], start=False, stop=True)
                        for kt in range(16):
                            e.matmul(PS[0][0:64, hk * 32 + 16:hk * 32 + 32], lhsT=ones_bf[:, 0:64], rhs=PT[:, kt, :], start=(kt == 0), stop=False)
                        ins = e.matmul(PS[0][0:64, hk * 32 + 16:hk * 32 + 32], lhsT=ones_bf[0:4, 0:64], rhs=pnb[:], start=False, stop=True)
                        return ins
                    S.op("pe", pvs, [r_KVb[kb_], r_PT, r_vnew, r_pn, r_ones], [PSR[0]])
                ov = PS[0][0:64, 0:128].rearrange("p (k two q) -> p k two q", k=4, two=2)
                S.op("dve", lambda e, ov=ov: e.reciprocal(out=rds[:], in_=ov[:, :, 1, :]), [PSR[0]], [r_rds])
                S.op("dve", lambda e, ov=ov, bt=bt: e.tensor_tensor(
                    out=oTs[:, :, bt * 4:(bt + 1) * 4].rearrange("p (k r) i -> p k r i", r=4),
                    in0=ov[:, :, 0, :].rearrange("p k (r i) -> p k r i", i=4),
                    in1=rds[:].rearrange("p k (r i) -> p k r i", i=4), op=ALU.mult), [PSR[0], r_rds], [r_oTs])
            S.dma("sp", oTb_scr[:, :, T:TT].rearrange("h e t -> e h t"), oTs[:], reads=[r_oTs], writes=[r_oTbs])
            S.barrier()
            if dbg_stop <= 3:
                return
            cv = Carver()
            wbo = cv.take([128, 8, D], BF16); r_wbo = R("wbo")
            wo_st = cv.take([128, 8, 256]); r_wo_st = R("b_wo_st")
            oin = [cv.take([128, 8, 512], BF16) for _ in range(2)]; r_oin = [R("oin0"), R("oin1")]
            xu = cv.take([128, DC, 512]); r_xu = R("b_xu")
            for c4 in range(4):
                S.dma("sp", wo_st[:], w_b_out.rearrange("(a p) n -> p a n", p=128)[:, :, c4 * 256:(c4 + 1) * 256], writes=[r_wo_st])
                S.op("pool", lambda e, c4=c4: e.tensor_copy(out=wbo[:, :, c4 * 256:(c4 + 1) * 256], in_=wo_st[:]), [r_wo_st], [r_wbo])
            for gi, (s, w) in enumerate(groups):
                ob_ = gi % 2
                S.dma("sp", oin[ob_][:, :, 0:w], oTb_scr[:, :, s:s + w].rearrange("(a two) e t -> (two e) a t", two=2),
                      reads=[r_oTbs], writes=[r_oin[ob_]])
                S.dma("sp", xu[:, :, 0:w], xs[:, :, s:s + w].rearrange("c p t -> p c t"), reads=[r_xs[gi]], writes=[r_xu])
                for dc in range(DC):
                    pb = dc % 2

                    def ymm(e, dc=dc, pb=pb, w=w, ob_=ob_):
                        ins = None
                        for a in range(8):
                            ins = e.matmul(PS[pb][:, 0:w], lhsT=wbo[:, a, dc * 128:(dc + 1) * 128], rhs=oin[ob_][:, a, 0:w], start=(a == 0), stop=(a == 7))
                        return ins
                    S.op("pe", ymm, [r_wbo, r_oin[ob_]], [PSR[pb]])
                    S.op("dve", lambda e, dc=dc, pb=pb, w=w: e.tensor_tensor(out=xu[:, dc, 0:w], in0=PS[pb][:, 0:w], in1=xu[:, dc, 0:w], op=ALU.add),
                         [PSR[pb], r_xu], [r_xu])
                S.dma("sp", xs[:, :, s:s + w].rearrange("c p t -> p c t"), xu[:, :, 0:w], reads=[r_xu], writes=[r_xs[gi]])
            S.barrier()

        MIX = {"c": mixer_c, "a": mixer_a, "d": mixer_d, "b": mixer_b}
        for l in range(depth):
            ffn(l, 0)
            if l < len(layer_mixers) and layer_mixers[l] in MIX:
                MIX[layer_mixers[l]](l)
            ffn(l, 1)

        S.barrier()
        for ti, (s, w) in enumerate(tiles):
            b = ti % 2
            S.dma("sp", xfm[b][:, :, 0:w], xs[:, :, s:s + w].rearrange("c p t -> p c t"),
                  reads=[r_xs[grp_of_tile(s)]], writes=[r_xfm[b]])

            def tr2(e, b=b, w=w):
                ins = None
                for c in range(DC):
                    ins = e.transpose(PS[6 + c // 4][0:w, (c % 4) * 128:(c % 4 + 1) * 128], xfm[b][:, c, 0:w], ident[:])
                return ins
            S.op("pe", tr2, [r_xfm[b], r_ident], [PSR[6], PSR[7]])
            for hh in range(2):
                S.op("act", lambda e, b=b, w=w, hh=hh: e.copy(out=xtok[b][0:w, 512 * hh:512 * hh + 512],
                                                             in_=PS[6 + hh][0:w, :]), [PSR[6 + hh]], [r_xtok[b]])
            dst = yp[s:s + w, :] if s < T else ysm[:, :]
            S.dma("sp", dst, xtok[b][0:w, :], reads=[r_xtok[b]])

        S.emit(final_waits=("sp",))
    return nc


_IDENT = np.eye(128, dtype=np.float32)
PAST_LEN = 2048


def const_inputs(T):
    nt = T // 128 + 1
    pos = np.zeros(nt * 128, np.float64)
    pos[:T] = np.arange(T)
    pos[T:T + NSAMP] = PAST_LEN + (np.arange(NSAMP) % 4)
    inv = 500000.0 ** (-np.arange(8, dtype=np.float64) / 8)
    ang = pos[:, None] * inv[None, :]
    kk = np.arange(128)[:, None]
    qq = np.arange(128)[None, :]
    mask2 = np.concatenate([(kk >= qq), (kk <= qq)], axis=1).astype(np.float32)
    return {
        "ident": _IDENT,
        "triu": np.triu(np.ones((32, 32), np.float32)),
        "rope_cos": np.cos(ang).astype(np.float32),
        "rope_sin": np.sin(ang).astype(np.float32),
        "mask2": mask2,
        "mask3": np.concatenate([(kk <= qq), np.ones((128, 128), bool)], axis=1).astype(np.float32),
        "mask4": (np.arange(4)[:, None] <= (np.arange(16)[None, :] % 4)).astype(np.float32),
        "kind": (1000.0 * (np.arange(16)[:, None] == (np.arange(T)[None, :] // 256))).astype(np.float32),
        "kinds": (1000.0 * (np.arange(8)[:, None] == (np.arange(2048)[None, :] // 256))).astype(np.float32),
        "pidx": np.arange(128, dtype=np.float32).reshape(128, 1),
        "ssd_l1": (np.arange(64)[:, None] > np.arange(64)[None, :]).astype(np.float32),
        "ssd_l2": (np.arange(64)[:, None] <= np.arange(64)[None, :]).astype(np.float32),
    }


def kernel(**inputs):
    T = 4096
    layer_mixers = ("a", "b", "c", "d")
    npool = int(inputs["cache_b_kv"].shape[1])
    nc = build_program(T=T, depth=4, layer_mixers=layer_mixers, NPOOL=npool)
    f = lambda k: np.ascontiguousarray(inputs[k])
    xpr = f("x_prompt")
    xsa = f("x_sample")
    consts = const_inputs(T)
    shared = {
        "norm_gain": f("norm_gain"), "w_ffn_up": f("w_ffn_up"), "w_ffn_down": f("w_ffn_down"),
        "w_a_in": f("w_a_in")[0], "a_qk_gain": f("a_qk_gain")[0], "w_a_out": f("w_a_out")[0],
        "w_b_in": f("w_b_in")[0], "b_qk_gain": f("b_qk_gain")[0], "w_b_out": f("w_b_out")[0],
        "pool_kv": f("cache_b_kv")[0].reshape(npool, 128, 2, 256),
        "w_c_in": f("w_c_in")[0], "w_c_gate2": f("w_c_gate2")[0], "b_c_gate": f("b_c_gate")[0],
        "c_norm_gain": f("c_norm_gain")[0], "w_c_out": f("w_c_out")[0],
        "w_d_in": f("w_d_in")[0], "d_conv_w": f("d_conv_w")[0], "d_conv_b": f("d_conv_b")[0],
        "d_dt_bias": f("d_dt_bias")[0], "d_a_log": f("d_a_log")[0], "d_skip": f("d_skip")[0],
        "d_norm_gain": f("d_norm_gain")[0], "w_d_out": f("w_d_out")[0],
    }
    shared.update(consts)
    shared = {k: np.ascontiguousarray(v) for k, v in shared.items()}
    in_maps = []
    for c in range(8):
        b0, b1 = c * 16, (c + 1) * 16
        m = dict(shared)
        m.update({
            "xp": xpr[c % 4],
            "xsm": xsa[b0:b1].reshape(NSAMP, D),
            "ca1": f("cache_a_w1")[0, b0:b1], "ca2": f("cache_a_w2")[0, b0:b1], "ca3": f("cache_a_w3")[0, b0:b1],
            "page_tab": f("page_table")[b0:b1].astype(np.int32),
            "sc_in": f("state_c")[0, b0:b1],
            "sd_ssm": f("state_d_ssm")[0, b0:b1], "sd_conv": f("state_d_conv")[0, b0:b1],
        })
        in_maps.append({k: np.ascontiguousarray(v) for k, v in m.items()})
    res = run_bass_kernel_spmd(nc, in_maps, core_ids=list(range(8)))
    r = res.results

    def pstack(name, shape):
        return np.stack([r[c][name].reshape(shape) for c in range(4)], axis=0)[None]

    def scat(name, shape):
        return np.concatenate([r[c][name].reshape((16,) + shape) for c in range(8)], axis=0)[None]

    y_prompt = np.stack([r[c]["yp"] for c in range(4)], axis=0)
    y_sample = np.concatenate([r[c]["ysm"].reshape(16, 4, D) for c in range(8)], axis=0)
    outs = [y_prompt, y_sample]
    for g, wn in enumerate((128, 512, 2048)):
        outs.append(pstack(f"aw{g}p", (wn, 2, 8, 64)))
        outs.append(scat(f"aw{g}s", (4, 2, 8, 64)))
    outs.append(pstack("bkvp", (T, 2, 4, 64)))
    outs.append(scat("bkvs", (4, 2, 4, 64)))
    outs.append(pstack("csp", (4, 128, 256)))
    outs.append(scat("css", (4, 128, 256)))
    outs.append(pstack("dssp", (32, 64, 128)))
    outs.append(scat("dsss", (32, 64, 128)))
    outs.append(pstack("dcp", (3, 3072)))
    outs.append(scat("dcs", (3, 3072)))
    return tuple(np.ascontiguousarray(o, dtype=np.float32) for o in outs)
```

```python
import contextlib
import numpy as np
import concourse.bass as bass
import concourse.mybir as mybir
from concourse.bass_utils import run_bass_kernel_spmd

F32 = mybir.dt.float32
BF16 = mybir.dt.bfloat16
I32 = mybir.dt.int32
AF = mybir.ActivationFunctionType
ALU = mybir.AluOpType
AX = mybir.AxisListType

ENG_NAMES = ("sp", "act", "dve", "pool", "pe")


class Region:
    __slots__ = ("name", "writer", "readers")

    def __init__(self, name):
        self.name = name
        self.writer = None
        self.readers = []


class Op:
    __slots__ = ("eng", "fn", "deps", "dma", "idx", "signal", "sig_val", "slot", "slot_prev")

    def __init__(self, eng, fn, dma):
        self.eng = eng
        self.fn = fn
        self.dma = dma
        self.deps = []
        self.signal = False
        self.sig_val = 0
        self.slot = None
        self.slot_prev = None


class Sched:
    def __init__(self, nc, n_dma_slots=24, same_engine_sync=True):
        self.nc = nc
        self.ops = {e: [] for e in ENG_NAMES}
        self.n_dma_slots = n_dma_slots
        self.same_engine_sync = same_engine_sync
        self.dma_count = {e: 0 for e in ENG_NAMES}
        self.slot_last = {}

    def region(self, name="r"):
        return Region(name)

    def regions(self, name, n):
        return [Region(f"{name}{i}") for i in range(n)]

    def op(self, eng, fn, reads=(), writes=(), dma=False):
        o = Op(eng, fn, dma)
        deps = []
        for r in reads:
            if r.writer is not None:
                deps.append(r.writer)
        for w in writes:
            if w.writer is not None:
                deps.append(w.writer)
            deps.extend(w.readers)
        seen = set()
        for d in deps:
            if id(d) in seen or d is o:
                continue
            seen.add(id(d))
            if d.eng == eng and not d.dma:
                if eng == "pe" or not self.same_engine_sync:
                    continue
            o.deps.append(d)
            d.signal = True
        for r in reads:
            r.readers.append(o)
        for w in writes:
            w.writer = o
            w.readers = []
        if dma:
            k = self.dma_count[eng]
            self.dma_count[eng] = k + 1
            o.slot = (eng, k % self.n_dma_slots)
            o.sig_val = 16 * (k // self.n_dma_slots + 1)
            o.slot_prev = self.slot_last.get(o.slot)
            if o.slot_prev is not None:
                o.slot_prev.signal = True
            self.slot_last[o.slot] = o
            o.signal = True
        self.ops[eng].append(o)
        return o

    def barrier(self):
        lasts = []
        for e in ENG_NAMES:
            for o in reversed(self.ops[e]):
                if o.fn is not None and not o.dma:
                    lasts.append(o)
                    break
        lasts += list(self.slot_last.values())
        for e in ENG_NAMES:
            if not self.ops[e]:
                continue
            o = Op(e, None, False)
            for d in lasts:
                if d.eng == e and not d.dma:
                    continue
                if all(d is not x for x in o.deps):
                    o.deps.append(d)
                    d.signal = True
            self.ops[e].append(o)

    def dma(self, eng, out, in_, reads=(), writes=(), **kw):
        return self.op(eng, lambda e: e.dma_start(out=out, in_=in_, **kw), reads, writes, dma=True)

    def emit(self, final_waits=()):
        nc = self.nc
        import contextlib
        with contextlib.ExitStack() as st:
            esem = {e: st.enter_context(nc.semaphore(f"s_{e}")) for e in ENG_NAMES}
            dsem = {}
            for e in ENG_NAMES:
                if self.dma_count[e] > 0:
                    for s in range(min(self.n_dma_slots, self.dma_count[e])):
                        dsem[(e, s)] = st.enter_context(nc.semaphore(f"d_{e}_{s}"))
            for e in ENG_NAMES:
                c = 0
                for o in self.ops[e]:
                    if not o.dma and o.signal:
                        c += 1
                        o.sig_val = c
            block = st.enter_context(nc.Block())

            def run(ename):
                def body(eng):
                    known = {}
                    for o in self.ops[ename]:
                        waits = []
                        deps = list(o.deps)
                        if o.dma and o.slot_prev is not None:
                            deps.append(o.slot_prev)
                        for d in deps:
                            sem = dsem[d.slot] if d.dma else esem[d.eng]
                            key = d.slot if d.dma else d.eng
                            if known.get(key, 0) >= d.sig_val:
                                continue
                            known[key] = d.sig_val
                            waits.append((sem, d.sig_val))
                        for sem, v in waits:
                            eng.wait_ge(sem, v)
                        if o.fn is None:
                            continue
                        ins = o.fn(eng)
                        if o.signal:
                            if o.dma:
                                ins.then_inc(dsem[o.slot], 16)
                            else:
                                ins.then_inc(esem[ename], 1)
                    if ename in final_waits:
                        for slot, last in self.slot_last.items():
                            if slot[0] == ename and known.get(slot, 0) < last.sig_val:
                                eng.wait_ge(dsem[slot], last.sig_val)
                return body

            if self.ops["sp"] or "sp" in final_waits:
                block.sync(run("sp"))
            if self.ops["act"]:
                block.scalar(run("act"))
            if self.ops["dve"]:
                block.vector(run("dve"))
            if self.ops["pool"] or "pool" in final_waits:
                block.gpsimd(run("pool"))
            if self.ops["pe"]:
                block.tensor(run("pe"))


D = 1024
DC = 8
DFF = 2816
NF = 22
EPS = 1e-6
NSAMP = 64
FPARTS = [(0, 4), (4, 8), (8, 12), (12, 16), (16, 19), (19, 22)]


class K:
    pass


def build_program(T=4096, depth=4, layer_mixers=(), n_ffn_parts=None, dbg_stop=99, dbg=(), NPOOL=2560):
    nc = bass.Bass("TRN2", target_bir_lowering=False)
    TT = T + NSAMP
    groups = [(s, 512) for s in range(0, T, 512)] + [(T, NSAMP)]
    tiles = [(s, 128) for s in range(0, T, 128)] + [(T, NSAMP)]

    def din(name, shape, dt=F32):
        return nc.dram_tensor(name, list(shape), dt, kind="ExternalInput").ap()

    def dout(name, shape, dt=F32):
        return nc.dram_tensor(name, list(shape), dt, kind="ExternalOutput").ap()

    xp = din("xp", [T, D])
    xsm = din("xsm", [NSAMP, D])
    norm_gain = din("norm_gain", [4, 3, D])
    w_up = din("w_ffn_up", [4, 2, D, 2 * DFF])
    w_down = din("w_ffn_down", [4, 2, DFF, D])
    ident_d = din("ident", [128, 128])
    triu_d = din("triu", [32, 32])
    NB = NSAMP // 4
    if "c" in layer_mixers:
        sc_in = din("sc_in", [NB, 4, 128, 256])
        w_c_in = din("w_c_in", [D, 3088])
        w_c_gate2 = din("w_c_gate2", [16, 512])
        b_c_gate = din("b_c_gate", [512])
        c_norm_gain = din("c_norm_gain", [256])
        w_c_out = din("w_c_out", [D, D])
        csp = dout("csp", [4, 128, 256])
        css = dout("css", [NB, 4, 128, 256])
    if "a" in layer_mixers:
        w_a_in = din("w_a_in", [D, 4608])
        a_qk_gain = din("a_qk_gain", [2, 64])
        w_a_out = din("w_a_out", [512, D])
        ca1 = din("ca1", [NB, 128, 2, 8, 64])
        ca2 = din("ca2", [NB, 512, 2, 8, 64])
        ca3 = din("ca3", [NB, 2048, 2, 8, 64])
        rope_cos = din("rope_cos", [len(tiles) * 128, 8])
        rope_sin = din("rope_sin", [len(tiles) * 128, 8])
        mask2_d = din("mask2", [128, 256])
        awp = [dout(f"aw{g}p", [min(wn, T), 2, 512]) for g, wn in enumerate((128, 512, 2048))]
        aws = [dout(f"aw{g}s", [NSAMP, 2, 512]) for g in range(3)]
        qT_scr = nc.dram_tensor("qT_scr", [3, 4, 128, TT], BF16).ap()
        kT_scr = nc.dram_tensor("kT_scr", [3, 4, 128, TT], BF16).ap()
        vtok_scr = nc.dram_tensor("vtok_scr", [3, TT, 512], BF16).ap()
        qs_scr = nc.dram_tensor("qs_scr", [3, NSAMP, 512], F32).ap()
        kvs_scr = nc.dram_tensor("kvs_scr", [3, 2, NSAMP, 512], F32).ap()
        os_scr = nc.dram_tensor("os_scr", [NB, 8, 4, 64], F32).ap()
    if "d" in layer_mixers:
        w_d_in = din("w_d_in", [D, 5152])
        d_conv_w = din("d_conv_w", [4, 3072])
        d_conv_b = din("d_conv_b", [3072])
        d_dt_bias = din("d_dt_bias", [32])
        d_a_log = din("d_a_log", [32])
        d_skip = din("d_skip", [32])
        d_norm_gain = din("d_norm_gain", [2048])
        w_d_out = din("w_d_out", [2048, D])
        sd_ssm = din("sd_ssm", [NB, 32, 64, 128])
        sd_conv = din("sd_conv", [NB, 3, 3072])
        ssd_l1 = din("ssd_l1", [64, 64])
        ssd_l2 = din("ssd_l2", [64, 64])
        dssp = dout("dssp", [32, 64, 128])
        dsss = dout("dsss", [NB, 32, 64, 128])
        dcp = dout("dcp", [3, 3072])
        dcs = dout("dcs", [NB, 3, 3072])
        xbcT_scr = nc.dram_tensor("xbcT_scr", [8, 128, TT], BF16).ap()
        xtokd_scr = nc.dram_tensor("xtokd_scr", [TT, 2560], BF16).ap()
        zs_scr = nc.dram_tensor("zs_scr", [TT, 2048], BF16).ap()
        dt_scr = nc.dram_tensor("dt_scr", [2, TT, 32], F32).ap()
    if "b" in layer_mixers:
        w_b_in = din("w_b_in", [D, 1536])
        b_qk_gain = din("b_qk_gain", [2, 64])
        w_b_out = din("w_b_out", [D, D])
        pool_kv = din("pool_kv", [NPOOL, 128, 2, 256])
        page_tab = din("page_tab", [NB, 16], I32)
        kind_d = din("kind", [16, T])
        kinds_d = din("kinds", [8, 2048])
        mask3_d = din("mask3", [128, 256])
        mask4_d = din("mask4", [4, 16])
        pidx_d = din("pidx", [128, 1])
        if "a" not in layer_mixers:
            rope_cos = din("rope_cos", [len(tiles) * 128, 8])
            rope_sin = din("rope_sin", [len(tiles) * 128, 8])
        bkvp = dout("bkvp", [T, 2, 256])
        if 'dbgo' in dbg:
            dbgo = dout("dbgo", [16, 64, 512])
        bkvs = dout("bkvs", [NSAMP, 2, 256])
        kaug_scr = nc.dram_tensor("kaug_scr", [4, 80, TT], BF16).ap()
        qaug_scr = nc.dram_tensor("qaug_scr", [16, 80, TT], BF16).ap()
        vtokb_scr = nc.dram_tensor("vtokb_scr", [TT, 256], BF16).ap()
        oTb_scr = nc.dram_tensor("oTb_scr", [16, 64, TT], BF16).ap()
        kmT_scr = nc.dram_tensor("kmT_scr", [2, 128, T // 256], F32).ap()
        qsb_scr = nc.dram_tensor("qsb_scr", [NSAMP, 1024], F32).ap()
        ksb_scr = nc.dram_tensor("ksb_scr", [NSAMP, 512], F32).ap()
    yp = dout("yp", [T, D])
    ysm = dout("ysm", [NSAMP, D])
    xs = nc.dram_tensor("xs_scratch", [DC, 128, TT], F32).ap()

    with contextlib.ExitStack() as st:
        def sb(name, shape, dt=F32):
            return st.enter_context(nc.sbuf_tensor("sb_" + name, list(shape), dt))

        def ps(name, shape, dt=F32):
            return st.enter_context(nc.psum_tensor("pp_" + name, list(shape), dt))

        S = Sched(nc)
        R = S.region

        ident = sb("ident", [128, 128]); r_ident = R("ident")
        ones_bf = sb("ones_bf", [128, 128], BF16); r_ones = R("ones")
        gain = sb("gain", [128, 12, DC]); r_gain = R("gain")
        epsc = sb("epsc", [128, 1]); r_eps = R("eps")
        S.dma("sp", ident[:], ident_d, writes=[r_ident])
        identb = sb("identb", [128, 128], BF16); r_identb = R("identb")
        S.op("dve", lambda e: e.tensor_copy(out=identb[:], in_=ident[:]), [r_ident], [r_identb])
        S.op("dve", lambda e: e.memset(ones_bf[:], 1.0), [], [r_ones])
        S.op("dve", lambda e: e.memset(epsc[:], EPS), [], [r_eps])
        S.dma("sp", gain[:], norm_gain.rearrange("l k (c p) -> p (l k) c", p=128), writes=[r_gain],
              allow_slow_non_contiguous=True)

        _ng = len(groups)
        HW = max(sum(w_ for (_, w_) in groups[:_ng // 2]), sum(w_ for (_, w_) in groups[_ng // 2:]))
        NA_H = 7
        hT_flat = sb("hT", [128, max(DC * TT, (DC + NA_H) * HW)], BF16); r_hT = [R(f"hT{g}") for g in range(len(groups))]
        hT = hT_flat[:, 0:DC * TT].rearrange("p (c t) -> p c t", c=DC)
        rs = [sb(f"rs{i}", [128, 512]) for i in range(2)]; r_rs = [R(f"rs{i}") for i in range(2)]
        ARENA = 34816
        arena = sb("arena", [128, ARENA])

        class Carver:
            def __init__(self):
                self.off = 0

            def take(self, shape, dt=F32):
                n = 1
                for d in shape[1:]:
                    n *= d
                nw = n if dt in (F32, I32) else (n + 1) // 2
                nw = (nw + 7) // 8 * 8
                v = arena[:, self.off:self.off + nw]
                self.off += nw
                assert self.off <= ARENA, ("arena overflow", self.off)
                if dt != F32:
                    v = v.bitcast(dt)
                v = v[:, 0:n]
                if shape[0] < 128:
                    v = v[0:shape[0], :]
                if len(shape) > 2:
                    names = "abcdefg"[:len(shape) - 1]
                    kw = {names[i]: shape[i + 1] for i in range(len(shape) - 2)}
                    v = v.rearrange("p (" + " ".join(names) + ") -> p " + " ".join(names), **kw)
                return v

        cv = Carver()
        xg = [cv.take([128, DC, 512]) for i in range(2)]; r_xg = [R(f"xg{i}") for i in range(2)]
        sq = cv.take([128, DC, 512], BF16); r_sq = R("sq")
        cvf = Carver()
        aT_h = hT_flat[:, DC * HW:DC * HW + NA_H * HW].rearrange("p (j t) -> p j t", j=NA_H)
        aT_a = cvf.take([128, NF - NA_H, HW], BF16)
        hTl = hT_flat[:, 0:DC * HW].rearrange("p (c t) -> p c t", c=DC)
        fxg = [cvf.take([128, DC, 256]) for i in range(2)]; r_fxg = [R("fxg0"), R("fxg1")]
        fsq = cvf.take([128, DC, 256], BF16); r_fsq = R("fsq")
        sg = [cvf.take([128, 512], BF16) for i in range(4)]; r_sg = [R(f"sg{i}") for i in range(4)]
        wst = [cvf.take([128, DC, 256]) for i in range(2)]; r_wst = [R(f"wst{i}") for i in range(2)]
        wbf = [cvf.take([128, DC, 256], BF16) for i in range(2)]; r_wbf = [R(f"wbf{i}") for i in range(2)]
        wdst = cvf.take([128, NF, 128]); r_wdst = R("wdst")
        wdbf = [cvf.take([128, NF, 128], BF16) for i in range(2)]; r_wdbf = [R("wdbf0"), R("wdbf1")]
        xl = [cvf.take([128, 512]) for i in range(2)]; r_xl = [R("xl0"), R("xl1")]
        r_aTf = [R(f"aTf{g}") for g in range(len(groups))]
        r_hTl = [R(f"hTl{g}") for g in range(len(groups))]
        cv0 = Carver()
        xtok = [cv0.take([128, D]) for i in range(2)]; r_xtok = [R(f"xtok{i}") for i in range(2)]
        xfm = [cv0.take([128, DC, 128]) for i in range(2)]; r_xfm = [R(f"xfm{i}") for i in range(2)]
        PS = [ps(f"ps{i}", [128, 512]) for i in range(8)]
        PSR = [R(f"ps{i}") for i in range(8)]

        r_xs = [R(f"xs{g}") for g in range(len(groups))]

        def grp_of_tile(s):
            return len(groups) - 1 if s >= T else s // 512

        cnt = {"w": 0, "x": 0, "ps": 0}
        for ti, (s, w) in enumerate(tiles):
            b = ti % 2
            src = xp[s:s + w, :] if s < T else xsm[:, :]
            S.dma("sp", xtok[b][0:w, :], src, writes=[r_xtok[b]])

            def tr(e, b=b, w=w):
                ins = None
                for c in range(DC):
                    ins = e.transpose(PS[6 + c // 4][:, (c % 4) * 128:(c % 4) * 128 + w],
                                      xtok[b][0:w, c * 128:(c + 1) * 128], ident[0:w, 0:w])
                return ins
            S.op("pe", tr, [r_xtok[b], r_ident], [PSR[6], PSR[7]])
            for hh in range(2):
                S.op("act", lambda e, b=b, w=w, hh=hh: e.copy(
                    out=xfm[b][:, 4 * hh:4 * hh + 4, 0:w],
                    in_=PS[6 + hh][:].rearrange("p (c t) -> p c t", c=4)[:, :, 0:w]),
                    [PSR[6 + hh]], [r_xfm[b]])
            S.dma("sp", xs[:, :, s:s + w].rearrange("c p t -> p c t"), xfm[b][:, :, 0:w],
                  reads=[r_xfm[b]], writes=[r_xs[grp_of_tile(s)]])

        S.barrier()

        def norm_pass(gk):
            for gi, (s, w) in enumerate(groups):
                b = gi % 2
                S.dma("sp", xg[b][:, :, 0:w], xs[:, :, s:s + w].rearrange("c p t -> p c t"),
                      reads=[r_xs[gi]], writes=[r_xg[b]])
                S.op("act", lambda e, b=b, w=w: e.activation(out=sq[:, :, 0:w], in_=xg[b][:, :, 0:w], func=AF.Square),
                     [r_xg[b]], [r_sq])
                pb = 4 + (gi % 2)

                def nm(e, w=w, pb=pb):
                    ins = None
                    for c in range(DC):
                        ins = e.matmul(PS[pb][:, 0:w], lhsT=ones_bf[:], rhs=sq[:, c, 0:w], start=(c == 0), stop=(c == DC - 1))
                    return ins
                S.op("pe", nm, [r_sq, r_ones], [PSR[pb]])
                S.op("act", lambda e, b=b, w=w, pb=pb: e.activation(out=rs[b][:, 0:w], in_=PS[pb][:, 0:w], func=AF.Ln,
                                                                    bias=epsc[:], scale=1.0 / D),
                     [PSR[pb], r_eps], [r_rs[b]])
                S.op("act", lambda e, b=b, w=w: e.activation(out=rs[b][:, 0:w], in_=rs[b][:, 0:w], func=AF.Exp, scale=-0.5),
                     [r_rs[b]], [r_rs[b]])

                def hmul(e, b=b, w=w, s=s):
                    ins = None
                    for c in range(DC):
                        ins = e.scalar_tensor_tensor(out=hT[:, c, s:s + w], in0=xg[b][:, c, 0:w],
                                                     scalar=gain[:, gk, c:c + 1], in1=rs[b][:, 0:w],
                                                     op0=ALU.mult, op1=ALU.mult)
                    return ins
                S.op("dve", hmul, [r_xg[b], r_rs[b], r_gain], [r_hT[gi]])

        def aT_of(j):
            return aT_h[:, j] if j < NA_H else aT_a[:, j - NA_H]

        def ffn(l, i):
            gk = l * 3 + (0 if i == 0 else 2)
            S.barrier()
            NG = len(groups)
            for gl in (list(range(0, NG // 2)), list(range(NG // 2, NG))):
                base = groups[gl[0]][0]
                subs = []
                for gi in gl:
                    s0, w0 = groups[gi]
                    if w0 == 512:
                        subs += [(gi, s0, 256), (gi, s0 + 256, 256)]
                    else:
                        subs.append((gi, s0, w0))
                for k_, (gi, s_, w) in enumerate(subs):
                    b = k_ % 2
                    S.dma("sp", fxg[b][:, :, 0:w], xs[:, :, s_:s_ + w].rearrange("c p t -> p c t"), reads=[r_xs[gi]], writes=[r_fxg[b]])
                    S.op("act", lambda e, b=b, w=w: e.activation(out=fsq[:, :, 0:w], in_=fxg[b][:, :, 0:w], func=AF.Square),
                         [r_fxg[b]], [r_fsq])
                    pb = 6 + b

                    def nm(e, w=w, pb=pb):
                        ins = None
                        for c in range(DC):
                            ins = e.matmul(PS[pb][:, 0:w], lhsT=ones_bf[:], rhs=fsq[:, c, 0:w], start=(c == 0), stop=(c == DC - 1))
                        return ins
                    S.op("pe", nm, [r_fsq, r_ones], [PSR[pb]])
                    S.op("act", lambda e, b=b, w=w, pb=pb: e.activation(out=rs[b][:, 0:w], in_=PS[pb][:, 0:w], func=AF.Ln,
                                                                        bias=epsc[:], scale=1.0 / D), [PSR[pb], r_eps], [r_rs[b]])
                    S.op("act", lambda e, b=b, w=w: e.activation(out=rs[b][:, 0:w], in_=rs[b][:, 0:w], func=AF.Exp, scale=-0.5),
                         [r_rs[b]], [r_rs[b]])

                    def hmul(e, b=b, w=w, lo=s_ - base):
                        ins = None
                        for c in range(DC):
                            ins = e.scalar_tensor_tensor(out=hTl[:, c, lo:lo + w], in0=fxg[b][:, c, 0:w], scalar=gain[:, gk, c:c + 1],
                                                         in1=rs[b][:, 0:w], op0=ALU.mult, op1=ALU.mult)
                        return ins
                    S.op("dve", hmul, [r_fxg[b], r_rs[b], r_gain], [r_hTl[gi]])
                wsrc = w_up[l, i].rearrange("(c p) n -> p c n", p=128)
                for j in range(NF):
                    wb = cnt["w"] % 2
                    cnt["w"] += 1
                    S.dma("sp", wst[wb][:, :, 0:128], wsrc[:, :, j * 128:(j + 1) * 128], writes=[r_wst[wb]])
                    S.dma("sp", wst[wb][:, :, 128:256], wsrc[:, :, DFF + j * 128:DFF + (j + 1) * 128], writes=[r_wst[wb]])
                    S.op("pool", lambda e, wb=wb: e.tensor_copy(out=wbf[wb][:], in_=wst[wb][:]), [r_wst[wb]], [r_wbf[wb]])
                    for gi in gl:
                        s_, w = groups[gi]
                        lo = s_ - base
                        pb = cnt["ps"] % 4
                        cnt["ps"] += 1

                        def up(e, wb=wb, lo=lo, w=w, pb=pb):
                            ins = None
                            for c in range(DC):
                                ins = e.matmul(PS[pb][:, 0:w], lhsT=wbf[wb][:, c, 0:128], rhs=hTl[:, c, lo:lo + w],
                                               start=(c == 0), stop=(c == DC - 1))
                            for c in range(DC):
                                ins = e.matmul(PS[4 + pb][:, 0:w], lhsT=wbf[wb][:, c, 128:256], rhs=hTl[:, c, lo:lo + w],
                                               start=(c == 0), stop=(c == DC - 1))
                            return ins
                        S.op("pe", up, [r_wbf[wb], r_hTl[gi]], [PSR[pb], PSR[4 + pb]])
                        S.op("act", lambda e, pb=pb, w=w: e.activation(out=sg[pb][:, 0:w], in_=PS[pb][:, 0:w], func=AF.Silu),
                             [PSR[pb]], [r_sg[pb]])
                        S.op("dve", lambda e, pb=pb, w=w, lo=lo, j=j: e.tensor_tensor(
                            out=aT_of(j)[:, lo:lo + w], in0=sg[pb][:, 0:w], in1=PS[4 + pb][:, 0:w], op=ALU.mult),
                            [r_sg[pb], PSR[4 + pb]], [r_aTf[gi]])
                steps = [(dc, gi) for dc in range(DC) for gi in gl]

                def issue_xload(k):
                    dc, gi = steps[k]
                    s_, w = groups[gi]
                    S.dma("sp", xl[k % 2][:, 0:w], xs[dc, :, s_:s_ + w], reads=[r_xs[gi]], writes=[r_xl[k % 2]])
                issue_xload(0)
                for k, (dc, gi) in enumerate(steps):
                    s_, w = groups[gi]
                    lo = s_ - base
                    db = dc % 2
                    if gi == gl[0]:
                        S.dma("sp", wdst[:], w_down[l, i, :, dc * 128:(dc + 1) * 128].rearrange("(j p) n -> p j n", p=128), writes=[r_wdst])
                        S.op("pool", lambda e, db=db: e.tensor_copy(out=wdbf[db][:], in_=wdst[:]), [r_wdst], [r_wdbf[db]])
                    if k + 1 < len(steps):
                        issue_xload(k + 1)
                    pb = 4 + (k % 2)

                    def dn(e, db=db, lo=lo, w=w, pb=pb):
                        ins = None
                        for j in range(NF):
                            ins = e.matmul(PS[pb][:, 0:w], lhsT=wdbf[db][:, j, :], rhs=aT_of(j)[:, lo:lo + w], start=(j == 0), stop=(j == NF - 1))
                        return ins
                    S.op("pe", dn, [r_wdbf[db], r_aTf[gi]], [PSR[pb]])
                    S.op("dve", lambda e, k=k, w=w, pb=pb: e.scalar_tensor_tensor(
                        out=xl[k % 2][:, 0:w], in0=PS[pb][:, 0:w], scalar=0.5, in1=xl[k % 2][:, 0:w], op0=ALU.mult, op1=ALU.add),
                        [PSR[pb], r_xl[k % 2]], [r_xl[k % 2]])
                    S.dma("act", xs[dc, :, s_:s_ + w], xl[k % 2][:, 0:w], reads=[r_xl[k % 2]], writes=[r_xs[gi]])
            S.barrier()

        def mixer_c(l):
            norm_pass(l * 3 + 1)
            S.barrier()
            cv = Carver()
            wcin = cv.take([128, DC, 3088], BF16); r_wcin = R("wcin")
            wcout = cv.take([128, DC, D], BF16); r_wcout = R("wcout")
            stg = [cv.take([128, DC, 256]) for _ in range(2)]; r_stg = [R("stg0"), R("stg1")]
            wg2s = cv.take([16, 512]); wg2 = cv.take([16, 512], BF16); r_wg2 = R("wg2")
            negb = cv.take([128, 4]); r_negb = R("negb")
            cng = cv.take([128, 2]); r_cng = R("cng")
            triu = cv.take([32, 32]); r_triu = R("triu")
            glr = cv.take([16, 256], BF16); r_glr = R("glr")
            W = 256
            l1 = cv.take([128, W]); r_l1 = R("l1")
            csA = cv.take([128, W]); csB = cv.take([128, W]); r_cs = [R("csA"), R("csB")]
            eb = cv.take([128, W]); enb = cv.take([128, W]); ekd = cv.take([128, W]); r_e = R("e")
            dec = cv.take([128, 16]); r_dec = R("dec")
            qe = cv.take([128, W], BF16); ke = cv.take([128, W], BF16); kd = cv.take([128, W], BF16); r_qk = R("qk")
            kdt = [cv.take([32, 128], BF16) for _ in range(2)]; r_kdt = [R("kdt0"), R("kdt1")]
            vt = [cv.take([32, 256], BF16) for _ in range(2)]; r_vt = [R("vt0"), R("vt1")]
            att = [cv.take([32, 32], BF16) for _ in range(2)]; r_att = [R("att0"), R("att1")]
            Sf = cv.take([128, 4, 256]); r_Sf = [R(f"Sf{h}") for h in range(4)]
            Sb = cv.take([128, 4, 256], BF16); r_Sb = [R(f"Sb{h}") for h in range(4)]
            oT = cv.take([128, 8, W]); r_oT = R("oT")
            osq = cv.take([128, 8, W], BF16); r_osq = R("osq")
            o2 = cv.take([128, 8, W], BF16); r_o2 = R("o2")
            rstd = cv.take([128, W]); r_rstd = R("rstd")
            sgr = cv.take([128, W]); r_sgr = R("sgr")
            t1 = cv.take([128, W]); r_t1 = R("t1")
            xu = cv.take([128, DC, W]); r_xu = R("xu")
            S.dma("sp", triu[:], triu_d, writes=[r_triu])
            S.dma("sp", wg2s[:], w_c_gate2, writes=[r_wg2])
            S.op("pool", lambda e: e.tensor_copy(out=wg2[:], in_=wg2s[:]), [r_wg2], [r_wg2])
            S.dma("sp", negb[:], b_c_gate.rearrange("(h p) -> p h", p=128), writes=[r_negb], allow_slow_non_contiguous=True)
            S.op("pool", lambda e: e.tensor_scalar(out=negb[:], in0=negb[:], scalar1=-1.0, scalar2=None, op0=ALU.mult),
                 [r_negb], [r_negb])
            S.dma("sp", cng[:], c_norm_gain.rearrange("(e p) -> p e", p=128), writes=[r_cng], allow_slow_non_contiguous=True)
            wsrc = w_c_in.rearrange("(c p) n -> p c n", p=128)
            k = 0
            for c0 in range(0, 3088, 256):
                cw = min(256, 3088 - c0)
                b = k % 2; k += 1
                S.dma("sp", stg[b][:, :, 0:cw], wsrc[:, :, c0:c0 + cw], writes=[r_stg[b]])
                S.op("pool", lambda e, b=b, c0=c0, cw=cw: e.tensor_copy(out=wcin[:, :, c0:c0 + cw], in_=stg[b][:, :, 0:cw]),
                     [r_stg[b]], [r_wcin])
            wsrc2 = w_c_out.rearrange("(c p) n -> p c n", p=128)
            for c0 in range(0, D, 256):
                b = k % 2; k += 1
                S.dma("sp", stg[b][:], wsrc2[:, :, c0:c0 + 256], writes=[r_stg[b]])
                S.op("pool", lambda e, b=b, c0=c0: e.tensor_copy(out=wcout[:, :, c0:c0 + 256], in_=stg[b][:]),
                     [r_stg[b]], [r_wcout])
            for h in range(4):
                S.op("dve", lambda e, h=h: e.memset(Sf[:, h, :], 0.0), [], [r_Sf[h]])
                S.op("dve", lambda e, h=h: e.memset(Sb[:, h, :], 0.0), [], [r_Sb[h]])

            cgroups = [(s0, W, 32, False) for s0 in range(0, T, W)] + [(T, NSAMP, 4, True)]
            cc = 0
            c_pending = [None]
            for (s, w, C, is_s) in cgroups:
                nch = w // C
                gi = grp_of_tile(s)
                S.dma("sp", xu[:, :, 0:w], xs[:, :, s:s + w].rearrange("c p t -> p c t"), reads=[r_xs[gi]], writes=[r_xu])

                def mm8(e, pst, col0, ncol, s=s, w=w):
                    ins = None
                    for c in range(DC):
                        ins = e.matmul(pst, lhsT=wcin[:, c, col0:col0 + ncol], rhs=hT[:, c, s:s + w],
                                       start=(c == 0), stop=(c == DC - 1))
                    return ins
                S.op("pe", lambda e, w=w, mm8=mm8: mm8(e, PS[2][0:16, 0:w], 3072, 16), [r_wcin, r_hT[gi]], [PSR[2]])
                S.op("act", lambda e, w=w: e.copy(out=glr[:, 0:w], in_=PS[2][0:16, 0:w]), [PSR[2]], [r_glr])
                for h in range(4):
                    S.op("pe", lambda e, w=w, h=h, mm8=mm8: mm8(e, PS[0][:, 0:w], h * 128, 128), [r_wcin, r_hT[gi]], [PSR[0]])
                    S.op("pe", lambda e, w=w, h=h, mm8=mm8: mm8(e, PS[1][:, 0:w], 512 + h * 128, 128), [r_wcin, r_hT[gi]], [PSR[1]])
                    S.op("pe", lambda e, w=w, h=h: e.matmul(PS[2][:, 0:w], lhsT=wg2[:, h * 128:(h + 1) * 128], rhs=glr[:, 0:w],
                                                            start=True, stop=True), [r_wg2, r_glr], [PSR[2]])
                    S.op("act", lambda e, w=w, h=h: e.activation(out=l1[:, 0:w], in_=PS[2][:, 0:w], func=AF.Exp,
                                                                 bias=negb[:, h:h + 1], scale=-1.0), [PSR[2], r_negb], [r_l1])
                    S.op("act", lambda e, w=w: e.activation(out=csA[:, 0:w], in_=l1[:, 0:w], func=AF.Ln, bias=1.0, scale=1.0),
                         [r_l1], [r_cs[0]])
                    bufs = [csA, csB]
                    cur = 0
                    kk = 1
                    while kk < C:
                        src = bufs[cur][:, 0:w].rearrange("p (n c) -> p n c", c=C)
                        dst = bufs[1 - cur][:, 0:w].rearrange("p (n c) -> p n c", c=C)
                        S.op("dve", lambda e, src=src, dst=dst, kk=kk: e.tensor_copy(out=dst[:, :, 0:kk], in_=src[:, :, 0:kk]),
                             [r_cs[cur]], [r_cs[1 - cur]])
                        S.op("dve", lambda e, src=src, dst=dst, kk=kk, C=C: e.tensor_tensor(
                            out=dst[:, :, kk:C], in0=src[:, :, kk:C], in1=src[:, :, 0:C - kk], op=ALU.add),
                            [r_cs[cur]], [r_cs[1 - cur]])
                        cur = 1 - cur
                        kk *= 2
                    csv = bufs[cur]; r_csv = r_cs[cur]
                    oth = bufs[1 - cur]; r_oth = r_cs[1 - cur]
                    cs3 = csv[:, 0:w].rearrange("p (n c) -> p n c", c=C)
                    S.op("act", lambda e, w=w, csv=csv: e.activation(out=eb[:, 0:w], in_=csv[:, 0:w], func=AF.Exp, scale=-1.0 / 16),
                         [r_csv], [r_e])
                    S.op("act", lambda e, w=w, csv=csv: e.activation(out=enb[:, 0:w], in_=csv[:, 0:w], func=AF.Exp, scale=1.0 / 16),
                         [r_csv], [r_e])
                    S.op("act", lambda e, cs3=cs3, nch=nch, C=C: e.activation(out=dec[:, 0:nch], in_=cs3[:, :, C - 1], func=AF.Exp,
                                                                           scale=-1.0 / 16), [r_csv], [r_dec])
                    S.op("dve", lambda e, cs3=cs3, oth=oth, w=w, C=C, nch=nch: e.tensor_tensor(
                        out=oth[:, 0:w].rearrange("p (n c) -> p n c", c=C), in0=cs3,
                        in1=cs3[:, :, C - 1:C].to_broadcast([128, nch, C]), op=ALU.subtract), [r_csv], [r_oth])
                    S.op("act", lambda e, w=w, oth=oth: e.activation(out=ekd[:, 0:w], in_=oth[:, 0:w], func=AF.Exp, scale=1.0 / 16),
                         [r_oth], [r_e])
                    S.op("dve", lambda e, w=w: e.scalar_tensor_tensor(out=qe[:, 0:w], in0=PS[0][:, 0:w], scalar=128 ** -0.5,
                                                                      in1=eb[:, 0:w], op0=ALU.mult, op1=ALU.mult),
                         [PSR[0], r_e], [r_qk])
                    S.op("dve", lambda e, w=w: e.tensor_tensor(out=ke[:, 0:w], in0=PS[1][:, 0:w], in1=enb[:, 0:w], op=ALU.mult),
                         [PSR[1], r_e], [r_qk])
                    S.op("dve", lambda e, w=w: e.tensor_tensor(out=kd[:, 0:w], in0=PS[1][:, 0:w], in1=ekd[:, 0:w], op=ALU.mult),
                         [PSR[1], r_e], [r_qk])
                    for m in range(nch):
                        pb = cc % 2; cc += 1
                        t0, t1_ = m * C, (m + 1) * C
                        if is_s:
                            if c_pending[0] is not None:
                                c_pending[0]()
                                c_pending[0] = None
                            S.dma("sp", Sf[:, h, :], sc_in[m, h], writes=[r_Sf[h]])
                            S.op("act", lambda e, h=h: e.copy(out=Sb[:, h, :], in_=Sf[:, h, :]), [r_Sf[h]], [r_Sb[h]])
                        def vproj(e, s=s, t0=t0, C=C, h=h):
                            ins = None
                            for c in range(DC):
                                ins = e.matmul(PS[3][0:C, 0:256], lhsT=hT[:, c, s + t0:s + t0 + C],
                                               rhs=wcin[:, c, 1024 + h * 256:1024 + (h + 1) * 256],
                                               start=(c == 0), stop=(c == DC - 1))
                            return ins
                        S.op("pe", vproj, [r_wcin, r_hT[gi]], [PSR[3]])
                        S.op("act", lambda e, pb=pb, C=C: e.copy(out=vt[pb][0:C, :], in_=PS[3][0:C, 0:256]), [PSR[3]], [r_vt[pb]])
                        pst = PS[4][:, 0:64].bitcast(BF16)
                        S.op("pe", lambda e, t0=t0, t1_=t1_, C=C, pst=pst: e.transpose(pst[0:C, 0:128], kd[:, t0:t1_], identb[:]),
                             [r_qk, r_identb], [PSR[4]])
                        S.op("dve", lambda e, pb=pb, C=C, pst=pst: e.tensor_copy(out=kdt[pb][0:C, :], in_=pst[0:C, 0:128]),
                             [PSR[4]], [r_kdt[pb]])
                        S.op("pe", lambda e, t0=t0, t1_=t1_, C=C: e.matmul(PS[5][0:C, 0:C], lhsT=ke[:, t0:t1_], rhs=qe[:, t0:t1_],
                                                                          start=True, stop=True), [r_qk], [PSR[5]])
                        S.op("dve", lambda e, pb=pb, C=C: e.tensor_tensor(out=att[pb][0:C, 0:C], in0=PS[5][0:C, 0:C],
                                                                          in1=triu[0:C, 0:C], op=ALU.mult),
                             [PSR[5], r_triu], [r_att[pb]])
                        if c_pending[0] is not None:
                            c_pending[0]()
                            c_pending[0] = None
                        def c_tail(pb=pb, C=C, h=h, t0=t0, t1_=t1_, m=m, is_s=is_s):
                            def omm(e, pb=pb, C=C, h=h, t0=t0, t1_=t1_):
                                ins = None
                                for ee in range(2):
                                    e.matmul(PS[6][:, ee * C:(ee + 1) * C], lhsT=vt[pb][0:C, ee * 128:(ee + 1) * 128],
                                             rhs=att[pb][0:C, 0:C], start=True, stop=False)
                                    ins = e.matmul(PS[6][:, ee * C:(ee + 1) * C], lhsT=Sb[:, h, ee * 128:(ee + 1) * 128],
                                                   rhs=qe[:, t0:t1_], start=False, stop=True)
                                return ins
                            S.op("pe", omm, [r_vt[pb], r_att[pb], r_Sb[h], r_qk], [PSR[6]])
                            S.op("act", lambda e, h=h, C=C, t0=t0, t1_=t1_: e.copy(
                                out=oT[:, 2 * h:2 * h + 2, t0:t1_], in_=PS[6][:, 0:2 * C].rearrange("p (a c) -> p a c", a=2)),
                                [PSR[6]], [r_oT])
                            S.op("pe", lambda e, pb=pb, C=C: e.matmul(PS[7][:, 0:256], lhsT=kdt[pb][0:C, :], rhs=vt[pb][0:C, :],
                                                                      start=True, stop=True), [r_kdt[pb], r_vt[pb]], [PSR[7]])
                            S.op("dve", lambda e, h=h, m=m: e.scalar_tensor_tensor(
                                out=Sf[:, h, :], in0=Sf[:, h, :], scalar=dec[:, m:m + 1], in1=PS[7][:, 0:256],
                                op0=ALU.mult, op1=ALU.add), [r_Sf[h], r_dec, PSR[7]], [r_Sf[h]])
                            if is_s:
                                S.dma("sp", css[m, h], Sf[:, h, :], reads=[r_Sf[h]])
                            else:
                                S.op("act", lambda e, h=h: e.copy(out=Sb[:, h, :], in_=Sf[:, h, :]), [r_Sf[h]], [r_Sb[h]])
                        c_pending[0] = c_tail
                    if c_pending[0] is not None:
                        c_pending[0]()
                        c_pending[0] = None
                S.op("act", lambda e, w=w: e.activation(out=osq[:, :, 0:w], in_=oT[:, :, 0:w], func=AF.Square), [r_oT], [r_osq])
                for h in range(4):
                    def nmm(e, h=h, w=w):
                        e.matmul(PS[2][:, 0:w], lhsT=ones_bf[:], rhs=osq[:, 2 * h, 0:w], start=True, stop=False)
                        return e.matmul(PS[2][:, 0:w], lhsT=ones_bf[:], rhs=osq[:, 2 * h + 1, 0:w], start=False, stop=True)
                    S.op("pe", nmm, [r_osq, r_ones], [PSR[2]])
                    S.op("act", lambda e, w=w: e.activation(out=rstd[:, 0:w], in_=PS[2][:, 0:w], func=AF.Ln, bias=epsc[:],
                                                            scale=1.0 / 256), [PSR[2], r_eps], [r_rstd])
                    S.op("act", lambda e, w=w: e.activation(out=rstd[:, 0:w], in_=rstd[:, 0:w], func=AF.Exp, scale=-0.5),
                         [r_rstd], [r_rstd])
                    for ee in range(2):
                        ch = 2 * h + ee
                        pb = ch % 2
                        S.op("pe", lambda e, w=w, ch=ch, pb=pb, mm8=mm8: mm8(e, PS[pb][:, 0:w], 2048 + ch * 128, 128),
                             [r_wcin, r_hT[gi]], [PSR[pb]])
                        S.op("act", lambda e, w=w, pb=pb: e.activation(out=sgr[:, 0:w], in_=PS[pb][:, 0:w], func=AF.Silu),
                             [PSR[pb]], [r_sgr])
                        S.op("dve", lambda e, w=w, ch=ch, ee=ee: e.scalar_tensor_tensor(
                            out=t1[:, 0:w], in0=oT[:, ch, 0:w], scalar=cng[:, ee:ee + 1], in1=rstd[:, 0:w],
                            op0=ALU.mult, op1=ALU.mult), [r_oT, r_cng, r_rstd], [r_t1])
                        S.op("dve", lambda e, w=w, ch=ch: e.tensor_tensor(out=o2[:, ch, 0:w], in0=t1[:, 0:w], in1=sgr[:, 0:w],
                                                                          op=ALU.mult), [r_t1, r_sgr], [r_o2])
                for dc in range(DC):
                    pb = dc % 2

                    def ymm(e, dc=dc, pb=pb, w=w):
                        ins = None
                        for ch in range(8):
                            ins = e.matmul(PS[pb][:, 0:w], lhsT=wcout[:, ch, dc * 128:(dc + 1) * 128], rhs=o2[:, ch, 0:w],
                                           start=(ch == 0), stop=(ch == 7))
                        return ins
                    S.op("pe", ymm, [r_wcout, r_o2], [PSR[pb]])
                    S.op("dve", lambda e, dc=dc, pb=pb, w=w: e.tensor_tensor(out=xu[:, dc, 0:w], in0=PS[pb][:, 0:w],
                                                                              in1=xu[:, dc, 0:w], op=ALU.add),
                         [PSR[pb], r_xu], [r_xu])
                S.dma("sp", xs[:, :, s:s + w].rearrange("c p t -> p c t"), xu[:, :, 0:w], reads=[r_xu], writes=[r_xs[gi]])
                if (not is_s) and s + w == T:
                    for h in range(4):
                        S.dma("sp", csp[h], Sf[:, h, :], reads=[r_Sf[h]])
            S.barrier()

        def mixer_a(l):
            norm_pass(l * 3 + 1)
            S.barrier()
            NT = len(tiles)
            DIL = (1, 4, 16)
            WIN = tuple(min(x_, T) for x_ in (128, 512, 2048))
            cv = Carver()
            wa_st = [cv.take([128, DC, 256]) for _ in range(2)]; r_wa_st = [R("wa_st0"), R("wa_st1")]
            wa_bf = [cv.take([128, DC, 512], BF16) for _ in range(2)]; r_wa_bf = [R("wa_bf0"), R("wa_bf1")]
            sqv = cv.take([128, 512]); r_sqv = R("sqv")
            xn = [cv.take([128, 512]) for _ in range(2)]; r_xn = [R("xn0"), R("xn1")]
            xb = [cv.take([128, 512], BF16) for _ in range(2)]; r_xb = [R("xb0"), R("xb1")]
            rtmp = cv.take([128, 4, 64]); r_rtmp = R("rtmp")
            xT = [cv.take([128, 4, 128], BF16) for _ in range(2)]; r_xT = [R("xT0"), R("xT1")]
            ss = cv.take([128, 8]); r_ss = R("ss")
            grep_ = cv.take([128, 2, 64]); r_grep = R("grep")
            cosT = cv.take([128, NT, 8]); sinT = cv.take([128, NT, 8]); r_cs = R("cossin")
            S.dma("sp", grep_[:], a_qk_gain.rearrange("j e -> (j e)").partition_broadcast(128)
                  .rearrange("p (j e) -> p j e", j=2), writes=[r_grep])
            S.dma("sp", cosT[:], rope_cos.rearrange("(n p) i -> p n i", p=128), writes=[r_cs])
            S.dma("sp", sinT[:], rope_sin.rearrange("(n p) i -> p n i", p=128), writes=[r_cs])
            r_qT = R("qTscr"); r_kT = R("kTscr"); r_vtok = R("vtokscr"); r_qs = R("qsscr")
            wsrc = w_a_in.rearrange("(c p) n -> p c n", p=128)
            blk = 0
            tcnt = 0
            pj_pending = [None]
            for g in range(3):
                for j in range(3):
                    wb = blk % 2; blk += 1
                    col0 = g * 1536 + j * 512
                    for hf in range(2):
                        S.dma("sp", wa_st[hf][:], wsrc[:, :, col0 + hf * 256:col0 + (hf + 1) * 256], writes=[r_wa_st[hf]])
                        S.op("pool", lambda e, wb=wb, hf=hf: e.tensor_copy(out=wa_bf[wb][:, :, hf * 256:(hf + 1) * 256],
                                                                          in_=wa_st[hf][:]), [r_wa_st[hf]], [r_wa_bf[wb]])
                    for ti, (s, w) in enumerate(tiles):
                        gi = grp_of_tile(s)
                        pb = tcnt % 2; tcnt += 1
                        is_s = s >= T
                        if is_s and 'nosample' in dbg:
                            continue
                        if j != 2 and 'noqk' in dbg:
                            continue

                        def pj(e, s=s, w=w, wb=wb, pb=pb):
                            ins = None
                            for c in range(DC):
                                ins = e.matmul(PS[pb][0:w, :], lhsT=hT[:, c, s:s + w], rhs=wa_bf[wb][:, c, :],
                                               start=(c == 0), stop=(c == DC - 1))
                            return ins
                        S.op("pe", pj, [r_hT[gi], r_wa_bf[wb]], [PSR[pb]])
                        if pj_pending[0] is not None:
                            pj_pending[0]()
                            pj_pending[0] = None
                        lo = max(s, T - WIN[g])
                        if j == 2 and 'nov' in dbg:
                            continue
                        if j == 2:
                            S.op("act", lambda e, pb=pb, w=w: e.copy(out=xn[pb][0:w, :], in_=PS[pb][0:w, :]), [PSR[pb]], [r_xn[pb]])
                            S.op("dve", lambda e, pb=pb, w=w: e.tensor_copy(out=xb[pb][0:w, :], in_=xn[pb][0:w, :]), [r_xn[pb]], [r_xb[pb]])
                            if 'novtok' not in dbg:
                                S.dma("sp", vtok_scr[g, s:s + w, :], xb[pb][0:w, :], reads=[r_xb[pb]], writes=[r_vtok])
                            if is_s:
                                S.dma("sp", aws[g][:, 1, :], xn[pb][0:w, :], reads=[r_xn[pb]])
                                S.dma("sp", kvs_scr[g, 1], xn[pb][0:w, :], reads=[r_xn[pb]], writes=[r_qs])
                            elif lo < s + w and 'noawp' not in dbg:
                                S.dma("sp", awp[g][lo - (T - WIN[g]):s + w - (T - WIN[g]), 1, :], xn[pb][lo - s:w, :], reads=[r_xn[pb]])
                            continue
                        S.op("act", lambda e, pb=pb, w=w: e.copy(out=xn[pb][0:w, :], in_=PS[pb][0:w, :]), [PSR[pb]], [r_xn[pb]])
                        S.op("act", lambda e, pb=pb, w=w: e.activation(out=sqv[0:w, :], in_=PS[pb][0:w, :], func=AF.Square),
                             [PSR[pb]], [r_sqv])
                        S.op("dve", lambda e, w=w: e.tensor_reduce(out=ss[0:w, :], in_=sqv[0:w, :].rearrange("p (h e) -> p h e", h=8),
                                                                   axis=AX.X, op=ALU.add), [r_sqv], [r_ss])
                        S.op("act", lambda e, w=w: e.activation(out=ss[0:w, :], in_=ss[0:w, :], func=AF.Ln, bias=epsc[0:w, :],
                                                                scale=1.0 / 64), [r_ss, r_eps], [r_ss])
                        S.op("act", lambda e, w=w: e.activation(out=ss[0:w, :], in_=ss[0:w, :], func=AF.Exp, scale=-0.5),
                             [r_ss], [r_ss])
                        x3 = xn[pb][0:w, :].rearrange("p (h e) -> p h e", h=8)
                        S.op("dve", lambda e, pb=pb, w=w, x3=x3: e.tensor_tensor(
                            out=x3, in0=x3,
                            in1=ss[0:w, :].unsqueeze(2).to_broadcast([w, 8, 64]), op=ALU.mult), [r_xn[pb], r_ss], [r_xn[pb]])
                        S.op("dve", lambda e, w=w, x3=x3, j=j: e.tensor_tensor(
                            out=x3, in0=x3, in1=grep_[0:w, j:j + 1, :].to_broadcast([w, 8, 64]), op=ALU.mult),
                            [r_xn[pb], r_grep], [r_xn[pb]])
                        cb = cosT[0:w, ti:ti + 1, :].to_broadcast([w, 8, 8])
                        sb_ = sinT[0:w, ti:ti + 1, :].to_broadcast([w, 8, 8])
                        rt = rtmp[0:w, :, :].rearrange("p a (h i) -> p a h i", h=8)
                        S.op("dve", lambda e, x3=x3, rt=rt, cb=cb: e.tensor_tensor(out=rt[:, 0], in0=x3[:, :, 0:8], in1=cb, op=ALU.mult),
                             [r_xn[pb], r_cs], [r_rtmp])
                        S.op("dve", lambda e, x3=x3, rt=rt, sb_=sb_: e.tensor_tensor(out=rt[:, 1], in0=x3[:, :, 8:16], in1=sb_, op=ALU.mult),
                             [r_xn[pb], r_cs], [r_rtmp])
                        S.op("dve", lambda e, x3=x3, rt=rt, cb=cb: e.tensor_tensor(out=rt[:, 2], in0=x3[:, :, 8:16], in1=cb, op=ALU.mult),
                             [r_xn[pb], r_cs], [r_rtmp])
                        S.op("dve", lambda e, x3=x3, rt=rt, sb_=sb_: e.tensor_tensor(out=rt[:, 3], in0=x3[:, :, 0:8], in1=sb_, op=ALU.mult),
                             [r_xn[pb], r_cs], [r_rtmp])
                        S.op("dve", lambda e, x3=x3, rt=rt: e.tensor_tensor(out=x3[:, :, 0:8], in0=rt[:, 0], in1=rt[:, 1], op=ALU.subtract),
                             [r_rtmp], [r_xn[pb]])
                        S.op("dve", lambda e, x3=x3, rt=rt: e.tensor_tensor(out=x3[:, :, 8:16], in0=rt[:, 2], in1=rt[:, 3], op=ALU.add),
                             [r_rtmp], [r_xn[pb]])
                        if j == 1:
                            if is_s:
                                S.dma("sp", aws[g][:, 0, :], xn[pb][0:w, :], reads=[r_xn[pb]])
                            elif lo < s + w:
                                S.dma("sp", awp[g][lo - (T - WIN[g]):s + w - (T - WIN[g]), 0, :], xn[pb][lo - s:w, :], reads=[r_xn[pb]])
                        if is_s:
                            dsts = qs_scr[g] if j == 0 else kvs_scr[g, 0]
                            S.dma("sp", dsts, xn[pb][0:w, :], reads=[r_xn[pb]], writes=[r_qs])
                            continue
                        if 'notr' in dbg:
                            continue
                        def a_tail(pb=pb, w=w, s=s, g=g, j=j):
                            S.op("act", lambda e, pb=pb, w=w: e.copy(out=xb[pb][0:w, :], in_=xn[pb][0:w, :]), [r_xn[pb]], [r_xb[pb]])
                            pst = PS[2 + pb][:, 0:256].bitcast(BF16)

                            def trq(e, pb=pb, w=w, pst=pst):
                                ins = None
                                for hc in range(4):
                                    ins = e.transpose(pst[:, hc * 128:hc * 128 + w], xb[pb][0:w, hc * 128:(hc + 1) * 128], identb[0:w, 0:w])
                                return ins
                            S.op("pe", trq, [r_xb[pb], r_identb], [PSR[2 + pb]])
                            S.op("dve", lambda e, pb=pb, w=w, pst=pst: e.tensor_copy(
                                out=xT[pb][:, :, 0:w], in_=pst.rearrange("p (a t) -> p a t", a=4)[:, :, 0:w]), [PSR[2 + pb]], [r_xT[pb]])
                            dst = (qT_scr if j == 0 else kT_scr)[g]
                            S.dma("sp", dst[:, :, s:s + w].rearrange("a p t -> p a t"), xT[pb][:, :, 0:w], reads=[r_xT[pb]],
                                  writes=[r_qT if j == 0 else r_kT])
                        pj_pending[0] = a_tail
            if pj_pending[0] is not None:
                pj_pending[0]()
                pj_pending[0] = None
            S.barrier()
            if dbg_stop <= 1:
                return
            cv = Carver()
            oT = cv.take([128, 4, TT], BF16); r_oT = R("a_oT")
            OD = cv.take([128, 2, T]); r_OD = R("OD")
            qTs = cv.take([128, T], BF16); kTs = cv.take([128, T], BF16); r_qk = R("a_qk")
            vres = cv.take([128, T // 128, 128], BF16); r_vres = R("vres")
            pf = [cv.take([128, 256]) for _ in range(2)]; r_pf = [R("pf0"), R("pf1")]
            pbf = [cv.take([128, 256], BF16) for _ in range(2)]; r_pbf = [R("pbf0"), R("pbf1")]
            mask2 = cv.take([128, 256]); r_mask2 = R("mask2")
            S.dma("sp", mask2[:], mask2_d, writes=[r_mask2])
            it = 0
            a_pending = [None]
            for hc in range(4):
                S.op("dve", lambda e: e.memset(OD[:], 0.0), [], [r_OD])
                for g in range(3):
                    d = DIL[g]
                    n = T // d
                    nb = n // 128
                    S.dma("sp", qTs[:], qT_scr[g, hc, :, 0:T], reads=[r_qT], writes=[r_qk])
                    S.dma("sp", kTs[:], kT_scr[g, hc, :, 0:T], reads=[r_kT], writes=[r_qk])
                    for r_ in range(d):
                        S.dma("sp", vres[:, r_ * nb:(r_ + 1) * nb, :],
                              vtok_scr[g, r_:T:d, hc * 128:(hc + 1) * 128].rearrange("(kb i) f -> i kb f", i=128),
                              reads=[r_vtok], writes=[r_vres])
                    for hh in range(2):
                        bp = 64 * hh
                        for r_ in range(d):
                            for qb in range(nb):
                                pb = it % 2; it += 1
                                c0 = r_ + d * 128 * qb
                                qcols = qTs[bp:bp + 64, c0:c0 + 127 * d + 1:d]
                                has_prev = qb > 0

                                def smm(e, pb=pb, bp=bp, c0=c0, d=d, qcols=qcols, has_prev=has_prev):
                                    ins = None
                                    if has_prev:
                                        p0 = c0 - 128 * d
                                        ins = e.matmul(PS[pb][:, 0:128], lhsT=kTs[bp:bp + 64, p0:p0 + 127 * d + 1:d], rhs=qcols,
                                                       start=True, stop=True)
                                    ins = e.matmul(PS[pb][:, 128:256], lhsT=kTs[bp:bp + 64, c0:c0 + 127 * d + 1:d], rhs=qcols,
                                                   start=True, stop=True)
                                    return ins
                                S.op("pe", smm, [r_qk], [PSR[pb]])
                                if a_pending[0] is not None:
                                    a_pending[0]()
                                lo_c = 0 if has_prev else 128
                                S.op("act", lambda e, pb=pb, lo_c=lo_c: e.activation(out=pf[pb][:, lo_c:256], in_=PS[pb][:, lo_c:256],
                                                                                     func=AF.Exp, scale=0.125), [PSR[pb]], [r_pf[pb]])
                                S.op("dve", lambda e, pb=pb, lo_c=lo_c: e.tensor_tensor(out=pbf[pb][:, lo_c:256], in0=pf[pb][:, lo_c:256],
                                                                                        in1=mask2[:, lo_c:256], op=ALU.mult),
                                     [r_pf[pb], r_mask2], [r_pbf[pb]])

                                def finish(pb=pb, r_=r_, qb=qb, nb=nb, has_prev=has_prev, bp=bp, c0=c0, d=d):
                                    def pvm(e):
                                        ins = None
                                        halves = ([0] if has_prev else []) + [1]
                                        for k_, hv in enumerate(halves):
                                            kb = qb - 1 + hv
                                            e.matmul(PS[2 + pb][:, 0:128], lhsT=vres[:, r_ * nb + kb, :], rhs=pbf[pb][:, hv * 128:(hv + 1) * 128],
                                                     start=(k_ == 0), stop=(k_ == len(halves) - 1))
                                        for k_, hv in enumerate(halves):
                                            ins = e.matmul(PS[2 + pb][:, 128:256], lhsT=ones_bf[:], rhs=pbf[pb][:, hv * 128:(hv + 1) * 128],
                                                           start=(k_ == 0), stop=(k_ == len(halves) - 1))
                                        return ins
                                    S.op("pe", pvm, [r_vres, r_pbf[pb], r_ones], [PSR[2 + pb]])
                                    odv = OD[bp:bp + 64, :, c0:c0 + 127 * d + 1:d]
                                    S.op("dve", lambda e: e.tensor_tensor(
                                        out=odv, in0=PS[2 + pb][bp:bp + 64, 0:256].rearrange("p (a t) -> p a t", a=2), in1=odv, op=ALU.add),
                                        [PSR[2 + pb], r_OD], [r_OD])
                                    a_pending[0] = None
                                a_pending[0] = finish
                    if a_pending[0] is not None:
                        a_pending[0]()
                S.op("dve", lambda e: e.reciprocal(out=OD[:, 1, :], in_=OD[:, 1, :]), [r_OD], [r_OD])
                S.op("dve", lambda e, hc=hc: e.tensor_tensor(out=oT[:, hc, 0:T], in0=OD[:, 0, :], in1=OD[:, 1, :], op=ALU.mult),
                     [r_OD], [r_oT])
            S.barrier()
            if dbg_stop <= 2:
                return
            cvs = Carver()
            _keep = cvs.take([128, 4, TT], BF16)
            Kt = cvs.take([128, 132, 64]); r_Kt = R("Kt")
            Vt = cvs.take([128, 132, 64]); r_Vt = R("Vt")
            qv = cvs.take([128, 3, 4, 64]); r_qv = R("qv")
            knew = cvs.take([128, 3, 2, 4, 64]); r_knew = R("knew")
            sc = cvs.take([128, 132]); r_sc = R("sc")
            den = cvs.take([128, 4, 4]); r_den = R("den")
            oacc = cvs.take([128, 4, 4, 64]); r_oacc = R("oacc")
            osum = cvs.take([128, 4, 64]); r_osum = R("osum")
            otok = cvs.take([64, 512]); r_otok = R("otok")
            otb = cvs.take([64, 512], BF16); r_otb = R("otb")
            for g in range(3):
                for b in range(NB):
                    S.dma("sp", qv[b * 8:(b + 1) * 8, g], qs_scr[g][b * 4:(b + 1) * 4, :].rearrange("i (h e) -> h i e", h=8),
                          reads=[r_qs], writes=[r_qv])
                    for kv in range(2):
                        S.dma("sp", knew[b * 8:(b + 1) * 8, g, kv],
                              kvs_scr[g, kv][b * 4:(b + 1) * 4, :].rearrange("i (h e) -> h i e", h=8), reads=[r_qs], writes=[r_knew])
            caches = (ca1, ca2, ca3)
            for i in range(4):
                for g in range(3):
                    d = DIL[g]
                    if g == 0:
                        nsl = 129
                        if i == 0:
                            for b in range(NB):
                                S.dma("sp", Kt[b * 8:(b + 1) * 8, 0:128, :], ca1[b, :, 0].rearrange("r h e -> h r e"), writes=[r_Kt])
                                S.dma("sp", Vt[b * 8:(b + 1) * 8, 0:128, :], ca1[b, :, 1].rearrange("r h e -> h r e"), writes=[r_Vt])
                            S.op("dve", lambda e: e.tensor_copy(out=Kt[:, 128:132, :], in_=knew[:, 0, 0]), [r_knew], [r_Kt])
                            S.op("dve", lambda e: e.tensor_copy(out=Vt[:, 128:132, :], in_=knew[:, 0, 1]), [r_knew], [r_Vt])
                        else:
                            for b in range(NB):
                                S.dma("sp", Kt[b * 8:(b + 1) * 8, 0:128, :], ca1[b, :, 0].rearrange("r h e -> h r e"), writes=[r_Kt])
                                S.dma("sp", Vt[b * 8:(b + 1) * 8, 0:128, :], ca1[b, :, 1].rearrange("r h e -> h r e"), writes=[r_Vt])
                            S.op("dve", lambda e: e.tensor_copy(out=Kt[:, 128:132, :], in_=knew[:, 0, 0]), [r_knew], [r_Kt])
                            S.op("dve", lambda e: e.tensor_copy(out=Vt[:, 128:132, :], in_=knew[:, 0, 1]), [r_knew], [r_Vt])
                        k0 = i
                    else:
                        nsl = 129
                        cch = caches[g]
                        for b in range(NB):
                            S.dma("sp", Kt[b * 8:(b + 1) * 8, 0:128, :], cch[b, i::d, 0].rearrange("r h e -> h r e"), writes=[r_Kt])
                            S.dma("sp", Vt[b * 8:(b + 1) * 8, 0:128, :], cch[b, i::d, 1].rearrange("r h e -> h r e"), writes=[r_Vt])
                        S.op("dve", lambda e, g=g, i=i: e.tensor_copy(out=Kt[:, 128, :], in_=knew[:, g, 0, i]), [r_knew], [r_Kt])
                        S.op("dve", lambda e, g=g, i=i: e.tensor_copy(out=Vt[:, 128, :], in_=knew[:, g, 1, i]), [r_knew], [r_Vt])
                        k0 = 0
                    Kw = Kt[:, k0:k0 + nsl, :]
                    Vw = Vt[:, k0:k0 + nsl, :]
                    S.op("dve", lambda e, Kw=Kw, g=g, i=i, nsl=nsl: e.tensor_tensor(
                        out=Kw, in0=Kw, in1=qv[:, g, i:i + 1, :].to_broadcast([128, nsl, 64]), op=ALU.mult), [r_Kt, r_qv], [r_Kt])
                    S.op("dve", lambda e, Kw=Kw, nsl=nsl: e.tensor_reduce(out=sc[:, 0:nsl], in_=Kw, axis=AX.X, op=ALU.add), [r_Kt], [r_sc])
                    S.op("act", lambda e, nsl=nsl: e.activation(out=sc[:, 0:nsl], in_=sc[:, 0:nsl], func=AF.Exp, scale=0.125), [r_sc], [r_sc])
                    S.op("dve", lambda e, nsl=nsl, g=g, i=i: e.tensor_reduce(out=den[:, i, g:g + 1], in_=sc[:, 0:nsl].unsqueeze(1), axis=AX.X, op=ALU.add),
                         [r_sc], [r_den])
                    S.op("dve", lambda e, Vw=Vw, nsl=nsl: e.tensor_tensor(
                        out=Vw, in0=Vw, in1=sc[:, 0:nsl].unsqueeze(2).to_broadcast([128, nsl, 64]), op=ALU.mult), [r_Vt, r_sc], [r_Vt])
                    S.op("dve", lambda e, Vw=Vw, g=g, i=i: e.tensor_reduce(out=oacc[:, i, g, :], in_=Vw.rearrange("p s e -> p e s"),
                                                                           axis=AX.X, op=ALU.add), [r_Vt], [r_oacc])
            S.op("dve", lambda e: e.tensor_reduce(out=den[:, :, 3], in_=den[:, :, 0:3], axis=AX.X, op=ALU.add), [r_den], [r_den])
            S.op("dve", lambda e: e.reciprocal(out=den[:, :, 3:4], in_=den[:, :, 3:4]), [r_den], [r_den])
            S.op("dve", lambda e: e.tensor_reduce(out=osum[:], in_=oacc[:].rearrange("p i g e -> p i e g"), axis=AX.X, op=ALU.add),
                 [r_oacc], [r_osum])
            S.op("dve", lambda e: e.tensor_tensor(out=osum[:], in0=osum[:], in1=den[:, :, 3:4].to_broadcast([128, 4, 64]), op=ALU.mult),
                 [r_osum, r_den], [r_osum])
            r_os = R("os_scr")
            S.dma("sp", os_scr.rearrange("b h i e -> (b h) i e"), osum[:], reads=[r_osum], writes=[r_os])
            for b in range(NB):
                S.dma("sp", otok[b * 4:(b + 1) * 4, :].rearrange("i (h e) -> i h e", h=8), os_scr[b].rearrange("h i e -> i h e"),
                      reads=[r_os], writes=[r_otok])
            S.op("act", lambda e: e.copy(out=otb[:], in_=otok[:]), [r_otok], [r_otb])
            pst = PS[4][:, 0:256].bitcast(BF16)

            def tro(e, pst=pst):
                ins = None
                for hc in range(4):
                    ins = e.transpose(pst[:, hc * 64:(hc + 1) * 64], otb[:, hc * 128:(hc + 1) * 128], identb[0:64, 0:64])
                return ins
            S.op("pe", tro, [r_otb, r_identb], [PSR[4]])
            S.op("dve", lambda e, pst=pst: e.tensor_copy(out=oT[:, :, T:TT], in_=pst[:, 0:256].rearrange("p (a t) -> p a t", a=4)),
                 [PSR[4]], [r_oT])
            if dbg_stop <= 3:
                return
            S.barrier()
            cvs = Carver()
            _keep = cvs.take([128, 4, TT], BF16)
            waout_b = cvs.take([128, 4, D], BF16); r_wb2 = R("waout_b")
            wo_st2 = cvs.take([128, 4, D // 2]); r_wo2 = R("wo_st2")
            xu = cvs.take([128, DC, 512]); r_xu = R("a_xu")
            for hf in range(2):
                S.dma("sp", wo_st2[:], w_a_out.rearrange("(a p) n -> p a n", p=128)[:, :, hf * 512:(hf + 1) * 512], writes=[r_wo2])
                S.op("pool", lambda e, hf=hf: e.tensor_copy(out=waout_b[:, :, hf * 512:(hf + 1) * 512], in_=wo_st2[:]),
                     [r_wo2], [r_wb2])
            for gi, (s, w) in enumerate(groups):
                S.dma("sp", xu[:, :, 0:w], xs[:, :, s:s + w].rearrange("c p t -> p c t"), reads=[r_xs[gi]], writes=[r_xu])
                for dc in range(DC):
                    pb = dc % 2

                    def ymm(e, dc=dc, pb=pb, s=s, w=w):
                        ins = None
                        for hc in range(4):
                            ins = e.matmul(PS[pb][:, 0:w], lhsT=waout_b[:, hc, dc * 128:(dc + 1) * 128], rhs=oT[:, hc, s:s + w],
                                           start=(hc == 0), stop=(hc == 3))
                        return ins
                    S.op("pe", ymm, [r_wb2, r_oT], [PSR[pb]])
                    S.op("dve", lambda e, dc=dc, pb=pb, w=w: e.tensor_tensor(out=xu[:, dc, 0:w], in0=PS[pb][:, 0:w], in1=xu[:, dc, 0:w],
                                                                              op=ALU.add), [PSR[pb], r_xu], [r_xu])
                S.dma("sp", xs[:, :, s:s + w].rearrange("c p t -> p c t"), xu[:, :, 0:w], reads=[r_xu], writes=[r_xs[gi]])
            S.barrier()

        def mixer_d(l):
            norm_pass(l * 3 + 1)
            S.barrier()
            NG = len(groups)
            cv = Carver()
            wst_ = [cv.take([128, DC, 256]) for _ in range(2)]; r_wst_ = [R("d_wst0"), R("d_wst1")]
            wbf_ = [cv.take([128, DC, 256], BF16) for _ in range(2)]; r_wbf_ = [R("d_wbf0"), R("d_wbf1")]
            cw = cv.take([128, 24, 4]); cb = cv.take([128, 24]); r_cw = R("cw")
            carry = cv.take([128, 24, 3]); r_carry = R("carry")
            cbT = cv.take([128, 24, NB * 3]); r_cbT = R("cbT")
            cst = cv.take([NB * 3, 3072]); r_cst = R("cst")
            rawS = cv.take([128, 24, NSAMP]); r_rawS = R("rawS")
            buf = [cv.take([128, 3 + 512]) for _ in range(2)]; r_buf = [R("buf0"), R("buf1")]
            bufs_ = cv.take([128, NB, 7]); r_bufs = R("bufs")
            acc = [cv.take([128, 512]) for _ in range(2)]; r_acc = [R("acc0"), R("acc1")]
            xbT = [cv.take([128, 512], BF16) for _ in range(2)]; r_xbT = [R("xbT0"), R("xbT1")]
            xtk = [cv.take([128, 4, 128], BF16) for _ in range(2)]; r_xtk = [R("xtk0"), R("xtk1")]
            zt = [cv.take([128, 512], BF16) for _ in range(2)]; r_zt = [R("zt0"), R("zt1")]
            dtb = cv.take([128, 32]); alg = cv.take([128, 32]); r_dtc = R("dtc")
            dtt = [cv.take([128, 32]) for _ in range(2)]; dta = [cv.take([128, 32]) for _ in range(2)]; r_dtt = [R("dtt0"), R("dtt1")]
            trs = cv.take([64, 3072]); r_trs = R("trs")
            r_xbcT = R("xbcT_scr"); r_xtokd = R("xtokd_scr"); r_zs = R("zs_scr"); r_dts = R("dt_scr")
            for k_ in range(4):
                S.dma("sp", cw[:, :, k_], d_conv_w[k_].rearrange("(c p) -> p c", p=128), writes=[r_cw], allow_slow_non_contiguous=True)
            S.dma("sp", cb[:], d_conv_b.rearrange("(c p) -> p c", p=128), writes=[r_cw], allow_slow_non_contiguous=True)
            S.dma("sp", dtb[:], d_dt_bias.partition_broadcast(128), writes=[r_dtc])
            S.dma("sp", alg[:], d_a_log.partition_broadcast(128), writes=[r_dtc])
            S.op("act", lambda e: e.activation(out=alg[:], in_=alg[:], func=AF.Exp), [r_dtc], [r_dtc])
            S.op("dve", lambda e: e.tensor_scalar(out=alg[:], in0=alg[:], scalar1=-1.0, scalar2=None, op0=ALU.mult), [r_dtc], [r_dtc])
            S.op("dve", lambda e: e.memset(carry[:], 0.0), [], [r_carry])
            S.dma("sp", cst[:], sd_conv.rearrange("b w f -> (b w) f"), writes=[r_cst])
            for q4 in range(6):
                def trc(e, q4=q4):
                    ins = None
                    for k_ in range(4):
                        fc = q4 * 4 + k_
                        ins = e.transpose(PS[6][:, k_ * 48:(k_ + 1) * 48], cst[:, fc * 128:(fc + 1) * 128], ident[0:48, 0:48])
                    return ins
                S.op("pe", trc, [r_cst, r_ident], [PSR[6]])
                S.op("act", lambda e, q4=q4: e.copy(out=cbT[:, q4 * 4:(q4 + 1) * 4, :],
                                                    in_=PS[6][:, 0:192].rearrange("p (a t) -> p a t", a=4)), [PSR[6]], [r_cbT])
            wsrc = w_d_in.rearrange("(c p) n -> p c n", p=128)
            wk = 0
            it = 0
            for f2 in range(12):
                wb = wk % 2; wk += 1
                c0 = 2048 + f2 * 256
                S.dma("sp", wst_[wb][:], wsrc[:, :, c0:c0 + 256], writes=[r_wst_[wb]])
                S.op("pool", lambda e, wb=wb: e.tensor_copy(out=wbf_[wb][:], in_=wst_[wb][:]), [r_wst_[wb]], [r_wbf_[wb]])
                for k2 in range(2):
                    fc = f2 * 2 + k2
                    for gi, (s, w) in enumerate(groups):
                        pb = it % 2; it += 1
                        is_s = s >= T

                        def pj(e, wb=wb, k2=k2, s=s, w=w, pb=pb):
                            ins = None
                            for c in range(DC):
                                ins = e.matmul(PS[pb][:, 0:w], lhsT=wbf_[wb][:, c, k2 * 128:(k2 + 1) * 128], rhs=hT[:, c, s:s + w],
                                               start=(c == 0), stop=(c == DC - 1))
                            return ins
                        S.op("pe", pj, [r_wbf_[wb], r_hT[gi]], [PSR[pb]])
                        if not is_s:
                            bv = buf[pb]
                            S.op("dve", lambda e, bv=bv, fc=fc: e.tensor_copy(out=bv[:, 0:3], in_=carry[:, fc, :]), [r_carry], [r_buf[pb]])
                            S.op("act", lambda e, bv=bv, pb=pb, w=w: e.copy(out=bv[:, 3:3 + w], in_=PS[pb][:, 0:w]), [PSR[pb]], [r_buf[pb]])
                            S.op("dve", lambda e, bv=bv, fc=fc, w=w: e.tensor_copy(out=carry[:, fc, :], in_=bv[:, w:w + 3]),
                                 [r_buf[pb]], [r_carry])
                            src = [bv[:, k_:k_ + w] for k_ in range(4)]
                            av = acc[pb][:, 0:w]
                        else:
                            S.op("dve", lambda e, fc=fc: e.tensor_copy(out=bufs_[:, :, 0:3],
                                                                       in_=cbT[:, fc, :].rearrange("p (b w) -> p b w", w=3)),
                                 [r_cbT], [r_bufs])
                            S.op("act", lambda e, pb=pb: e.copy(out=bufs_[:, :, 3:7], in_=PS[pb][:, 0:NSAMP].rearrange("p (b i) -> p b i", i=4)),
                                 [PSR[pb]], [r_bufs])
                            S.op("dve", lambda e, fc=fc: e.tensor_copy(out=rawS[:, fc, :].rearrange("p (b i) -> p b i", i=4),
                                                                       in_=bufs_[:, :, 3:7]), [r_bufs], [r_rawS])
                            src = [bufs_[:, :, k_:k_ + 4] for k_ in range(4)]
                            av = acc[pb][:, 0:NSAMP].rearrange("p (b i) -> p b i", i=4)
                        rb = r_bufs if is_s else r_buf[pb]
                        S.op("dve", lambda e, av=av, src=src, fc=fc: e.tensor_scalar(out=av, in0=src[0], scalar1=cw[:, fc, 0:1],
                                                                                    scalar2=cb[:, fc:fc + 1], op0=ALU.mult, op1=ALU.add),
                             [rb, r_cw], [r_acc[pb]])
                        for k_ in range(1, 4):
                            S.op("dve", lambda e, av=av, src=src, fc=fc, k_=k_: e.scalar_tensor_tensor(
                                out=av, in0=src[k_], scalar=cw[:, fc, k_:k_ + 1], in1=av, op0=ALU.mult, op1=ALU.add),
                                [rb, r_cw, r_acc[pb]], [r_acc[pb]])
                        S.op("act", lambda e, pb=pb, w=w: e.activation(out=xbT[pb][:, 0:w], in_=acc[pb][:, 0:w], func=AF.Silu),
                             [r_acc[pb]], [r_xbT[pb]])
                        if fc >= 16:
                            S.dma("sp", xbcT_scr[fc - 16, :, s:s + w], xbT[pb][:, 0:w], reads=[r_xbT[pb]], writes=[r_xbcT])
                        if fc < 20:
                            nt_ = (w + 127) // 128
                            tw = min(w, 128)
                            pst = PS[2 + pb][:, 0:256].bitcast(BF16)

                            def trx(e, pb=pb, nt_=nt_, tw=tw, pst=pst):
                                ins = None
                                for k_ in range(nt_):
                                    ins = e.transpose(pst[0:tw, k_ * 128:(k_ + 1) * 128], xbT[pb][:, k_ * 128:k_ * 128 + tw], identb[:])
                                return ins
                            S.op("pe", trx, [r_xbT[pb], r_identb], [PSR[2 + pb]])
                            S.op("dve", lambda e, pb=pb, nt_=nt_, tw=tw, pst=pst: e.tensor_copy(
                                out=xtk[pb][0:tw, 0:nt_, :], in_=pst[0:tw, 0:nt_ * 128].rearrange("p (a f) -> p a f", a=nt_)),
                                [PSR[2 + pb]], [r_xtk[pb]])
                            S.dma("sp", xtokd_scr[s:s + w, fc * 128:(fc + 1) * 128].rearrange("(a p) f -> p a f", p=tw),
                                  xtk[pb][0:tw, 0:nt_, :], reads=[r_xtk[pb]], writes=[r_xtokd])
            for q4 in range(6):
                def trp(e, q4=q4):
                    ins = None
                    for k_ in range(4):
                        fc = q4 * 4 + k_
                        ins = e.transpose(PS[6][0:3, k_ * 128:(k_ + 1) * 128], carry[:, fc, :], ident[:])
                    return ins
                S.op("pe", trp, [r_carry, r_ident], [PSR[6]])
                S.op("act", lambda e, q4=q4: e.copy(out=trs[0:3, q4 * 512:(q4 + 1) * 512], in_=PS[6][0:3, :]), [PSR[6]], [r_trs])
            S.dma("sp", dcp, trs[0:3, :], reads=[r_trs])
            for q4 in range(6):
                def trs_(e, q4=q4):
                    ins = None
                    for k_ in range(4):
                        fc = q4 * 4 + k_
                        ins = e.transpose(PS[7][0:NSAMP, k_ * 128:(k_ + 1) * 128], rawS[:, fc, :], ident[:])
                    return ins
                S.op("pe", trs_, [r_rawS, r_ident], [PSR[7]])
                S.op("act", lambda e, q4=q4: e.copy(out=trs[:, q4 * 512:(q4 + 1) * 512], in_=PS[7][0:NSAMP, :]), [PSR[7]], [r_trs])
            for b in range(NB):
                S.dma("sp", dcs[b], trs[4 * b + 1:4 * b + 4, :], reads=[r_trs])
            for zb in range(4):
                wb = wk % 2
                for hf in range(2):
                    wb = wk % 2; wk += 1
                    S.dma("sp", wst_[wb][:], wsrc[:, :, zb * 512 + hf * 256:zb * 512 + (hf + 1) * 256], writes=[r_wst_[wb]])
                    S.op("pool", lambda e, wb=wb: e.tensor_copy(out=wbf_[wb][:], in_=wst_[wb][:]), [r_wst_[wb]], [r_wbf_[wb]])
                    for ti, (s, w) in enumerate(tiles):
                        gi = grp_of_tile(s)
                        pb = it % 2; it += 1

                        def pz(e, wb=wb, s=s, w=w, pb=pb):
                            ins = None
                            for c in range(DC):
                                ins = e.matmul(PS[pb][0:w, 0:256], lhsT=hT[:, c, s:s + w], rhs=wbf_[wb][:, c, :],
                                               start=(c == 0), stop=(c == DC - 1))
                            return ins
                        S.op("pe", pz, [r_wbf_[wb], r_hT[gi]], [PSR[pb]])
                        S.op("act", lambda e, pb=pb, w=w: e.activation(out=zt[pb][0:w, 0:256], in_=PS[pb][0:w, 0:256], func=AF.Silu),
                             [PSR[pb]], [r_zt[pb]])
                        S.dma("sp", zs_scr[s:s + w, zb * 512 + hf * 256:zb * 512 + (hf + 1) * 256], zt[pb][0:w, 0:256],
                              reads=[r_zt[pb]], writes=[r_zs])
            wb = wk % 2; wk += 1
            S.dma("sp", wst_[wb][:, :, 0:32], wsrc[:, :, 5120:5152], writes=[r_wst_[wb]])
            S.op("pool", lambda e, wb=wb: e.tensor_copy(out=wbf_[wb][:, :, 0:32], in_=wst_[wb][:, :, 0:32]), [r_wst_[wb]], [r_wbf_[wb]])
            for ti, (s, w) in enumerate(tiles):
                gi = grp_of_tile(s)
                pb = it % 2; it += 1

                def pdt(e, wb=wb, s=s, w=w, pb=pb):
                    ins = None
                    for c in range(DC):
                        ins = e.matmul(PS[pb][0:w, 0:32], lhsT=hT[:, c, s:s + w], rhs=wbf_[wb][:, c, 0:32], start=(c == 0), stop=(c == DC - 1))
                    return ins
                S.op("pe", pdt, [r_wbf_[wb], r_hT[gi]], [PSR[pb]])
                S.op("dve", lambda e, pb=pb, w=w: e.tensor_tensor(out=dtt[pb][0:w, :], in0=PS[pb][0:w, 0:32], in1=dtb[0:w, :], op=ALU.add),
                     [PSR[pb], r_dtc], [r_dtt[pb]])
                S.op("act", lambda e, pb=pb, w=w: e.activation(out=dtt[pb][0:w, :], in_=dtt[pb][0:w, :], func=AF.Exp), [r_dtt[pb]], [r_dtt[pb]])
                S.op("act", lambda e, pb=pb, w=w: e.activation(out=dtt[pb][0:w, :], in_=dtt[pb][0:w, :], func=AF.Ln, bias=1.0, scale=1.0),
                     [r_dtt[pb]], [r_dtt[pb]])
                S.op("dve", lambda e, pb=pb, w=w: e.tensor_tensor(out=dta[pb][0:w, :], in0=dtt[pb][0:w, :], in1=alg[0:w, :], op=ALU.mult),
                     [r_dtt[pb], r_dtc], [r_dtt[pb]])
                S.dma("sp", dt_scr[0, s:s + w, :], dtt[pb][0:w, :], reads=[r_dtt[pb]], writes=[r_dts])
                S.dma("sp", dt_scr[1, s:s + w, :], dta[pb][0:w, :], reads=[r_dtt[pb]], writes=[r_dts])
            S.barrier()
            if dbg_stop <= 1:
                return
            cv = Carver()
            wdo = cv.take([128, 16, D], BF16); r_wdo = R("wdo")
            GW = 256
            gT = cv.take([128, 16, GW], BF16); r_gT = R("gT")
            xu = cv.take([128, DC, GW]); r_xu = R("d_xu")
            xt_ = cv.take([64, 32, 64], BF16); r_xt = R("d_xt")
            Bt = cv.take([64, 4, 128], BF16); r_Bt = R("d_Bt")
            BCT = cv.take([128, 8, 64], BF16); r_BCT = R("d_BCT")
            zt2 = cv.take([64, 2048], BF16); r_zt2 = R("d_zt2")
            yt = cv.take([64, 2048]); r_yt = R("d_yt")
            gb = cv.take([64, 2048], BF16); r_gb = R("d_gb")
            xd = cv.take([64, 32, 64], BF16); r_xd = R("d_xd")
            Rm = cv.take([64, 8, 64]); r_Rm = R("d_R")
            Em = cv.take([64, 8, 64]); r_Em = R("d_E")
            Lw = cv.take([64, 8, 64], BF16); r_Lw = R("d_Lw")
            attm = cv.take([64, 64]); r_attm = R("d_attm")
            tmpy = cv.take([64, 512]); r_tmpy = R("d_tmpy")
            hS = cv.take([64, 32, 128]); r_hS = [R(f"hS{g}") for g in range(4)]
            hb = cv.take([64, 8, 128], BF16); r_hb = R("d_hb")
            hTb = cv.take([128, 32, 64], BF16); r_hTb = [R(f"hTb{g}") for g in range(4)]
            L1 = cv.take([64, 64]); L2 = cv.take([64, 64]); on64 = cv.take([64, 64]); r_Lc = R("d_Lc")
            Dm = cv.take([64, 32, 64], BF16); r_Dm = R("d_Dm")
            dsk = cv.take([64, 32]); r_dsk = R("d_dsk")
            ngr = cv.take([64, 2048], BF16); ngs = cv.take([64, 2048]); r_ngr = R("d_ngr")
            dtc = cv.take([64, 32]); dac = cv.take([64, 32]); r_dtch = R("d_dtch")
            cum = cv.take([64, 32]); ecum = cv.take([64, 32]); wdv = cv.take([64, 32]); decr = cv.take([64, 32]); r_cum = R("d_cum")
            ssq = cv.take([64, 4]); r_ssq = R("d_ssq")
            wdst_ = cv.take([128, 4, 512]); r_wdst_ = R("d_wdst")
            S.dma("sp", L1[:], ssd_l1, writes=[r_Lc])
            S.dma("sp", L2[:], ssd_l2, writes=[r_Lc])
            S.op("dve", lambda e: e.memset(on64[:], 1.0), [], [r_Lc])
            S.dma("sp", dsk[:], d_skip.partition_broadcast(64), writes=[r_dsk])
            S.op("dve", lambda e: e.tensor_tensor(out=Dm[:], in0=ident[0:64, 0:64].unsqueeze(1).to_broadcast([64, 32, 64]),
                                                  in1=dsk[:].unsqueeze(2).to_broadcast([64, 32, 64]), op=ALU.mult),
                 [r_ident, r_dsk], [r_Dm])
            S.dma("sp", ngs[:], d_norm_gain.partition_broadcast(64), writes=[r_ngr])
            S.op("dve", lambda e: e.tensor_copy(out=ngr[:], in_=ngs[:]), [r_ngr], [r_ngr])
            wsrc2 = w_d_out.rearrange("(a p) n -> p a n", p=128)
            for a4 in range(4):
                for hf in range(2):
                    S.dma("sp", wdst_[:], wsrc2[:, a4 * 4:(a4 + 1) * 4, hf * 512:(hf + 1) * 512], writes=[r_wdst_])
                    S.op("pool", lambda e, a4=a4, hf=hf: e.tensor_copy(out=wdo[:, a4 * 4:(a4 + 1) * 4, hf * 512:(hf + 1) * 512], in_=wdst_[:]),
                         [r_wdst_], [r_wdo])
            for g in range(4):
                S.op("dve", lambda e, g=g: e.memset(hS[:, g * 8:(g + 1) * 8, :], 0.0), [], [r_hS[g]])
                S.op("dve", lambda e, g=g: e.memset(hTb[:, g * 8:(g + 1) * 8, :], 0.0), [], [r_hTb[g]])
            dgroups = [(s0, GW, 64, False) for s0 in range(0, T, GW)] + [(T, NSAMP, 4, True)]
            for (s, w, Q, is_s) in dgroups:
                gi = grp_of_tile(s)
                nch = w // Q
                S.dma("sp", xu[:, :, 0:w], xs[:, :, s:s + w].rearrange("c p t -> p c t"), reads=[r_xs[gi]], writes=[r_xu])
                for m in range(nch):
                    t0 = s + m * Q
                    cl = m * Q
                    S.dma("sp", xt_[0:Q].rearrange("p h e -> p (h e)"), xtokd_scr[t0:t0 + Q, 0:2048], reads=[r_xtokd], writes=[r_xt])
                    S.dma("sp", Bt[0:Q].rearrange("p g n -> p (g n)"), xtokd_scr[t0:t0 + Q, 2048:2560], reads=[r_xtokd], writes=[r_Bt])
                    S.dma("sp", BCT[:, :, 0:Q], xbcT_scr[:, :, t0:t0 + Q].rearrange("a p t -> p a t"), reads=[r_xbcT], writes=[r_BCT])
                    S.dma("sp", zt2[0:Q, :], zs_scr[t0:t0 + Q, :], reads=[r_zs], writes=[r_zt2])
                    S.dma("sp", dtc[0:Q, :], dt_scr[0, t0:t0 + Q, :], reads=[r_dts], writes=[r_dtch])
                    S.dma("sp", dac[0:Q, :], dt_scr[1, t0:t0 + Q, :], reads=[r_dts], writes=[r_dtch])
                    if is_s:
                        for g in range(4):
                            S.dma("sp", hS[:, g * 8:(g + 1) * 8, :], sd_ssm[m, g * 8:(g + 1) * 8].rearrange("h p n -> p h n"), writes=[r_hS[g]])
                            S.op("act", lambda e, g=g: e.copy(out=hb[:], in_=hS[:, g * 8:(g + 1) * 8, :]), [r_hS[g]], [r_hb])
                            pst = PS[6][:, 0:256].bitcast(BF16)

                            def trh(e, pst=pst):
                                ins = None
                                for k_ in range(8):
                                    ins = e.transpose(pst[:, k_ * 64:(k_ + 1) * 64], hb[:, k_, :], identb[0:64, 0:64])
                                return ins
                            S.op("pe", trh, [r_hb, r_identb], [PSR[6]])
                            S.op("dve", lambda e, g=g, pst=pst: e.tensor_copy(out=hTb[:, g * 8:(g + 1) * 8, :],
                                                                              in_=pst.rearrange("p (a t) -> p a t", a=8)), [PSR[6]], [r_hTb[g]])
                    S.op("pe", lambda e, Q=Q: e.matmul(PS[0][0:Q, 0:32], lhsT=L2[0:Q, 0:Q], rhs=dac[0:Q, :], start=True, stop=True),
                         [r_Lc, r_dtch], [PSR[0]])
                    S.op("pe", lambda e, Q=Q: e.matmul(PS[0][0:64, 32:64], lhsT=on64[0:Q, 0:64], rhs=dac[0:Q, :], start=True, stop=True),
                         [r_Lc, r_dtch], [PSR[0]])
                    S.op("act", lambda e, Q=Q: e.copy(out=cum[0:Q, :], in_=PS[0][0:Q, 0:32]), [PSR[0]], [r_cum])
                    S.op("act", lambda e, Q=Q: e.activation(out=ecum[0:Q, :], in_=PS[0][0:Q, 0:32], func=AF.Exp), [PSR[0]], [r_cum])
                    S.op("act", lambda e: e.activation(out=decr[:], in_=PS[0][0:64, 32:64], func=AF.Exp), [PSR[0]], [r_cum])
                    S.op("act", lambda e, Q=Q: e.copy(out=wdv[0:Q, :], in_=PS[0][0:Q, 32:64]), [PSR[0]], [r_cum])
                    S.op("dve", lambda e, Q=Q: e.tensor_tensor(out=wdv[0:Q, :], in0=wdv[0:Q, :], in1=cum[0:Q, :], op=ALU.subtract),
                         [r_cum], [r_cum])
                    S.op("act", lambda e, Q=Q: e.activation(out=wdv[0:Q, :], in_=wdv[0:Q, :], func=AF.Exp), [r_cum], [r_cum])
                    S.op("dve", lambda e, Q=Q: e.tensor_tensor(out=wdv[0:Q, :], in0=wdv[0:Q, :], in1=dtc[0:Q, :], op=ALU.mult),
                         [r_cum, r_dtch], [r_cum])
                    S.op("pool", lambda e, Q=Q: e.tensor_tensor(out=xd[0:Q], in0=xt_[0:Q], in1=wdv[0:Q, :].unsqueeze(2).to_broadcast([Q, 32, 64]),
                                                               op=ALU.mult), [r_xt, r_cum], [r_xd])
                    for g in range(4):
                        hs = slice(g * 8, (g + 1) * 8)
                        S.op("pe", lambda e, g=g, Q=Q: e.matmul(PS[1][0:Q, 0:Q], lhsT=BCT[:, g, 0:Q], rhs=BCT[:, 4 + g, 0:Q], start=True, stop=True),
                             [r_BCT], [PSR[1]])
                        S.op("dve", lambda e, Q=Q: e.tensor_tensor(out=attm[0:Q, 0:Q], in0=PS[1][0:Q, 0:Q], in1=L2[0:Q, 0:Q], op=ALU.mult),
                             [PSR[1], r_Lc], [r_attm])
                        S.op("pool", lambda e, hs=hs, Q=Q: e.tensor_tensor(
                            out=Rm[0:Q, :, 0:Q], in0=L2[0:Q, 0:Q].unsqueeze(1).to_broadcast([Q, 8, Q]),
                            in1=dac[0:Q, hs].unsqueeze(2).to_broadcast([Q, 8, Q]), op=ALU.mult), [r_Lc, r_dtch], [r_Rm])

                        def segmm(e, Q=Q):
                            ins = None
                            for k_ in range(8):
                                ins = e.matmul(PS[2][0:Q, k_ * 64:k_ * 64 + Q], lhsT=L1[0:Q, 0:Q], rhs=Rm[0:Q, k_, 0:Q], start=True, stop=True)
                            return ins
                        S.op("pe", segmm, [r_Lc, r_Rm], [PSR[2]])
                        S.op("act", lambda e, Q=Q: e.activation(out=Em[0:Q, :, 0:Q], in_=PS[2][0:Q, :].rearrange("p (h t) -> p h t", h=8)[:, :, 0:Q],
                                                                func=AF.Exp), [PSR[2]], [r_Em])
                        S.op("pool", lambda e, hs=hs, Q=Q: e.tensor_tensor(out=Em[0:Q, :, 0:Q], in0=Em[0:Q, :, 0:Q],
                                                                          in1=dtc[0:Q, hs].unsqueeze(2).to_broadcast([Q, 8, Q]), op=ALU.mult),
                             [r_Em, r_dtch], [r_Em])
                        S.op("dve", lambda e, Q=Q: e.tensor_tensor(out=Em[0:Q, :, 0:Q], in0=Em[0:Q, :, 0:Q],
                                                                   in1=attm[0:Q, 0:Q].unsqueeze(1).to_broadcast([Q, 8, Q]), op=ALU.mult),
                             [r_Em, r_attm], [r_Em])
                        S.op("pool", lambda e, hs=hs, Q=Q: e.tensor_tensor(out=Lw[0:Q, :, 0:Q], in0=Em[0:Q, :, 0:Q], in1=Dm[0:Q, hs, 0:Q], op=ALU.add),
                             [r_Em, r_Dm], [r_Lw])

                        def ymm(e, g=g, Q=Q):
                            ins = None
                            for k_ in range(8):
                                ins = e.matmul(PS[3][0:Q, k_ * 64:(k_ + 1) * 64], lhsT=Lw[0:Q, k_, 0:Q], rhs=xt_[0:Q, g * 8 + k_, :],
                                               start=True, stop=True)
                            return ins
                        S.op("pe", ymm, [r_Lw, r_xt], [PSR[3]])
                        S.op("pe", lambda e, g=g, Q=Q, hs=hs: e.matmul(PS[4][0:Q, :], lhsT=BCT[:, 4 + g, 0:Q],
                                                                       rhs=hTb[:, hs, :].rearrange("p h e -> p (h e)"), start=True, stop=True),
                             [r_BCT, r_hTb[g]], [PSR[4]])
                        S.op("dve", lambda e, Q=Q, hs=hs: e.tensor_tensor(
                            out=tmpy[0:Q, :].rearrange("p (h e) -> p h e", h=8), in0=PS[4][0:Q, :].rearrange("p (h e) -> p h e", h=8),
                            in1=ecum[0:Q, hs].unsqueeze(2).to_broadcast([Q, 8, 64]), op=ALU.mult), [PSR[4], r_cum], [r_tmpy])
                        S.op("dve", lambda e, g=g, Q=Q: e.tensor_tensor(out=yt[0:Q, g * 512:(g + 1) * 512], in0=PS[3][0:Q, :], in1=tmpy[0:Q, :],
                                                                        op=ALU.add), [PSR[3], r_tmpy], [r_yt])
                        def stmm(e, g=g, Q=Q):
                            ins = None
                            for k_ in range(8):
                                ins = e.matmul(PS[5 + k_ // 4][0:64, (k_ % 4) * 128:(k_ % 4 + 1) * 128], lhsT=xd[0:Q, g * 8 + k_, :],
                                               rhs=Bt[0:Q, g, :], start=True, stop=True)
                            return ins
                        S.op("pe", stmm, [r_xd, r_Bt], [PSR[5], PSR[6]])
                        S.op("pool", lambda e, hs=hs: e.tensor_tensor(out=hS[:, hs, :], in0=hS[:, hs, :],
                                                                     in1=decr[:, hs].unsqueeze(2).to_broadcast([64, 8, 128]), op=ALU.mult),
                             [r_hS[g], r_cum], [r_hS[g]])
                        for hh2 in range(2):
                            S.op("dve", lambda e, g=g, hh2=hh2: e.tensor_tensor(
                                out=hS[:, g * 8 + hh2 * 4:g * 8 + hh2 * 4 + 4, :], in0=hS[:, g * 8 + hh2 * 4:g * 8 + hh2 * 4 + 4, :],
                                in1=PS[5 + hh2][0:64, :].rearrange("p (h n) -> p h n", h=4), op=ALU.add), [r_hS[g], PSR[5 + hh2]], [r_hS[g]])
                        if is_s:
                            S.dma("sp", dsss[m, g * 8:(g + 1) * 8].rearrange("h p n -> p h n"), hS[:, hs, :], reads=[r_hS[g]])
                        else:
                            S.op("act", lambda e, hs=hs: e.copy(out=hb[:], in_=hS[:, hs, :]), [r_hS[g]], [r_hb])
                            pst = PS[7][:, 0:256].bitcast(BF16)

                            def trh2(e, pst=pst):
                                ins = None
                                for k_ in range(8):
                                    ins = e.transpose(pst[:, k_ * 64:(k_ + 1) * 64], hb[:, k_, :], identb[0:64, 0:64])
                                return ins
                            S.op("pe", trh2, [r_hb, r_identb], [PSR[7]])
                            S.op("dve", lambda e, hs=hs, pst=pst: e.tensor_copy(out=hTb[:, hs, :], in_=pst.rearrange("p (a t) -> p a t", a=8)),
                                 [PSR[7]], [r_hTb[g]])
                    S.op("pool", lambda e, Q=Q: e.tensor_tensor(out=yt[0:Q, :], in0=yt[0:Q, :], in1=zt2[0:Q, :], op=ALU.mult), [r_yt, r_zt2], [r_yt])
                    S.op("act", lambda e, Q=Q: e.activation(out=ngs[0:Q, :], in_=yt[0:Q, :], func=AF.Square), [r_yt, r_ngr], [r_ngr])
                    S.op("dve", lambda e, Q=Q: e.tensor_reduce(out=ssq[0:Q, :], in_=ngs[0:Q, :].rearrange("p (g f) -> p g f", g=4), axis=AX.X, op=ALU.add),
                         [r_ngr], [r_ssq])
                    S.op("act", lambda e, Q=Q: e.activation(out=ssq[0:Q, :], in_=ssq[0:Q, :], func=AF.Ln, bias=epsc[0:Q, :], scale=1.0 / 512),
                         [r_ssq, r_eps], [r_ssq])
                    S.op("act", lambda e, Q=Q: e.activation(out=ssq[0:Q, :], in_=ssq[0:Q, :], func=AF.Exp, scale=-0.5), [r_ssq], [r_ssq])
                    S.op("dve", lambda e, Q=Q: e.tensor_tensor(out=yt[0:Q, :].rearrange("p (g f) -> p g f", g=4),
                                                               in0=yt[0:Q, :].rearrange("p (g f) -> p g f", g=4),
                                                               in1=ssq[0:Q, :].unsqueeze(2).to_broadcast([Q, 4, 512]), op=ALU.mult),
                         [r_yt, r_ssq], [r_yt])
                    S.op("pool", lambda e, Q=Q: e.tensor_tensor(out=gb[0:Q, :], in0=yt[0:Q, :], in1=ngr[0:Q, :], op=ALU.mult), [r_yt, r_ngr], [r_gb])
                    for q4 in range(4):
                        pb = 1 + (q4 % 2)
                        pst = PS[pb][:, 0:256].bitcast(BF16)

                        def trg(e, q4=q4, Q=Q, pst=pst):
                            ins = None
                            for k_ in range(4):
                                ch = q4 * 4 + k_
                                ins = e.transpose(pst[:, k_ * 64:k_ * 64 + Q], gb[0:Q, ch * 128:(ch + 1) * 128], identb[0:Q, 0:Q])
                            return ins
                        S.op("pe", trg, [r_gb, r_identb], [PSR[pb]])
                        S.op("act", lambda e, q4=q4, Q=Q, cl=cl, pst=pst: e.copy(
                            out=gT[:, q4 * 4:(q4 + 1) * 4, cl:cl + Q], in_=pst[:, 0:256].rearrange("p (a t) -> p a t", a=4)[:, :, 0:Q]),
                            [PSR[pb]], [r_gT])
                for dc in range(DC):
                    pb = 3 + dc % 2

                    def omm(e, dc=dc, pb=pb, w=w):
                        ins = None
                        for ch in range(16):
                            ins = e.matmul(PS[pb][:, 0:w], lhsT=wdo[:, ch, dc * 128:(dc + 1) * 128], rhs=gT[:, ch, 0:w], start=(ch == 0), stop=(ch == 15))
                        return ins
                    S.op("pe", omm, [r_wdo, r_gT], [PSR[pb]])
                    S.op("dve", lambda e, dc=dc, pb=pb, w=w: e.tensor_tensor(out=xu[:, dc, 0:w], in0=PS[pb][:, 0:w], in1=xu[:, dc, 0:w], op=ALU.add),
                         [PSR[pb], r_xu], [r_xu])
                S.dma("sp", xs[:, :, s:s + w].rearrange("c p t -> p c t"), xu[:, :, 0:w], reads=[r_xu], writes=[r_xs[gi]])
                if (not is_s) and s + w == T:
                    for g in range(4):
                        S.dma("sp", dssp[g * 8:(g + 1) * 8].rearrange("h p n -> p h n"), hS[:, g * 8:(g + 1) * 8, :], reads=[r_hS[g]])
            S.barrier()

        def mixer_b(l):
            norm_pass(l * 3 + 1)
            S.barrier()
            NT = len(tiles)
            NBLK = T // 256
            BIGNEG = -1.0e30
            cv = Carver()
            wst_ = [cv.take([128, DC, 256]) for _ in range(2)]; r_wst_ = [R("b_wst0"), R("b_wst1")]
            wbf_ = cv.take([128, DC, 512], BF16); r_wbf_ = R("b_wbf")
            sqv = cv.take([128, 512]); r_sqv = R("b_sqv")
            xn = [cv.take([128, 512]) for _ in range(2)]; r_xn = [R("b_xn0"), R("b_xn1")]
            xb = [cv.take([128, 512], BF16) for _ in range(2)]; r_xb = [R("b_xb0"), R("b_xb1")]
            rtmp = cv.take([128, 4, 64]); r_rtmp = R("b_rtmp")
            xT = [cv.take([128, 4, 128], BF16) for _ in range(2)]; r_xT = [R("b_xT0"), R("b_xT1")]
            ss = cv.take([128, 8]); r_ss = R("b_ss")
            grep_ = cv.take([128, 2, 64]); r_grep = R("b_grep")
            cosT = cv.take([128, NT, 8]); sinT = cv.take([128, NT, 8]); r_cs = R("b_cossin")
            kTres = cv.take([128, 2, T], BF16); r_kTres = R("kTres")
            kmT = cv.take([128, 2, NBLK]); r_kmT = R("kmT")
            kmd = cv.take([128, 4, NBLK]); r_kmd = R("kmd")
            qTf = cv.take([128, 4, 128]); r_qTf = R("qTf")
            gc = cv.take([128, 8, 16]); r_gc = R("gc")
            m8 = cv.take([128, 8, 8]); r_m8 = R("m8")
            biasf = cv.take([128, 8, 16]); r_biasf = R("biasf")
            biasb = cv.take([128, 8, 16], BF16); r_biasb = R("biasb")
            bT = [cv.take([16, 8, 128], BF16) for _ in range(2)]; r_bT = [R("bT0"), R("bT1")]
            indf = cv.take([16, 512]); indb = cv.take([16, 512], BF16); r_ind = R("ind")
            S.dma("sp", grep_[:], b_qk_gain.rearrange("j e -> (j e)").partition_broadcast(128)
                  .rearrange("p (j e) -> p j e", j=2), writes=[r_grep])
            S.dma("sp", cosT[:], rope_cos.rearrange("(n p) i -> p n i", p=128), writes=[r_cs])
            S.dma("sp", sinT[:], rope_sin.rearrange("(n p) i -> p n i", p=128), writes=[r_cs])
            r_oTbs = R("oTb_scr"); r_kaug = R("kaug_scr"); r_qaug = R("qaug_scr"); r_vtokb = R("vtokb_scr"); r_sq = R("b_sscr"); r_kms = R("kmT_scr")
            for c5 in range(T // 512):
                S.dma("sp", indf[:], kind_d[:, c5 * 512:(c5 + 1) * 512], writes=[r_ind])
                S.op("dve", lambda e: e.tensor_copy(out=indb[:], in_=indf[:]), [r_ind], [r_ind])
                for hk in range(4):
                    S.dma("sp", kaug_scr[hk, 64:80, c5 * 512:(c5 + 1) * 512], indb[:], reads=[r_ind], writes=[r_kaug])
            wsrc = w_b_in.rearrange("(c p) n -> p c n", p=128)

            def load_block(col0):
                for hf in range(2):
                    S.dma("sp", wst_[hf][:], wsrc[:, :, col0 + hf * 256:col0 + (hf + 1) * 256], writes=[r_wst_[hf]])
                    S.op("pool", lambda e, hf=hf: e.tensor_copy(out=wbf_[:, :, hf * 256:(hf + 1) * 256], in_=wst_[hf][:]),
                         [r_wst_[hf]], [r_wbf_])

            def normrope(pb, w, ti, nh, jg, c0):
                ncol = nh * 64
                S.op("act", lambda e: e.copy(out=xn[pb][0:w, :], in_=PS[pb][0:w, :]), [PSR[pb]], [r_xn[pb]])
                S.op("act", lambda e: e.activation(out=sqv[0:w, 0:ncol], in_=PS[pb][0:w, c0:c0 + ncol], func=AF.Square),
                     [PSR[pb]], [r_sqv])
                S.op("dve", lambda e: e.tensor_reduce(out=ss[0:w, 0:nh], in_=sqv[0:w, 0:ncol].rearrange("p (h e) -> p h e", h=nh),
                                                      axis=AX.X, op=ALU.add), [r_sqv], [r_ss])
                S.op("act", lambda e: e.activation(out=ss[0:w, 0:nh], in_=ss[0:w, 0:nh], func=AF.Ln, bias=epsc[0:w, :], scale=1.0 / 64),
                     [r_ss, r_eps], [r_ss])
                S.op("act", lambda e: e.activation(out=ss[0:w, 0:nh], in_=ss[0:w, 0:nh], func=AF.Exp, scale=-0.5), [r_ss], [r_ss])
                x3 = xn[pb][0:w, c0:c0 + ncol].rearrange("p (h e) -> p h e", h=nh)
                S.op("dve", lambda e: e.tensor_tensor(out=x3, in0=x3,
                                                      in1=ss[0:w, 0:nh].unsqueeze(2).to_broadcast([w, nh, 64]), op=ALU.mult),
                     [r_xn[pb], r_ss], [r_xn[pb]])
                S.op("dve", lambda e: e.tensor_tensor(out=x3, in0=x3, in1=grep_[0:w, jg:jg + 1, :].to_broadcast([w, nh, 64]), op=ALU.mult),
                     [r_xn[pb], r_grep], [r_xn[pb]])
                cb_ = cosT[0:w, ti:ti + 1, :].to_broadcast([w, nh, 8])
                sb_ = sinT[0:w, ti:ti + 1, :].to_broadcast([w, nh, 8])
                rt = rtmp[0:w, :, 0:nh * 8].rearrange("p a (h i) -> p a h i", h=nh)
                S.op("dve", lambda e: e.tensor_tensor(out=rt[:, 0], in0=x3[:, :, 0:8], in1=cb_, op=ALU.mult), [r_xn[pb], r_cs], [r_rtmp])
                S.op("dve", lambda e: e.tensor_tensor(out=rt[:, 1], in0=x3[:, :, 8:16], in1=sb_, op=ALU.mult), [r_xn[pb], r_cs], [r_rtmp])
                S.op("dve", lambda e: e.tensor_tensor(out=rt[:, 2], in0=x3[:, :, 8:16], in1=cb_, op=ALU.mult), [r_xn[pb], r_cs], [r_rtmp])
                S.op("dve", lambda e: e.tensor_tensor(out=rt[:, 3], in0=x3[:, :, 0:8], in1=sb_, op=ALU.mult), [r_xn[pb], r_cs], [r_rtmp])
                S.op("dve", lambda e: e.tensor_tensor(out=x3[:, :, 0:8], in0=rt[:, 0], in1=rt[:, 1], op=ALU.subtract), [r_rtmp], [r_xn[pb]])
                S.op("dve", lambda e: e.tensor_tensor(out=x3[:, :, 8:16], in0=rt[:, 2], in1=rt[:, 3], op=ALU.add), [r_rtmp], [r_xn[pb]])

            def proj(pb, s, w, gi):
                def pj(e):
                    ins = None
                    for c in range(DC):
                        ins = e.matmul(PS[pb][0:w, :], lhsT=hT[:, c, s:s + w], rhs=wbf_[:, c, :], start=(c == 0), stop=(c == DC - 1))
                    return ins
                S.op("pe", pj, [r_hT[gi], r_wbf_], [PSR[pb]])

            load_block(1024)
            tcnt = 0
            for ti, (s, w) in enumerate(tiles):
                gi = grp_of_tile(s)
                pb = tcnt % 2; tcnt += 1
                is_s = s >= T
                proj(pb, s, w, gi)
                normrope(pb, w, ti, 4, 1, 0)
                if is_s:
                    S.dma("sp", bkvs.rearrange("t k f -> t (k f)"), xn[pb][0:w, :], reads=[r_xn[pb]])
                    S.dma("sp", ksb_scr, xn[pb][0:w, :], reads=[r_xn[pb]], writes=[r_sq])
                    continue
                S.dma("sp", bkvp[s:s + w].rearrange("t k f -> t (k f)"), xn[pb][0:w, :], reads=[r_xn[pb]])
                S.op("dve", lambda e, pb=pb, w=w: e.tensor_copy(out=xb[pb][0:w, :], in_=xn[pb][0:w, :]), [r_xn[pb]], [r_xb[pb]])
                S.dma("sp", vtokb_scr[s:s + w, :], xb[pb][0:w, 256:512], reads=[r_xb[pb]], writes=[r_vtokb])
                pst = PS[2 + pb][:, 0:256].bitcast(BF16)

                def trk(e, pb=pb, w=w, pst=pst):
                    ins = None
                    for a in range(2):
                        ins = e.transpose(pst[:, a * 128:a * 128 + w], xb[pb][0:w, a * 128:(a + 1) * 128], identb[0:w, 0:w])
                    return ins
                S.op("pe", trk, [r_xb[pb], r_identb], [PSR[2 + pb]])
                S.op("dve", lambda e, w=w, s=s, pst=pst: e.tensor_copy(out=kTres[:, :, s:s + w],
                                                                       in_=pst[:, 0:256].rearrange("p (a t) -> p a t", a=2)[:, :, 0:w]),
                     [PSR[2 + pb]], [r_kTres])
                for half in range(2):
                    S.dma("sp", kaug_scr[half:4:2, 0:64, s:s + w].rearrange("a e t -> e a t"), kTres[half * 64:(half + 1) * 64, :, s:s + w],
                          reads=[r_kTres], writes=[r_kaug])
            S.op("dve", lambda e: e.tensor_reduce(out=kmT[:], in_=kTres[:].rearrange("p a (n k) -> p a n k", k=256), axis=AX.X, op=ALU.add),
                 [r_kTres], [r_kmT])
            S.op("dve", lambda e: e.tensor_scalar(out=kmT[:], in0=kmT[:], scalar1=1.0 / 256, scalar2=None, op0=ALU.mult), [r_kmT], [r_kmT])
            S.dma("sp", kmT_scr.rearrange("a p n -> p a n"), kmT[:], reads=[r_kmT], writes=[r_kms])
            for half in range(2):
                S.dma("sp", kmd[half * 64:(half + 1) * 64], kmT_scr.rearrange("a (b e) n -> e (a b) n", b=2), reads=[r_kms], writes=[r_kmd])
            bq_pending = [None]
            for qblk in range(2):
                load_block(qblk * 512)
                for ti, (s, w) in enumerate(tiles):
                    gi = grp_of_tile(s)
                    pb = tcnt % 2; tcnt += 1
                    is_s = s >= T
                    proj(pb, s, w, gi)
                    if bq_pending[0] is not None:
                        bq_pending[0]()
                        bq_pending[0] = None
                    normrope(pb, w, ti, 8, 0, 0)
                    if is_s:
                        S.dma("sp", qsb_scr[:, qblk * 512:(qblk + 1) * 512], xn[pb][0:w, :], reads=[r_xn[pb]], writes=[r_sq])
                        continue
                    def b_tail(pb=pb, w=w, s=s, qblk=qblk):
                        blk = s // 256
                        S.op("act", lambda e, pb=pb, w=w: e.copy(out=xb[pb][0:w, :], in_=xn[pb][0:w, :]), [r_xn[pb]], [r_xb[pb]])
                        pst = PS[2 + pb][:, 0:256].bitcast(BF16)

                        def trq(e, pb=pb, w=w, pst=pst):
                            ins = None
                            for a in range(4):
                                ins = e.transpose(pst[:, a * 128:a * 128 + w], xb[pb][0:w, a * 128:(a + 1) * 128], identb[0:w, 0:w])
                            return ins
                        S.op("pe", trq, [r_xb[pb], r_identb], [PSR[2 + pb]])
                        S.op("dve", lambda e, pb=pb, w=w, pst=pst: e.tensor_copy(out=xT[pb][:, :, 0:w], in_=pst.rearrange("p (a t) -> p a t", a=4)[:, :, 0:w]),
                             [PSR[2 + pb]], [r_xT[pb]])
                        for half in range(2):
                            S.dma("sp", qaug_scr[qblk * 8 + half:qblk * 8 + 8:2, 0:64, s:s + w].rearrange("a e t -> e a t"),
                                  xT[pb][half * 64:(half + 1) * 64, :, 0:w], reads=[r_xT[pb]], writes=[r_qaug])
                        if blk <= 3:
                            S.op("dve", lambda e, w=w: e.memset(biasf[0:w], -1.0), [], [r_biasf])
                            S.op("dve", lambda e, w=w, blk=blk: e.memset(biasf[0:w, :, 0:blk + 1], 0.0), [], [r_biasf])
                        else:
                            def trf(e, pb=pb, w=w):
                                ins = None
                                for a in range(4):
                                    ins = e.transpose(PS[4][:, a * 128:a * 128 + w], xn[pb][0:w, a * 128:(a + 1) * 128], ident[0:w, 0:w])
                                return ins
                            S.op("pe", trf, [r_xn[pb], r_ident], [PSR[4]])
                            S.op("act", lambda e, w=w: e.copy(out=qTf[:, :, 0:w], in_=PS[4][:].rearrange("p (a t) -> p a t", a=4)[:, :, 0:w]),
                                 [PSR[4]], [r_qTf])

                            def gmm(e, w=w, qblk=qblk):
                                ins = None
                                for hl in range(8):
                                    a, half = hl // 2, hl % 2
                                    hk = (qblk * 8 + hl) // 4
                                    ins = e.matmul(PS[5][0:w, hl * 16:hl * 16 + NBLK], lhsT=qTf[half * 64:(half + 1) * 64, a, 0:w],
                                                   rhs=kmd[half * 64:(half + 1) * 64, hk, :], start=True, stop=True)
                                return ins
                            S.op("pe", gmm, [r_qTf, r_kmd], [PSR[5]])
                            S.op("dve", lambda e, w=w: e.tensor_copy(out=gc[0:w, :, 0:NBLK],
                                                                     in_=PS[5][0:w, 0:128].rearrange("p (h n) -> p h n", h=8)[:, :, 0:NBLK]),
                                 [PSR[5]], [r_gc])
                            S.op("dve", lambda e, w=w, blk=blk: e.memset(gc[0:w, :, blk:16], BIGNEG), [], [r_gc])

                            def mx(e, w=w):
                                ins = None
                                for hl in range(8):
                                    ins = e.max(out=m8[0:w, hl, :], in_=gc[0:w, hl, :])
                                return ins
                            S.op("dve", mx, [r_gc], [r_m8])
                            S.op("dve", lambda e, w=w: e.tensor_tensor(out=biasf[0:w], in0=gc[0:w], in1=m8[0:w, :, 2:3].to_broadcast([w, 8, 16]),
                                                                       op=ALU.is_ge), [r_gc, r_m8], [r_biasf])
                            S.op("dve", lambda e, w=w: e.tensor_scalar(out=biasf[0:w], in0=biasf[0:w], scalar1=-1.0, scalar2=None, op0=ALU.add),
                                 [r_biasf], [r_biasf])
                            S.op("dve", lambda e, w=w, blk=blk: e.memset(biasf[0:w, :, blk:blk + 1], 0.0), [], [r_biasf])
                        S.op("dve", lambda e, w=w: e.tensor_copy(out=biasb[0:w], in_=biasf[0:w]), [r_biasf], [r_biasb])
                        pstb = PS[6 + pb][:, 0:512].bitcast(BF16)

                        def trb(e, w=w, pstb=pstb):
                            ins = None
                            for hl in range(8):
                                ins = e.transpose(pstb[0:16, hl * 128:hl * 128 + w], biasb[0:w, hl, :], identb[0:w, 0:w])
                            return ins
                        S.op("pe", trb, [r_biasb, r_identb], [PSR[6 + pb]])
                        S.op("act", lambda e, pb=pb, w=w, pstb=pstb: e.copy(out=bT[pb][:, :, 0:w],
                                                                            in_=pstb[0:16, :].rearrange("p (h t) -> p h t", h=8)[:, :, 0:w]),
                             [PSR[6 + pb]], [r_bT[pb]])
                        S.dma("sp", qaug_scr[qblk * 8:(qblk + 1) * 8, 64:80, s:s + w].rearrange("h n t -> n h t"), bT[pb][:, :, 0:w],
                              reads=[r_bT[pb]], writes=[r_qaug])
                    bq_pending[0] = b_tail
            if bq_pending[0] is not None:
                bq_pending[0]()
                bq_pending[0] = None
            S.barrier()
            if dbg_stop <= 1:
                return
            cv = Carver()
            kaug = cv.take([80, T], BF16); r_kaug_sb = R("kaug_sb")
            qaug = [cv.take([80, T], BF16) for _ in range(2)]; r_qaug_sb = [R("qaug_sb0"), R("qaug_sb1")]
            vtk = cv.take([128, T // 128, 72], BF16); r_vtk = R("vtk")
            rr = cv.take([65, 256]); r_rr = R("b_rr")
            onesf = cv.take([65, 64]); r_onesf = R("b_onesf")
            oTh = [cv.take([64, T], BF16) for _ in range(2)]; r_oTh = [R("oTh0"), R("oTh1")]
            pf = [cv.take([128, 256]) for _ in range(2)]; r_pf = [R("b_pf0"), R("b_pf1")]
            pbf = [cv.take([128, 256], BF16) for _ in range(3)]; r_pbf = [R("b_pbf0"), R("b_pbf1"), R("b_pbf2")]
            mask3 = cv.take([128, 256]); r_mask3 = R("mask3")
            rden = cv.take([64, 256]); r_rden = R("rden")
            S.dma("sp", mask3[:], mask3_d, writes=[r_mask3])
            S.op("dve", lambda e: e.memset(onesf[:], 1.0), [], [r_onesf])
            S.op("dve", lambda e: e.memset(vtk[:, :, 64:72], 1.0), [], [r_vtk])
            it = 0
            pc = 0
            for hk in range(4):
                S.dma("sp", kaug[:], kaug_scr[hk, :, 0:T], reads=[r_kaug], writes=[r_kaug_sb])
                S.dma("sp", vtk[:, :, 0:64], vtokb_scr[0:T, hk * 64:(hk + 1) * 64].rearrange("(kt i) f -> i kt f", i=128), reads=[r_vtokb], writes=[r_vtk])
                for r4 in range(4):
                    h = hk * 4 + r4
                    qa = qaug[h % 2]; r_qa = r_qaug_sb[h % 2]
                    ob = oTh[h % 2]; r_ob = r_oTh[h % 2]
                    S.dma("sp", qa[:], qaug_scr[h, :, 0:T], reads=[r_qaug], writes=[r_qa])
                    for blk in range(NBLK):
                        q0 = blk * 256
                        ob_ps = 2 + (it % 2)
                        dn_ps = 4 + (it % 2); it += 1
                        OP = PS[ob_ps]
                        DP = PS[dn_ps]
                        nkt = 2 * blk + 2
                        def emit_s(kt, st, q0=q0, qa=qa, r_qa=r_qa):
                            sp_, pbi, own0, own1, qlo, nq = st
                            S.op("pe", lambda e: e.matmul(
                                PS[sp_][:, 0:nq], lhsT=kaug[:, kt * 128:(kt + 1) * 128], rhs=qa[:, q0 + qlo:q0 + 256], start=True, stop=True),
                                [r_kaug_sb, r_qa], [PSR[sp_]])

                        def emit_exp(kt, st):
                            sp_, pbi, own0, own1, qlo, nq = st
                            if own0 or own1:
                                S.op("act", lambda e: e.activation(out=pf[sp_][:, 0:nq], in_=PS[sp_][:, 0:nq], func=AF.Exp, scale=0.125),
                                     [PSR[sp_]], [r_pf[sp_]])
                                S.op("dve", lambda e: e.tensor_tensor(out=pbf[pbi][:, 0:nq], in0=pf[sp_][:, 0:nq], in1=mask3[:, 0:nq], op=ALU.mult),
                                     [r_pf[sp_], r_mask3], [r_pbf[pbi]])
                            else:
                                S.op("act", lambda e: e.activation(out=pbf[pbi][:, 0:256], in_=PS[sp_][:, 0:256], func=AF.Exp, scale=0.125),
                                     [PSR[sp_]], [r_pbf[pbi]])

                        def emit_pv(kt, st, OP=OP, ob_ps=ob_ps):
                            sp_, pbi, own0, own1, qlo, nq = st

                            def pv(e):
                                ins = None
                                first = (kt == 0)
                                if own1:
                                    segs = [(128, 256, 0, 128, True)]
                                elif own0:
                                    segs = [(0, 128, 0, 128, True), (128, 256, 128, 256, False)]
                                else:
                                    segs = [(0, 256, 0, 256, False)]
                                for (o0, o1, p0, p1, last) in segs:
                                    ins = e.matmul(OP[0:65, o0:o1], lhsT=vtk[:, kt, 0:65], rhs=pbf[pbi][:, p0:p1], start=first, stop=last)
                                return ins
                            S.op("pe", pv, [r_vtk, r_pbf[pbi]], [PSR[ob_ps]])

                        prev = None
                        for kt in range(nkt):
                            sp_ = pc % 2; pc += 1
                            pbi = pc % 3
                            own0 = (kt == 2 * blk)
                            own1 = (kt == 2 * blk + 1)
                            qlo = 128 if own1 else 0
                            st = (sp_, pbi, own0, own1, qlo, 256 - qlo)
                            emit_s(kt, st)
                            if prev is not None:
                                emit_pv(*prev)
                            emit_exp(kt, st)
                            prev = (kt, st)
                        emit_pv(*prev)
                        S.op("dve", lambda e, OP=OP: e.reciprocal(out=rr[64:65, :], in_=OP[64:65, 0:256]), [PSR[ob_ps]], [r_rr])
                        S.op("pe", lambda e, DP=DP: e.matmul(DP[0:64, 0:256], lhsT=onesf[64:65, :], rhs=rr[64:65, :], start=True, stop=True),
                             [r_rr, r_onesf], [PSR[dn_ps]])
                        S.op("act", lambda e, DP=DP: e.copy(out=rden[:], in_=DP[0:64, 0:256]), [PSR[dn_ps]], [r_rden])
                        S.op("dve", lambda e, OP=OP, ob=ob, q0=q0: e.tensor_tensor(out=ob[:, q0:q0 + 256], in0=OP[0:64, 0:256], in1=rden[:], op=ALU.mult),
                             [PSR[ob_ps], r_rden], [r_ob])
                    S.dma("sp", oTb_scr[h, :, 0:T], ob[:], reads=[r_ob], writes=[r_oTbs])
            S.barrier()
            if dbg_stop <= 2:
                return
            cv = Carver()
            ptf = cv.take([128, NB * 16]); r_ptf = R("ptf")
            pti = cv.take([128, NB * 16], I32); r_pti = R("pti")
            pcol = cv.take([128, 1]); r_pcol = R("pcol")
            KV = cv.take([128, 16, 512]); r_KV = R("KV")
            KVb = [cv.take([128, 16, 512], BF16) for _ in range(2)]; r_KVb = [R("KVb0"), R("KVb1")]
            kTs = cv.take([64, 4, 2048], BF16); r_kTs = R("kTs")
            kms = cv.take([64, 4, 8]); r_kms2 = R("kms")
            qtok = cv.take([64, 1024]); r_qtok = R("qtok")
            ktok = cv.take([64, 512]); r_ktok = R("ktok")
            qTall = cv.take([64, 16, 64]); r_qTall = R("qTall")
            kTn = cv.take([64, 4, 64], BF16); r_kTn = R("kTn")
            vnew = cv.take([4, NB, 256], BF16); vnewf = cv.take([4, NB, 256]); r_vnew = R("vnew")
            qbf_ = cv.take([64, 4, 16]); qbb = cv.take([64, 4, 16], BF16); r_qb = R("qb")
            gs = cv.take([16, 4, 8]); m8s = cv.take([16, 4, 8]); bsf = cv.take([16, 4, 8]); bsb = cv.take([16, 4, 8], BF16); r_gs = R("gs")
            bTs = cv.take([8, 4, 16], BF16); r_bTs = R("bTs")
            inds = cv.take([8, 2048], BF16); indsf = cv.take([8, 2048]); r_inds = R("inds")
            PT = cv.take([128, 16, 16], BF16); r_PT = R("PT")
            pnf = cv.take([4, 16]); pnb = cv.take([4, 16], BF16); mask4 = cv.take([4, 16]); r_pn = R("pn")
            rds = cv.take([64, 4, 16]); r_rds = R("rds")
            oTs = cv.take([64, 16, NSAMP], BF16); r_oTs = R("oTs")
            S.dma("sp", pti[:], page_tab.rearrange("b g -> (b g)").partition_broadcast(128), writes=[r_pti])
            S.dma("sp", pcol[:], pidx_d, writes=[r_pcol])
            S.op("dve", lambda e: e.tensor_copy(out=ptf[:], in_=pti[:]), [r_pti], [r_ptf])
            S.op("dve", lambda e: e.tensor_scalar(out=ptf[:], in0=ptf[:], scalar1=128.0, scalar2=pcol[:, 0:1], op0=ALU.mult, op1=ALU.add),
                 [r_ptf, r_pcol], [r_ptf])
            S.op("dve", lambda e: e.tensor_copy(out=pti[:], in_=ptf[:]), [r_ptf], [r_pti])
            S.dma("sp", indsf[:], kinds_d, writes=[r_inds])
            S.op("dve", lambda e: e.tensor_copy(out=inds[:], in_=indsf[:]), [r_inds], [r_inds])
            S.dma("sp", mask4[:], mask4_d, writes=[r_pn])
            S.dma("sp", qtok[:], qsb_scr, reads=[r_sq], writes=[r_qtok])
            S.dma("sp", ktok[:], ksb_scr, reads=[r_sq], writes=[r_ktok])
            S.dma("sp", vnewf[:], ksb_scr[:, 256:512].rearrange("(b i) f -> i b f", i=4), reads=[r_sq], writes=[r_vnew])
            S.op("dve", lambda e: e.tensor_copy(out=vnew[:], in_=vnewf[:]), [r_vnew], [r_vnew])
            for q4 in range(4):
                def trqs(e, q4=q4):
                    ins = None
                    for k_ in range(4):
                        h = q4 * 4 + k_
                        ins = e.transpose(PS[0][0:64, k_ * 64:(k_ + 1) * 64], qtok[:, h * 64:(h + 1) * 64], ident[0:64, 0:64])
                    return ins
                S.op("pe", trqs, [r_qtok, r_ident], [PSR[0]])
                S.op("act", lambda e, q4=q4: e.copy(out=qTall[:, q4 * 4:(q4 + 1) * 4, :], in_=PS[0][0:64, 0:256].rearrange("p (a t) -> p a t", a=4)),
                     [PSR[0]], [r_qTall])

            def trks(e):
                ins = None
                for hk in range(4):
                    ins = e.transpose(PS[1][0:64, hk * 64:(hk + 1) * 64], ktok[:, hk * 64:(hk + 1) * 64], ident[0:64, 0:64])
                return ins
            S.op("pe", trks, [r_ktok, r_ident], [PSR[1]])
            S.op("act", lambda e: e.copy(out=kTn[:], in_=PS[1][0:64, 0:256].rearrange("p (a t) -> p a t", a=4)), [PSR[1]], [r_kTn])
            pool_rows = pool_kv.rearrange("g t k f -> (g t) (k f)")
            for bt in range(NB):
                kb_ = bt % 2
                for pg in range(16):
                    S.op("pool", lambda e, bt=bt, pg=pg: e.indirect_dma_start(
                        out=KV[:, pg, :], out_offset=None, in_=pool_rows,
                        in_offset=bass.IndirectOffsetOnAxis(ap=pti[:, bt * 16 + pg:bt * 16 + pg + 1], axis=0)),
                        [r_pti], [r_KV], dma=True)
                S.op("dve", lambda e, kb_=kb_: e.tensor_copy(out=KVb[kb_][:, 0:8, :], in_=KV[:, 0:8, :]), [r_KV], [r_KVb[kb_]])
                S.op("act", lambda e, kb_=kb_: e.copy(out=KVb[kb_][:, 8:16, :], in_=KV[:, 8:16, :]), [r_KV], [r_KVb[kb_]])
                for pg2 in range(8):
                    pst = PS[2 + pg2 % 2][:, 0:512].bitcast(BF16)

                    def trk2(e, kb_=kb_, pg2=pg2, pst=pst):
                        ins = None
                        for k_ in range(2):
                            pg = pg2 * 2 + k_
                            for hk in range(4):
                                ins = e.transpose(pst[0:64, hk * 256 + k_ * 128:hk * 256 + (k_ + 1) * 128],
                                                  KVb[kb_][:, pg, hk * 64:(hk + 1) * 64], identb[:])
                        return ins
                    S.op("pe", trk2, [r_KVb[kb_], r_identb], [PSR[2 + pg2 % 2]])
                    S.op("dve", lambda e, pg2=pg2, pst=pst: e.tensor_copy(out=kTs[:, :, pg2 * 256:(pg2 + 1) * 256],
                                                                         in_=pst[0:64, 0:1024].rearrange("p (h t) -> p h t", h=4)),
                         [PSR[2 + pg2 % 2]], [r_kTs])
                S.op("dve", lambda e: e.tensor_reduce(out=kms[:], in_=kTs[:].rearrange("p h (n k) -> p h n k", k=256), axis=AX.X, op=ALU.add),
                     [r_kTs], [r_kms2])
                S.op("dve", lambda e: e.tensor_scalar(out=kms[:], in0=kms[:], scalar1=1.0 / 256, scalar2=None, op0=ALU.mult), [r_kms2], [r_kms2])
                S.op("dve", lambda e, bt=bt: e.tensor_copy(out=qbf_[:].rearrange("p k (r i) -> p k r i", i=4),
                                                           in_=qTall[:, :, bt * 4:(bt + 1) * 4].rearrange("p (k r) i -> p k r i", r=4)),
                     [r_qTall], [r_qb])
                S.op("dve", lambda e: e.tensor_copy(out=qbb[:], in_=qbf_[:]), [r_qb], [r_qb])

                def gms(e):
                    ins = None
                    for hk in range(4):
                        ins = e.matmul(PS[4][0:16, hk * 8:(hk + 1) * 8], lhsT=qbf_[:, hk, :], rhs=kms[:, hk, :], start=True, stop=True)
                    return ins
                S.op("pe", gms, [r_qb, r_kms2], [PSR[4]])
                S.op("dve", lambda e: e.tensor_copy(out=gs[:], in_=PS[4][0:16, 0:32].rearrange("p (k n) -> p k n", k=4)), [PSR[4]], [r_gs])

                def mxs(e):
                    ins = None
                    for hk in range(4):
                        ins = e.max(out=m8s[:, hk, :], in_=gs[:, hk, :])
                    return ins
                S.op("dve", mxs, [r_gs], [r_gs])
                S.op("dve", lambda e: e.tensor_tensor(out=bsf[:], in0=gs[:], in1=m8s[:, :, 2:3].to_broadcast([16, 4, 8]), op=ALU.is_ge), [r_gs], [r_gs])
                S.op("dve", lambda e: e.tensor_scalar(out=bsb[:], in0=bsf[:], scalar1=-1.0, scalar2=None, op0=ALU.add), [r_gs], [r_gs])
                pstb = PS[5][:, 0:64].bitcast(BF16)

                def trbs(e, pstb=pstb):
                    ins = None
                    for hk in range(4):
                        ins = e.transpose(pstb[0:8, hk * 16:(hk + 1) * 16], bsb[:, hk, :], identb[0:16, 0:16])
                    return ins
                S.op("pe", trbs, [r_gs, r_identb], [PSR[5]])
                S.op("act", lambda e, pstb=pstb: e.copy(out=bTs[:], in_=pstb[0:8, 0:64].rearrange("p (k q) -> p k q", k=4)), [PSR[5]], [r_bTs])
                for hk in range(4):
                    sp_ = 6 + hk % 2

                    def smm(e, hk=hk, sp_=sp_):
                        ins = None
                        for kt in range(16):
                            e.matmul(PS[sp_][:, kt * 16:(kt + 1) * 16], lhsT=kTs[:, hk, kt * 128:(kt + 1) * 128], rhs=qbb[:, hk, :], start=True, stop=False)
                            ins = e.matmul(PS[sp_][:, kt * 16:(kt + 1) * 16], lhsT=inds[:, kt * 128:(kt + 1) * 128], rhs=bTs[:, hk, :], start=False, stop=True)
                        ins = e.matmul(PS[sp_][0:4, 256:272], lhsT=kTn[:, hk, bt * 4:(bt + 1) * 4], rhs=qbb[:, hk, :], start=True, stop=True)
                        return ins
                    S.op("pe", smm, [r_kTs, r_qb, r_inds, r_bTs, r_kTn], [PSR[sp_]])
                    S.op("act", lambda e, sp_=sp_: e.activation(out=PT[:].rearrange("p a b -> p (a b)"), in_=PS[sp_][:, 0:256], func=AF.Exp, scale=0.125),
                         [PSR[sp_]], [r_PT])
                    S.op("act", lambda e, sp_=sp_: e.activation(out=pnf[:], in_=PS[sp_][0:4, 256:272], func=AF.Exp, scale=0.125), [PSR[sp_]], [r_pn])
                    S.op("dve", lambda e: e.tensor_tensor(out=pnb[:], in0=pnf[:], in1=mask4[:], op=ALU.mult), [r_pn], [r_pn])

                    def pvs(e, hk=hk, kb_=kb_, bt=bt):
                        ins = None
                        for kt in range(16):
                            e.matmul(PS[0][0:64, hk * 32:hk * 32 + 16], lhsT=KVb[kb_][:, kt, 256 + hk * 64:256 + (hk + 1) * 64], rhs=PT[:, kt, :],
                                     start=(kt == 0), stop=False)
                        e.matmul(PS[0][0:64, hk * 32:hk * 32 + 16], lhsT=vnew[:, bt, hk * 64:(hk + 1) * 64], rhs=pnb[:], start=False, stop=True)
                        for kt in range(16):
                            e.matmul(PS[0][0:64, hk * 32 + 16:hk * 32 + 32], lhsT=ones_bf[:, 0:64], rhs=PT[:, kt, :], start=(kt == 0), stop=False)
                        ins = e.matmul(PS[0][0:64, hk * 32 + 16:hk * 32 + 32], lhsT=ones_bf[0:4, 0:64], rhs=pnb[:], start=False, stop=True)
                        return ins
                    S.op("pe", pvs, [r_KVb[kb_], r_PT, r_vnew, r_pn, r_ones], [PSR[0]])
                ov = PS[0][0:64, 0:128].rearrange("p (k two q) -> p k two q", k=4, two=2)
                S.op("dve", lambda e, ov=ov: e.reciprocal(out=rds[:], in_=ov[:, :, 1, :]), [PSR[0]], [r_rds])
                S.op("dve", lambda e, ov=ov, bt=bt: e.tensor_tensor(
                    out=oTs[:, :, bt * 4:(bt + 1) * 4].rearrange("p (k r) i -> p k r i", r=4),
                    in0=ov[:, :, 0, :].rearrange("p k (r i) -> p k r i", i=4),
                    in1=rds[:].rearrange("p k (r i) -> p k r i", i=4), op=ALU.mult), [PSR[0], r_rds], [r_oTs])
            S.dma("sp", oTb_scr[:, :, T:TT].rearrange("h e t -> e h t"), oTs[:], reads=[r_oTs], writes=[r_oTbs])
            S.barrier()
            if dbg_stop <= 3:
                return
            cv = Carver()
            wbo = cv.take([128, 8, D], BF16); r_wbo = R("wbo")
            wo_st = cv.take([128, 8, 256]); r_wo_st = R("b_wo_st")
            oin = [cv.take([128, 8, 512], BF16) for _ in range(2)]; r_oin = [R("oin0"), R("oin1")]
            xu = cv.take([128, DC, 512]); r_xu = R("b_xu")
            for c4 in range(4):
                S.dma("sp", wo_st[:], w_b_out.rearrange("(a p) n -> p a n", p=128)[:, :, c4 * 256:(c4 + 1) * 256], writes=[r_wo_st])
                S.op("pool", lambda e, c4=c4: e.tensor_copy(out=wbo[:, :, c4 * 256:(c4 + 1) * 256], in_=wo_st[:]), [r_wo_st], [r_wbo])
            for gi, (s, w) in enumerate(groups):
                ob_ = gi % 2
                S.dma("sp", oin[ob_][:, :, 0:w], oTb_scr[:, :, s:s + w].rearrange("(a two) e t -> (two e) a t", two=2),
                      reads=[r_oTbs], writes=[r_oin[ob_]])
                S.dma("sp", xu[:, :, 0:w], xs[:, :, s:s + w].rearrange("c p t -> p c t"), reads=[r_xs[gi]], writes=[r_xu])
                for dc in range(DC):
                    pb = dc % 2

                    def ymm(e, dc=dc, pb=pb, w=w, ob_=ob_):
                        ins = None
                        for a in range(8):
                            ins = e.matmul(PS[pb][:, 0:w], lhsT=wbo[:, a, dc * 128:(dc + 1) * 128], rhs=oin[ob_][:, a, 0:w], start=(a == 0), stop=(a == 7))
                        return ins
                    S.op("pe", ymm, [r_wbo, r_oin[ob_]], [PSR[pb]])
                    S.op("dve", lambda e, dc=dc, pb=pb, w=w: e.tensor_tensor(out=xu[:, dc, 0:w], in0=PS[pb][:, 0:w], in1=xu[:, dc, 0:w], op=ALU.add),
                         [PSR[pb], r_xu], [r_xu])
                S.dma("sp", xs[:, :, s:s + w].rearrange("c p t -> p c t"), xu[:, :, 0:w], reads=[r_xu], writes=[r_xs[gi]])
            S.barrier()

        MIX = {"c": mixer_c, "a": mixer_a, "d": mixer_d, "b": mixer_b}
        for l in range(depth):
            ffn(l, 0)
            if l < len(layer_mixers) and layer_mixers[l] in MIX:
                MIX[layer_mixers[l]](l)
            ffn(l, 1)

        S.barrier()
        for ti, (s, w) in enumerate(tiles):
            b = ti % 2
            S.dma("sp", xfm[b][:, :, 0:w], xs[:, :, s:s + w].rearrange("c p t -> p c t"),
                  reads=[r_xs[grp_of_tile(s)]], writes=[r_xfm[b]])

            def tr2(e, b=b, w=w):
                ins = None
                for c in range(DC):
                    ins = e.transpose(PS[6 + c // 4][0:w, (c % 4) * 128:(c % 4 + 1) * 128], xfm[b][:, c, 0:w], ident[:])
                return ins
            S.op("pe", tr2, [r_xfm[b], r_ident], [PSR[6], PSR[7]])
            for hh in range(2):
                S.op("act", lambda e, b=b, w=w, hh=hh: e.copy(out=xtok[b][0:w, 512 * hh:512 * hh + 512],
                                                             in_=PS[6 + hh][0:w, :]), [PSR[6 + hh]], [r_xtok[b]])
            dst = yp[s:s + w, :] if s < T else ysm[:, :]
            S.dma("sp", dst, xtok[b][0:w, :], reads=[r_xtok[b]])

        S.emit(final_waits=("sp",))
    return nc


_IDENT = np.eye(128, dtype=np.float32)
PAST_LEN = 2048


def const_inputs(T):
    nt = T // 128 + 1
    pos = np.zeros(nt * 128, np.float64)
    pos[:T] = np.arange(T)
    pos[T:T + NSAMP] = PAST_LEN + (np.arange(NSAMP) % 4)
    inv = 500000.0 ** (-np.arange(8, dtype=np.float64) / 8)
    ang = pos[:, None] * inv[None, :]
    kk = np.arange(128)[:, None]
    qq = np.arange(128)[None, :]
    mask2 = np.concatenate([(kk >= qq), (kk <= qq)], axis=1).astype(np.float32)
    return {
        "ident": _IDENT,
        "triu": np.triu(np.ones((32, 32), np.float32)),
        "rope_cos": np.cos(ang).astype(np.float32),
        "rope_sin": np.sin(ang).astype(np.float32),
        "mask2": mask2,
        "mask3": np.concatenate([(kk <= qq), np.ones((128, 128), bool)], axis=1).astype(np.float32),
        "mask4": (np.arange(4)[:, None] <= (np.arange(16)[None, :] % 4)).astype(np.float32),
        "kind": (1000.0 * (np.arange(16)[:, None] == (np.arange(T)[None, :] // 256))).astype(np.float32),
        "kinds": (1000.0 * (np.arange(8)[:, None] == (np.arange(2048)[None, :] // 256))).astype(np.float32),
        "pidx": np.arange(128, dtype=np.float32).reshape(128, 1),
        "ssd_l1": (np.arange(64)[:, None] > np.arange(64)[None, :]).astype(np.float32),
        "ssd_l2": (np.arange(64)[:, None] <= np.arange(64)[None, :]).astype(np.float32),
    }


def kernel(**inputs):
    T = 4096
    layer_mixers = ("a", "b", "c", "d")
    npool = int(inputs["cache_b_kv"].shape[1])
    nc = build_program(T=T, depth=4, layer_mixers=layer_mixers, NPOOL=npool)
    f = lambda k: np.ascontiguousarray(inputs[k])
    xpr = f("x_prompt")
    xsa = f("x_sample")
    consts = const_inputs(T)
    shared = {
        "norm_gain": f("norm_gain"), "w_ffn_up": f("w_ffn_up"), "w_ffn_down": f("w_ffn_down"),
        "w_a_in": f("w_a_in")[0], "a_qk_gain": f("a_qk_gain")[0], "w_a_out": f("w_a_out")[0],
        "w_b_in": f("w_b_in")[0], "b_qk_gain": f("b_qk_gain")[0], "w_b_out": f("w_b_out")[0],
        "pool_kv": f("cache_b_kv")[0].reshape(npool, 128, 2, 256),
        "w_c_in": f("w_c_in")[0], "w_c_gate2": f("w_c_gate2")[0], "b_c_gate": f("b_c_gate")[0],
        "c_norm_gain": f("c_norm_gain")[0], "w_c_out": f("w_c_out")[0],
        "w_d_in": f("w_d_in")[0], "d_conv_w": f("d_conv_w")[0], "d_conv_b": f("d_conv_b")[0],
        "d_dt_bias": f("d_dt_bias")[0], "d_a_log": f("d_a_log")[0], "d_skip": f("d_skip")[0],
        "d_norm_gain": f("d_norm_gain")[0], "w_d_out": f("w_d_out")[0],
    }
    shared.update(consts)
    shared = {k: np.ascontiguousarray(v) for k, v in shared.items()}
    in_maps = []
    for c in range(8):
        b0, b1 = c * 16, (c + 1) * 16
        m = dict(shared)
        m.update({
            "xp": xpr[c % 4],
            "xsm": xsa[b0:b1].reshape(NSAMP, D),
            "ca1": f("cache_a_w1")[0, b0:b1], "ca2": f("cache_a_w2")[0, b0:b1], "ca3": f("cache_a_w3")[0, b0:b1],
            "page_tab": f("page_table")[b0:b1].astype(np.int32),
            "sc_in": f("state_c")[0, b0:b1],
            "sd_ssm": f("state_d_ssm")[0, b0:b1], "sd_conv": f("state_d_conv")[0, b0:b1],
        })
        in_maps.append({k: np.ascontiguousarray(v) for k, v in m.items()})
    res = run_bass_kernel_spmd(nc, in_maps, core_ids=list(range(8)))
    r = res.results

    def pstack(name, shape):
        return np.stack([r[c][name].reshape(shape) for c in range(4)], axis=0)[None]

    def scat(name, shape):
        return np.concatenate([r[c][name].reshape((16,) + shape) for c in range(8)], axis=0)[None]

    y_prompt = np.stack([r[c]["yp"] for c in range(4)], axis=0)
    y_sample = np.concatenate([r[c]["ysm"].reshape(16, 4, D) for c in range(8)], axis=0)
    outs = [y_prompt, y_sample]
    for g, wn in enumerate((128, 512, 2048)):
        outs.append(pstack(f"aw{g}p", (wn, 2, 8, 64)))
        outs.append(scat(f"aw{g}s", (4, 2, 8, 64)))
    outs.append(pstack("bkvp", (T, 2, 4, 64)))
    outs.append(scat("bkvs", (4, 2, 4, 64)))
    outs.append(pstack("csp", (4, 128, 256)))
    outs.append(scat("css", (4, 128, 256)))
    outs.append(pstack("dssp", (32, 64, 128)))
    outs.append(scat("dsss", (32, 64, 128)))
    outs.append(pstack("dcp", (3, 3072)))
    outs.append(scat("dcs", (3, 3072)))
    return tuple(np.ascontiguousarray(o, dtype=np.float32) for o in outs)
```
